# Optimizing a Trainium2 kernel written in Bass

```python
import math
import jax, jax.numpy as jnp
from jax import lax
import numpy as np

D_MODEL = 1024
BATCH = 2
SEQ = 8192
DEPTH = 4

GRID_W = 64
CTX_LEN = 256
HEAD_DIM = 64
BRANCH_WIDTH = D_MODEL // 2
FOURIER_GROUP_DIM = 64
FOURIER_GROUPS = BRANCH_WIDTH // FOURIER_GROUP_DIM
CONV_WIDTH = BRANCH_WIDTH
CONV_KERNEL = 31
GQA_Q_HEADS = BRANCH_WIDTH // HEAD_DIM
GQA_GROUP = 4
GQA_KV_HEADS = GQA_Q_HEADS // GQA_GROUP
DIFF_QK_DIM = HEAD_DIM
DIFF_V_DIM = 2 * HEAD_DIM
DIFF_HEADS = BRANCH_WIDTH // DIFF_V_DIM
N_BRANCH = 4
FFN_HIDDEN = -(-8 * D_MODEL // (3 * 256)) * 256
ROPE_BASE = 10000.0
AXIS_ROT_DIM = HEAD_DIM // 2
Q_BLOCK = 128
EPS = 1e-6
LN_EPS = 1e-5

_IN_SIZES = (BRANCH_WIDTH,
             2 * CONV_WIDTH,
             GQA_Q_HEADS * HEAD_DIM,
             GQA_KV_HEADS * HEAD_DIM,
             GQA_KV_HEADS * HEAD_DIM,
             DIFF_HEADS * 2 * DIFF_QK_DIM,
             DIFF_HEADS * 2 * DIFF_QK_DIM,
             DIFF_HEADS * DIFF_V_DIM,
             N_BRANCH * D_MODEL)
IN_COLS = sum(_IN_SIZES)
SPLIT_IDX = tuple(int(v) for v in np.cumsum(_IN_SIZES)[:-1])

kernel_name = "hybrid_gated_dit_block"


def rms_norm(x, g):
    x32 = x.astype(jnp.float32)
    y = x32 * lax.rsqrt(jnp.mean(x32 * x32, axis=-1, keepdims=True) + EPS)
    return (y * g.astype(jnp.float32)).astype(x.dtype)


def layer_norm(x, g, b):
    x32 = x.astype(jnp.float32)
    mu = jnp.mean(x32, axis=-1, keepdims=True)
    var = jnp.mean(jnp.square(x32 - mu), axis=-1, keepdims=True)
    y = (x32 - mu) * lax.rsqrt(var + LN_EPS)
    return (y * g.astype(jnp.float32) + b.astype(jnp.float32)).astype(x.dtype)


def modulate(h, shift, scale):
    return h * (1.0 + scale) + shift


def axial_rope_tables(n_rows):
    row = jnp.repeat(jnp.arange(n_rows, dtype=jnp.float32), GRID_W)
    col = jnp.tile(jnp.arange(GRID_W, dtype=jnp.float32), n_rows)
    inv = ROPE_BASE ** (-jnp.arange(0, AXIS_ROT_DIM, 2, dtype=jnp.float32) / AXIS_ROT_DIM)
    ang = jnp.concatenate([row[:, None] * inv, col[:, None] * inv], axis=-1)
    return jnp.cos(ang), jnp.sin(ang)


def rope(x, cos, sin):
    half = x.shape[-1] // 2
    x32 = x.astype(jnp.float32)
    x1, x2 = x32[..., :half], x32[..., half:]
    return jnp.concatenate([x1 * cos - x2 * sin, x2 * cos + x1 * sin], axis=-1).astype(x.dtype)


def fourier_mix(z):
    b, l, _ = z.shape
    zg = z.reshape(b, l, FOURIER_GROUPS, FOURIER_GROUP_DIM).astype(jnp.float32)
    f = jnp.fft.fft2(zg, axes=(1, 3), norm="ortho").real
    return f.reshape(b, l, BRANCH_WIDTH).astype(z.dtype)


def conformer_conv(z, w, bias, ln_g, ln_b):
    a, g = jnp.split(z, 2, axis=-1)
    u = a * jax.nn.sigmoid(g)
    pad = CONV_KERNEL // 2
    y = lax.conv_general_dilated(u, w[:, None, :].astype(u.dtype), window_strides=(1,),
                                 padding=[(pad, pad)], dimension_numbers=("NWC", "WIO", "NWC"),
                                 feature_group_count=CONV_WIDTH) + bias
    return jax.nn.silu(layer_norm(y, ln_g, ln_b))


def gqa_q(zq, g, cos, sin):
    b, l, _ = zq.shape
    q = rms_norm(zq.reshape(b, l, GQA_Q_HEADS, HEAD_DIM), g).transpose(0, 2, 1, 3)
    if cos is not None:
        q = rope(q, cos, sin)
    return q.reshape(b, GQA_KV_HEADS, GQA_GROUP, l, HEAD_DIM)


def gqa_kv(zk, zv, g, cos, sin):
    b, l, _ = zk.shape
    k = rms_norm(zk.reshape(b, l, GQA_KV_HEADS, HEAD_DIM), g).transpose(0, 2, 1, 3)
    v = zv.reshape(b, l, GQA_KV_HEADS, HEAD_DIM).transpose(0, 2, 1, 3)
    if cos is not None:
        k = rope(k, cos, sin)
    return k, v


def diff_q(zq, cos, sin):
    b, l, _ = zq.shape
    q = zq.reshape(b, l, DIFF_HEADS, 2, DIFF_QK_DIM).transpose(0, 2, 3, 1, 4)
    if cos is not None:
        q = rope(q, cos, sin)
    return q


def diff_kv(zk, zv, cos, sin):
    b, l, _ = zk.shape
    k = zk.reshape(b, l, DIFF_HEADS, 2, DIFF_QK_DIM).transpose(0, 2, 3, 1, 4)
    v = zv.reshape(b, l, DIFF_HEADS, DIFF_V_DIM).transpose(0, 2, 1, 3)
    if cos is not None:
        k = rope(k, cos, sin)
    return k, v


def gqa_attend(q, k, v):
    s = jnp.einsum("bhgqd,bhkd->bhgqk", q, k).astype(jnp.float32) * (HEAD_DIM ** -0.5)
    p = jax.nn.softmax(s, axis=-1).astype(v.dtype)
    return jnp.einsum("bhgqk,bhkd->bhgqd", p, v)


def diff_attend(q, k, v, lam):
    s = jnp.einsum("bhmqd,bhmkd->bhmqk", q, k).astype(jnp.float32) * (DIFF_QK_DIM ** -0.5)
    p = jax.nn.softmax(s, axis=-1)
    w = (p[:, :, 0] - lam * p[:, :, 1]).astype(v.dtype)
    return jnp.einsum("bhqk,bhkd->bhqd", w, v)


def sweep_query_blocks(fn, q):
    *lead, s, d = q.shape
    nb = s // Q_BLOCK
    qb = jnp.moveaxis(q.reshape(*lead, nb, Q_BLOCK, d), -3, 0)
    out = jnp.moveaxis(lax.map(fn, qb), 0, -3)
    return out.reshape(*out.shape[:-3], s, out.shape[-1])


def gqa_merge(o):
    b, _, _, l, d = o.shape
    return o.reshape(b, GQA_Q_HEADS, l, d).transpose(0, 2, 1, 3).reshape(b, l, GQA_Q_HEADS * d)


def diff_merge(o, g, lam_init):
    b, h, l, dv = o.shape
    o = rms_norm(o, g) * (1.0 - lam_init)
    return o.transpose(0, 2, 1, 3).reshape(b, l, h * dv)


def context_keys_values(pc, lp):
    k, v = gqa_kv(pc[3], pc[4], lp["k_norm"], None, None)
    kd, vd = diff_kv(pc[6], pc[7], None, None)
    return (k, v, kd, vd)


def mixer_stream(parts, lp, cos, sin, ctx_kv, lam, lam_init):
    zf, zconv, zgq, zgk, zgv, zdq, zdk, zdv, zgate = parts
    y_four = fourier_mix(zf) @ lp["w_four"]
    y_conv = conformer_conv(zconv, lp["conv_w"], lp["conv_b"], lp["conv_ln_g"], lp["conv_ln_b"]) @ lp["w_conv"]
    q = gqa_q(zgq, lp["q_norm"], cos, sin)
    k, v = gqa_kv(zgk, zgv, lp["k_norm"], cos, sin)
    qd = diff_q(zdq, cos, sin)
    kd, vd = diff_kv(zdk, zdv, cos, sin)
    if ctx_kv is None:
        o_gqa = gqa_attend(q, k, v)
        o_diff = diff_attend(qd, kd, vd, lam)
    else:
        kc, vc, kdc, vdc = ctx_kv
        k_all = jnp.concatenate([k, kc], axis=2)
        v_all = jnp.concatenate([v, vc], axis=2)
        kd_all = jnp.concatenate([kd, kdc], axis=3)
        vd_all = jnp.concatenate([vd, vdc], axis=2)
        o_gqa = sweep_query_blocks(lambda qb: gqa_attend(qb, k_all, v_all), q)
        o_diff = sweep_query_blocks(lambda qb: diff_attend(qb, kd_all, vd_all, lam), qd)
    y_gqa = gqa_merge(o_gqa) @ lp["w_gqa"]
    y_diff = diff_merge(o_diff, lp["diff_norm"], lam_init) @ lp["w_diff"]
    b, l, _ = zgate.shape
    gates = jax.nn.sigmoid(zgate.reshape(b, l, N_BRANCH, D_MODEL))
    ys = jnp.stack([y_four, y_conv, y_gqa, y_diff], axis=2)
    return jnp.sum(gates * ys, axis=2) @ lp["w_out"]


def swiglu(h, w1, w3, w2):
    return (jax.nn.silu(h @ w1) * (h @ w3)) @ w2


def setup_inputs(seed: int = 0) -> dict:
    key = jax.random.key(seed)
    ks = iter(jax.random.split(key, 40))

    def nrm(shape, scale):
        return scale * jax.random.normal(next(ks), shape, jnp.float32)

    def gain(shape):
        return 1.0 + nrm(shape, 0.02)

    L = DEPTH
    return {
        "x": nrm((BATCH, SEQ, D_MODEL), 1.0),
        "c": nrm((BATCH, D_MODEL), 1.0),
        "ctx": nrm((BATCH, CTX_LEN, D_MODEL), 1.0),
        "c_ctx": nrm((D_MODEL,), 1.0),
        "w_ada": nrm((L, D_MODEL, 6 * D_MODEL), 0.5 * D_MODEL ** -0.5),
        "b_ada": nrm((L, 6 * D_MODEL), 0.02),
        "norm_mix": gain((L, D_MODEL)),
        "w_in": nrm((L, D_MODEL, IN_COLS), D_MODEL ** -0.5),
        "w_four": nrm((L, BRANCH_WIDTH, D_MODEL), BRANCH_WIDTH ** -0.5),
        "conv_w": nrm((L, CONV_KERNEL, CONV_WIDTH), CONV_KERNEL ** -0.5),
        "conv_b": nrm((L, CONV_WIDTH), 0.02),
        "conv_ln_g": gain((L, CONV_WIDTH)),
        "conv_ln_b": nrm((L, CONV_WIDTH), 0.02),
        "w_conv": nrm((L, CONV_WIDTH, D_MODEL), CONV_WIDTH ** -0.5),
        "q_norm": gain((L, HEAD_DIM)),
        "k_norm": gain((L, HEAD_DIM)),
        "w_gqa": nrm((L, GQA_Q_HEADS * HEAD_DIM, D_MODEL), (GQA_Q_HEADS * HEAD_DIM) ** -0.5),
        "lam_q1": nrm((L, DIFF_QK_DIM), 0.1),
        "lam_k1": nrm((L, DIFF_QK_DIM), 0.1),
        "lam_q2": nrm((L, DIFF_QK_DIM), 0.1),
        "lam_k2": nrm((L, DIFF_QK_DIM), 0.1),
        "diff_norm": gain((L, DIFF_V_DIM)),
        "w_diff": nrm((L, DIFF_HEADS * DIFF_V_DIM, D_MODEL), (DIFF_HEADS * DIFF_V_DIM) ** -0.5),
        "w_out": nrm((L, D_MODEL, D_MODEL), D_MODEL ** -0.5),
        "norm_ffn": gain((L, D_MODEL)),
        "w_ffn1": nrm((L, D_MODEL, FFN_HIDDEN), D_MODEL ** -0.5),
        "w_ffn3": nrm((L, D_MODEL, FFN_HIDDEN), D_MODEL ** -0.5),
        "w_ffn2": nrm((L, FFN_HIDDEN, D_MODEL), FFN_HIDDEN ** -0.5),
        "final_norm": gain((D_MODEL,)),
    }


def reference(x, c, ctx, c_ctx, w_ada, b_ada, norm_mix, w_in, w_four, conv_w, conv_b, conv_ln_g,
              conv_ln_b, w_conv, q_norm, k_norm, w_gqa, lam_q1, lam_k1, lam_q2, lam_k2, diff_norm,
              w_diff, w_out, norm_ffn, w_ffn1, w_ffn3, w_ffn2, final_norm):
    n_rows = x.shape[1] // GRID_W
    cos, sin = axial_rope_tables(n_rows)
    silu_c = jax.nn.silu(c)
    silu_cc = jax.nn.silu(c_ctx)
    for l in range(DEPTH):
        last = l == DEPTH - 1
        lam_init = 0.8 - 0.6 * math.exp(-0.3 * l)
        lp = dict(w_four=w_four[l], conv_w=conv_w[l], conv_b=conv_b[l], conv_ln_g=conv_ln_g[l],
                  conv_ln_b=conv_ln_b[l], w_conv=w_conv[l], q_norm=q_norm[l], k_norm=k_norm[l],
                  w_gqa=w_gqa[l], diff_norm=diff_norm[l], w_diff=w_diff[l], w_out=w_out[l])
        mod_x = (silu_c @ w_ada[l] + b_ada[l])[:, None, :]
        sh_m, sc_m, g_m, sh_f, sc_f, g_f = jnp.split(mod_x, 6, axis=-1)
        mod_c = silu_cc @ w_ada[l] + b_ada[l]
        csh_m, csc_m, cg_m, csh_f, csc_f, cg_f = jnp.split(mod_c, 6, axis=-1)
        lam = (jnp.exp(jnp.sum(lam_q1[l].astype(jnp.float32) * lam_k1[l].astype(jnp.float32)))
               - jnp.exp(jnp.sum(lam_q2[l].astype(jnp.float32) * lam_k2[l].astype(jnp.float32)))
               + lam_init)
        hx = modulate(rms_norm(x, norm_mix[l]), sh_m, sc_m)
        hc = modulate(rms_norm(ctx, norm_mix[l]), csh_m, csc_m)
        px = jnp.split(hx @ w_in[l], SPLIT_IDX, axis=-1)
        pc = jnp.split(hc @ w_in[l], SPLIT_IDX, axis=-1)
        ctx_kv = context_keys_values(pc, lp)
        x = x + g_m * mixer_stream(px, lp, cos, sin, ctx_kv, lam, lam_init)
        if not last:
            ctx = ctx + cg_m * mixer_stream(pc, lp, None, None, None, lam, lam_init)
            ctx = ctx + cg_f * swiglu(modulate(rms_norm(ctx, norm_ffn[l]), csh_f, csc_f),
                                      w_ffn1[l], w_ffn3[l], w_ffn2[l])
        x = x + g_f * swiglu(modulate(rms_norm(x, norm_ffn[l]), sh_f, sc_f),
                             w_ffn1[l], w_ffn3[l], w_ffn2[l])
    return rms_norm(x, final_norm)
```

```python
import math
from contextlib import ExitStack

import numpy as np
import ml_dtypes

import concourse.bass as bass
import concourse.mybir as mybir
from concourse.bass_utils import run_bass_kernel_spmd

F32 = mybir.dt.float32
BF16 = mybir.dt.bfloat16
AF = mybir.ActivationFunctionType
ALU = mybir.AluOpType
AX = mybir.AxisListType
NPBF = ml_dtypes.bfloat16

D = 1024
KC = 8
SEQ = 8192
NB = 2
CTXL = 256
TL = 2048
TA = TL + CTXL
NIN = 7936
HID = 2816
EPS = 1e-6
LN_EPS = 1e-5
TILES = [(0, 512, 0), (512, 512, 0), (1024, 512, 0), (1536, 512, 0), (2048, 256, 1)]
LTILES = TILES[:4]
UL0 = 15
UC0 = 2078 + 15
UTW = 2078 + 286
SEM_EPOCH = 20000

VC_C = 0
VC_BADA = 16
VC_NMIX = 64
VC_NFFN = 72
VC_NFIN = 80
VC_CONVW = 88
VC_CONVB = 212
VC_LNG = 216
VC_LNB = 220
VC_QN = 224
VC_KN = 225
VC_DN = 226
VC_LAM = 227
VC_NLI = 483
VC_1LI = 484
NVEC = 485


class Sched:
    def __init__(self, nc, es):
        self.nc = nc
        self.es = es
        self.ops = []
        self.lastw = {}
        self.readers = {}
        self.dma_cnt = {}

    def add(self, eng, fn, reads=(), writes=(), dma_key=None):
        idx = len(self.ops)
        cdeps = {}
        ddeps = {}

        def dep(o):
            if o is None:
                return
            op = self.ops[o]
            if op["dma_key"] is not None:
                k = op["dma_key"]
                ddeps[k] = 16 * self.dma_cnt[k]
            else:
                e = op["eng"]
                if e == "pe" and eng == "pe" and dma_key is None:
                    return
                if e not in cdeps or cdeps[e] < o:
                    cdeps[e] = o

        for r in list(reads) + list(writes):
            dep(self.lastw.get(r))
        for r in writes:
            rd = self.readers.get(r)
            if rd:
                for o in rd.values():
                    dep(o)
        for e, o in getattr(self, "fence_c", {}).items():
            if not (e == "pe" and eng == "pe" and dma_key is None):
                if e not in cdeps or cdeps[e] < o:
                    cdeps[e] = o
        for k, v in getattr(self, "fence_d", {}).items():
            if ddeps.get(k, 0) < v:
                ddeps[k] = v
        op = dict(eng=eng, fn=fn, cdeps=cdeps, ddeps=ddeps, dma_key=dma_key)
        if dma_key is not None:
            self.dma_cnt[dma_key] = self.dma_cnt.get(dma_key, 0) + 1
        self.ops.append(op)
        for r in writes:
            self.lastw[r] = idx
            self.readers[r] = {}
        for r in reads:
            key = dma_key if dma_key is not None else eng
            self.readers.setdefault(r, {})[("d", key) if dma_key is not None else key] = idx
        return idx

    def fence(self):
        last = {}
        for i, op in enumerate(self.ops):
            if op["dma_key"] is None:
                last[op["eng"]] = i
        self.fence_c = last
        self.fence_d = {k: 16 * v for k, v in self.dma_cnt.items()}

    def emit(self, final_waits=()):
        nc = self.nc
        ops = self.ops
        needed = set()
        for op in ops:
            for o in op["cdeps"].values():
                needed.add(o)
        cnt = {}
        sems = {}

        def new_sem(name):
            return self.es.enter_context(nc.semaphore(name))

        for i, op in enumerate(ops):
            if op["dma_key"] is not None:
                k = ("dma", op["dma_key"])
                if k not in sems:
                    sems[k] = new_sem("d%d" % len(sems))
                op["sem"] = sems[k]
                continue
            if i in needed:
                e = op["eng"]
                c = cnt.get(e, 0)
                ep = c // SEM_EPOCH
                k = (e, ep)
                if k not in sems:
                    sems[k] = new_sem("%s%d" % (e, ep))
                op["sem"] = sems[k]
                op["val"] = c % SEM_EPOCH + 1
                cnt[e] = c + 1
        by_eng = {}
        for i, op in enumerate(ops):
            by_eng.setdefault(op["eng"], []).append(i)
        block = self.es.enter_context(nc.Block())
        dma_cnt = self.dma_cnt

        def run(engname, e):
            waited = {}
            for i in by_eng.get(engname, []):
                op = ops[i]
                for o in op["cdeps"].values():
                    p = ops[o]
                    s, v = p["sem"], p["val"]
                    if waited.get(id(s), 0) < v:
                        e.wait_ge(s, v)
                        waited[id(s)] = v
                for k, v in op["ddeps"].items():
                    s = sems[("dma", k)]
                    if waited.get(id(s), 0) < v:
                        e.wait_ge(s, v)
                        waited[id(s)] = v
                ins = op["fn"](e)
                if op["dma_key"] is not None:
                    ins.then_inc(op["sem"], 16)
                elif "sem" in op:
                    ins.then_inc(op["sem"], 1)
            if engname == "sp":
                for k in final_waits:
                    e.wait_ge(sems[("dma", k)], 16 * dma_cnt[k])

        @block.tensor
        def _(e):
            run("pe", e)

        @block.scalar
        def _(e):
            run("act", e)

        @block.vector
        def _(e):
            run("dve", e)

        @block.gpsimd
        def _(e):
            run("pool", e)

        @block.sync
        def _(e):
            run("sp", e)


class Rot:
    def __init__(self, items):
        self.items = items
        self.i = 0

    def next(self):
        it = self.items[self.i % len(self.items)]
        self.i += 1
        return it


class Builder:
    def __init__(self, do_post, do_pre, last, lam_init):
        self.do_post, self.do_pre, self.last, self.lam_init = do_post, do_pre, last, lam_init
        self.nc = bass.Bass("TRN2", target_bir_lowering=False)
        self.es = ExitStack()
        self.S = Sched(self.nc, self.es)
        self.uid = 0
        self.out_keys = []

    def din(self, name, shape, dt=F32):
        return self.nc.dram_tensor(name, list(shape), dt, kind="ExternalInput").ap()

    def dout(self, name, shape, dt=F32):
        return self.nc.dram_tensor(name, list(shape), dt, kind="ExternalOutput").ap()

    def dscr(self, name, shape, dt):
        return self.nc.dram_tensor(name, list(shape), dt).ap()

    def sb(self, name, shape, dt, es=None):
        self.uid += 1
        return (es or self.es).enter_context(self.nc.sbuf_tensor("%s_%d" % (name, self.uid), list(shape), dt))

    def free(self, es):
        self.S.fence()
        es.close()

    def rot(self, name, shape, dt, n, es=None):
        return Rot([(self.sb("%s%d" % (name, i), shape, dt, es), "%s%d" % (name, i)) for i in range(n)])

    def op(self, eng, fn, r=(), w=()):
        return self.S.add(eng, fn, r, w)

    def dma(self, q, out, in_, r=(), w=(), key=None):
        self.S.add(q, lambda e, o=out, i=in_: e.dma_start(out=o, in_=i), r, w, dma_key=key)

    def mm(self, out, lhsT, rhs, start, stop, r=(), w=()):
        self.S.add("pe", lambda e, o=out, l=lhsT, rr=rhs, s=start, t=stop: e.matmul(o, lhsT=l, rhs=rr, start=s, stop=t), r, w)

    def act(self, out, in_, func, r=(), w=(), bias=None, scale=None):
        kw = {}
        if bias is not None:
            kw["bias"] = bias
        if scale is not None:
            kw["scale"] = scale
        self.S.add("act", lambda e, o=out, i=in_, f=func, k=kw: e.activation(out=o, in_=i, func=f, **k), r, w)

    def tt(self, eng, out, in0, in1, op, r=(), w=()):
        self.S.add(eng, lambda e, o=out, a=in0, b=in1, p=op: e.tensor_tensor(out=o, in0=a, in1=b, op=p), r, w)

    def ts(self, eng, out, in0, s1, s2, op0, op1=None, r=(), w=()):
        if op1 is None:
            self.S.add(eng, lambda e, o=out, a=in0, x=s1, p=op0: e.tensor_scalar(out=o, in0=a, scalar1=x, scalar2=None, op0=p), r, w)
        else:
            self.S.add(eng, lambda e, o=out, a=in0, x=s1, y=s2, p=op0, q=op1: e.tensor_scalar(out=o, in0=a, scalar1=x, scalar2=y, op0=p, op1=q), r, w)

    def stt(self, eng, out, in0, scalar, in1, op0, op1, r=(), w=()):
        self.S.add(eng, lambda e, o=out, a=in0, s=scalar, b=in1, p=op0, q=op1: e.scalar_tensor_tensor(out=o, in0=a, scalar=s, in1=b, op0=p, op1=q), r, w)

    def rsqrt(self, out, outres, in_, inres, eps, scale=1.0):
        self.act(out, in_, AF.Sqrt, r=[inres, "epsc"], w=[outres], bias=self.epsc[:, 0:1] if eps == EPS else self.epsc[:, 1:2], scale=scale)
        self.S.add("dve", lambda e, o=out: e.reciprocal(out=o, in_=o), [outres], [outres])

    def cp(self, eng, out, in_, r=(), w=()):
        self.S.add(eng, lambda e, o=out, i=in_: e.tensor_copy(out=o, in_=i), r, w)

    def memset(self, eng, ap, val, w=()):
        self.S.add(eng, lambda e, a=ap, v=val: e.memset(a, v), (), w)

    def build(self):
        b = self
        post, pre = self.do_post, self.do_pre
        if post:
            b.xT_in = b.din("xT", [D, TA])
            b.w = {n: b.din(n, s) for n, s in [("w_in", [D, NIN]), ("w_four", [512, D]),
                                               ("w_conv", [512, D]), ("w_gqa", [512, D]), ("w_diff", [512, D]),
                                               ("w_out", [D, D]), ("w_ffn1", [D, HID]), ("w_ffn3", [D, HID]),
                                               ("w_ffn2", [HID, D])]}
            b.vec_d = b.din("vec", [128, NVEC])
            b.modin_d = b.din("modin", [128, 128])
            b.ropeC_d = b.din("ropeC", [128, TA])
            b.ropeS_d = b.din("ropeS", [128, TA])
            b.dftc_d = b.din("dftc", [2, 2, 8, 128, 4, 512], BF16)
            b.dfts_d = b.din("dfts", [2, 2, 8, 128, 4, 512], BF16)
            b.dftcc_d = b.din("dftcc", [128, 2, 256], BF16)
            b.dftsc_d = b.din("dftsc", [128, 2, 256], BF16)
            b.cmat_d = b.din("cmat", [128, 6, 128], BF16)
            b.ZfG = b.din("ZfG", [SEQ, 512], BF16)
            b.KT_all = b.din("KT_all", [128, SEQ], BF16)
            b.dKT_all = b.din("dKT_all", [512, SEQ], BF16)
            b.V_all = b.din("V_all", [SEQ, 128], BF16)
            b.dV_all = b.din("dV_all", [SEQ, 512], BF16)
            b.halo = b.din("halo", [512, 30], BF16)
            b.out_d = b.dout("out", [D, TL])
            b.xT_out = b.dout("xT_out", [D, TA])
            self.out_keys += ["out", "xT_out"]
            b.gatesD = b.dscr("gatesD", [4 * D, TA], BF16)
            b.xmidD = b.dscr("xmidD", [D, TA], F32)
        if pre:
            sfx = "_n" if post else ""
            b.w2 = {"w_ada": b.din("w_ada" + sfx, [D, 6 * D]), "w_in": b.din("w_inp" + sfx, [D, 2816])}
            b.vec2_d = b.din("vec_n" if post else "vec", [128, NVEC])
            if post:
                b.xT_pre = b.xT_out
            else:
                b.xT_pre = b.din("xT", [D, TA])
                b.ropeC_d = b.din("ropeC", [128, TA])
                b.ropeS_d = b.din("ropeS", [128, TA])
                b.cmat_d = b.din("cmat", [128, 6, 128], BF16)
            b.o_Zf = b.dout("o_Zf", [TL, 512], BF16)
            b.o_KT = b.dout("o_KT", [128, TL], BF16)
            b.o_dKT = b.dout("o_dKT", [512, TL], BF16)
            b.o_V = b.dout("o_V", [TL, 128], BF16)
            b.o_dV = b.dout("o_dV", [TL, 512], BF16)
            b.o_ue = b.dout("o_ue", [512, 32], BF16)
            b.o_mod = b.dout("o_mod", [128, 128])
            self.out_keys += ["o_Zf", "o_KT", "o_dKT", "o_V", "o_dV", "o_ue", "o_mod"]

        b.ps = b.es.enter_context(b.nc.psum_tensor("ps", [128, 8, 512], F32))
        b.cmat = b.sb("cmat", [128, 6, 128], BF16)
        b.dma("sp", b.cmat[:], b.cmat_d[:, :, :], w=["cmat"], key="cmat")
        b.C128, b.S128N, b.RMAT, b.ONESD, b.BONES, b.IDENT = [b.cmat[:, i, :] for i in range(6)]
        b.ones1 = b.sb("ones1", [128, 128], BF16)
        b.epsc = b.sb("epsc", [128, 2], F32)
        b.memset("pool", b.epsc[:, 0:1], EPS, w=["epsc"])
        b.memset("pool", b.epsc[:, 1:2], LN_EPS, w=["epsc"])
        b.memset("pool", b.ones1[:], 1.0, w=["ones1"])
        b.wb = b.rot("wb", [128, KC, 512], BF16, 2)
        b.t32 = b.rot("t32_", [128, 512], F32, 4)
        b.tb = b.rot("tb_", [128, 512], BF16, 4)
        b.psr = Rot(list(range(8)))

        if post:
            self.post_program()
        if pre:
            self.pre_program()
        finals = []
        self.S.emit(final_waits=self.final_keys)
        return self.nc

    def load_rope(self, es):
        b = self
        b.ropeC = b.sb("ropeC", [128, TA], F32, es)
        b.ropeS = b.sb("ropeS", [128, TA], F32, es)
        b.dma("sp", b.ropeC[:], b.ropeC_d[:, :], w=["ropeC"], key="rope")
        b.dma("sp", b.ropeS[:], b.ropeS_d[:, :], w=["ropeS"], key="rope")
        b.wb2 = b.rot("wc", [128, KC, 512], BF16, 2, es)

    def load_vec(self, vec_d, tag):
        b = self
        vec = b.sb("vec" + tag, [128, NVEC], F32)
        b.dma("sp", vec[:], vec_d[:, :], w=["vec" + tag], key="vec" + tag)
        return vec

    def phase_mod(self, w_ada, vec, tag):
        b = self
        vr = "vec" + tag
        modall = b.sb("modall" + tag, [128, 128], F32)
        mod = modall[:, 0:96].rearrange("p (j s) -> p j s", s=2)
        amix = modall[:, 96:112].rearrange("p (j s) -> p j s", s=2)
        affn = modall[:, 112:128].rearrange("p (j s) -> p j s", s=2)
        cvb = b.sb("cvb" + tag, [128, 16], BF16)
        mr = "mod" + tag
        b.act(cvb[:], vec[:, VC_C:VC_C + 16], AF.Silu, r=[vr], w=["cvb" + tag])
        wv = w_ada.rearrange("(kc p) n -> p kc n", p=128)
        psb = b.psr.next()
        psm = b.ps[:, psb, 0:96]
        first = True
        for pc in range(12):
            wbuf, wr = b.wb.next()
            b.dma("pool", wbuf[:, :, :], wv[:, :, pc * 512:(pc + 1) * 512], w=[wr], key=wr)
            for sub in range(4):
                j = pc * 4 + sub
                for kc in range(KC):
                    b.mm(psm[:, 2 * j:2 * j + 2], wbuf[:, kc, sub * 128:(sub + 1) * 128], cvb[:, 2 * kc:2 * kc + 2],
                         kc == 0, kc == KC - 1, r=[wr, "cvb" + tag], w=[("ps", psb)])
        psm3 = b.ps[:, psb, 0:96].rearrange("p (j s) -> p j s", s=2)
        for s in range(2):
            b.tt("dve", mod[:, :, s], psm3[:, :, s], vec[:, VC_BADA:VC_BADA + 48], ALU.add, r=[("ps", psb), vr], w=[mr])
        for s in range(2):
            b.stt("dve", amix[:, :, s], mod[:, 8:16, s], 1.0, vec[:, VC_NMIX:VC_NMIX + 8], ALU.add, ALU.mult, r=[mr, vr], w=[mr + "a"])
            b.stt("dve", affn[:, :, s], mod[:, 32:40, s], 1.0, vec[:, VC_NFFN:VC_NFFN + 8], ALU.add, ALU.mult, r=[mr, vr], w=[mr + "a"])
        return dict(mod=mod, amix=amix, affn=affn, r=[mr, mr + "a"], modall=modall)

    def phase_norm(self, src_d, srcres, tiles, hT, hres, A, Sh, rres, out_d=None, outres=None, es=None):
        b = self
        xv = src_d.rearrange("(kc p) t -> p kc t", p=128)
        es = ExitStack()
        xr = b.rot("nx", [128, KC, 512], F32, 1, es)
        sq = b.rot("nsq", [128, KC, 512], BF16, 1, es)
        rs = b.rot("nrs", [128, 512], F32, 2, es)
        ov = out_d.rearrange("(kc p) t -> p kc t", p=128) if out_d is not None else None
        for ti, (t0, ts, s) in enumerate(tiles):
            xt, xn = xr.next()
            b.dma("sp", xt[:, :, :ts], xv[:, :, t0:t0 + ts], r=[(srcres, ti)], w=[xn], key=xn)
            sqt, sn = sq.next()
            b.act(sqt[:, :, :ts], xt[:, :, :ts], AF.Square, r=[xn], w=[sn])
            pb = b.psr.next()
            for kc in range(KC):
                b.mm(b.ps[:, pb, :ts], b.ONESD, sqt[:, kc, :ts], kc == 0, kc == KC - 1, r=[sn, "cmat"], w=[("ps", pb)])
            rt, rn = rs.next()
            b.rsqrt(rt[:, :ts], rn, b.ps[:, pb, :ts], ("ps", pb), EPS)
            for kc in range(KC):
                t3, tn = b.t32.next()
                a_ap = A[:, kc, s:s + 1] if len(A.shape) == 3 else A[:, kc:kc + 1]
                b.stt("dve", t3[:, :ts], xt[:, kc, :ts], a_ap, rt[:, :ts], ALU.mult, ALU.mult, r=[xn, rn] + rres, w=[tn])
                if out_d is None:
                    b.act(hT[:, kc, t0:t0 + ts], t3[:, :ts], AF.Identity, r=[tn] + rres, w=[(hres, ti)], bias=Sh[:, kc, s:s + 1])
                else:
                    b.dma("sp", ov[:, kc, t0:t0 + ts], t3[:, :ts], r=[tn], w=[(outres, ti)], key=outres)
        b.free(es)

    def linear(self, W, kc_n, pieces, srcT, srcres, tiles, epi, tile_ids=None):
        b = self
        wv = W.rearrange("(kc p) n -> p kc n", p=128)
        loaded = {}

        def load(i):
            c0, nw = pieces[i]
            wbuf, wr = b.wb.next()
            b.dma("pool", wbuf[:, 0:kc_n, 0:nw], wv[:, :, c0:c0 + nw], w=[wr], key=wr)
            loaded[i] = (wbuf, wr)

        load(0)
        for i, (c0, nw) in enumerate(pieces):
            if i + 1 < len(pieces):
                load(i + 1)
            wbuf, wr = loaded.pop(i)
            for sub in range(nw // 128):
                for tix, (t0, ts, s) in enumerate(tiles):
                    ti = tile_ids[tix] if tile_ids else tix
                    pb = b.psr.next()
                    for kc in range(kc_n):
                        b.mm(b.ps[:, pb, :ts], wbuf[:, kc, sub * 128:(sub + 1) * 128], srcT[:, kc, t0:t0 + ts],
                             kc == 0, kc == kc_n - 1, r=[wr, (srcres, ti)], w=[("ps", pb)])
                    epi(c0 + sub * 128, ti, (t0, ts, s), b.ps[:, pb, :ts], ("ps", pb))

    def linear2(self, Wa, ca, Wb, cb, ncols, kc_n, srcT, srcres, tiles, epi, tile_ids=None):
        b = self
        wva = Wa.rearrange("(kc p) n -> p kc n", p=128)
        wvb = Wb.rearrange("(kc p) n -> p kc n", p=128)
        pieces = [(o, min(512, ncols - o)) for o in range(0, ncols, 512)]
        loaded = {}

        def load(i):
            o, nw = pieces[i]
            wa, war = b.wb.next()
            wb_, wbr = b.wb2.next()
            b.dma("pool", wa[:, 0:kc_n, 0:nw], wva[:, :, ca + o:ca + o + nw], w=[war], key=war)
            b.dma("pool", wb_[:, 0:kc_n, 0:nw], wvb[:, :, cb + o:cb + o + nw], w=[wbr], key=wbr)
            loaded[i] = (wa, war, wb_, wbr)

        load(0)
        for i, (o, nw) in enumerate(pieces):
            if i + 1 < len(pieces):
                load(i + 1)
            wa, war, wb_, wbr = loaded.pop(i)
            for sub in range(nw // 128):
                for tix, (t0, ts, s) in enumerate(tiles):
                    ti = tile_ids[tix] if tile_ids else tix
                    pa, pb = b.psr.next(), b.psr.next()
                    for kc in range(kc_n):
                        b.mm(b.ps[:, pa, :ts], wa[:, kc, sub * 128:(sub + 1) * 128], srcT[:, kc, t0:t0 + ts],
                             kc == 0, kc == kc_n - 1, r=[war, (srcres, ti)], w=[("ps", pa)])
                    for kc in range(kc_n):
                        b.mm(b.ps[:, pb, :ts], wb_[:, kc, sub * 128:(sub + 1) * 128], srcT[:, kc, t0:t0 + ts],
                             kc == 0, kc == kc_n - 1, r=[wbr, (srcres, ti)], w=[("ps", pb)])
                    epi((o + sub * 128) // 128, ti, (t0, ts, s), b.ps[:, pa, :ts], ("ps", pa), b.ps[:, pb, :ts], ("ps", pb))

    def linear_tm(self, W, c0, nw, srcT, srcres, tok_subs, epi):
        b = self
        wv = W.rearrange("(kc p) n -> p kc n", p=128)
        wbuf, wr = b.wb.next()
        b.dma("pool", wbuf[:, :, 0:nw], wv[:, :, c0:c0 + nw], w=[wr], key=wr)
        for tok0, ti in tok_subs:
            pb = b.psr.next()
            for kc in range(KC):
                b.mm(b.ps[:, pb, :nw], srcT[:, kc, tok0:tok0 + 128], wbuf[:, kc, 0:nw], kc == 0, kc == KC - 1,
                     r=[wr, (srcres, ti)], w=[("ps", pb)])
            epi(tok0, b.ps[:, pb, :nw], ("ps", pb))

    def rope_epi(self, ps, psres, tile, dst, dstres, gain=None, gres=()):
        b = self
        t0, ts, s = tile
        xb, xn = b.tb.next()
        if gain is not None:
            sq, sn = b.tb.next()
            b.act(sq[:, :ts], ps, AF.Square, r=[psres], w=[sn])
            p2 = b.psr.next()
            b.mm(b.ps[:, p2, :ts], b.BONES, sq[:, :ts], True, True, r=[sn, "cmat"], w=[("ps", p2)])
            rt, rn = b.t32.next()
            b.rsqrt(rt[:, :ts], rn, b.ps[:, p2, :ts], ("ps", p2), EPS)
            b.stt("dve", xb[:, :ts], ps, gain, rt[:, :ts], ALU.mult, ALU.mult, r=[psres, rn] + list(gres), w=[xn])
        else:
            b.act(xb[:, :ts], ps, AF.Copy, r=[psres], w=[xn])
        p3 = b.psr.next()
        b.mm(b.ps[:, p3, :ts], b.RMAT, xb[:, :ts], True, True, r=[xn, "cmat"], w=[("ps", p3)])
        t1, n1 = b.t32.next()
        t2, n2 = b.t32.next()
        b.tt("dve", t1[:, :ts], xb[:, :ts], b.ropeC[:, t0:t0 + ts], ALU.mult, r=[xn, "ropeC"], w=[n1])
        b.tt("dve", t2[:, :ts], b.ps[:, p3, :ts], b.ropeS[:, t0:t0 + ts], ALU.mult, r=[("ps", p3), "ropeS"], w=[n2])
        b.tt("pool", dst, t1[:, :ts], t2[:, :ts], ALU.add, r=[n1, n2], w=[dstres])

    def pre_program(self):
        b = self
        es = ExitStack()
        vec = b.load_vec(b.vec2_d, "P")
        m = b.phase_mod(b.w2["w_ada"], vec, "P")
        b.load_rope(es)
        hT = b.sb("hTp", [128, KC, TL], BF16, es)
        srcres = "xT_out" if b.do_post else "xT_in"
        b.phase_norm(b.xT_pre, srcres, LTILES, hT, "hTp", m["amix"], m["mod"][:, 0:8, :], m["r"] + ["vecP"])
        W = b.w2["w_in"]
        toks = [(tok0, tok0 // 512) for tok0 in range(0, TL, 128)]
        stage = b.rot("pst", [128, 512], BF16, 3, es)

        def tm_out(dst):
            def epi(tok0, ps, psres):
                st, sn = stage.next()
                nw = ps.shape[1]
                b.act(st[:, :nw], ps, AF.Copy, r=[psres], w=[sn])
                b.dma("sp", dst[tok0:tok0 + 128, :], st[:, :nw], r=[sn], w=[("pre_out", id(dst), tok0)], key="pre_out")
            return epi

        b.dma("sp", b.o_mod[:, :], m["modall"][:, :], r=m["r"], w=["o_mod"], key="pre_out")
        b.linear_tm(W, 0, 512, hT, "hTp", toks, tm_out(b.o_Zf))
        b.linear_tm(W, 1664, 128, hT, "hTp", toks, tm_out(b.o_V))
        b.linear_tm(W, 2304, 512, hT, "hTp", toks, tm_out(b.o_dV))

        def k_epi(dst, c_base, gain):
            def epi(col, ti, tile, ps, psres):
                t0, ts, s = tile
                st, sn = stage.next()
                b.rope_epi(ps, psres, tile, st[:, :ts], sn, gain=gain, gres=["vecP"])
                r0 = col - c_base
                b.dma("sp", dst[r0:r0 + 128, t0:t0 + ts], st[:, :ts], r=[sn], w=[("pre_out", id(dst), col, ti)], key="pre_out")
            return epi

        b.linear(W, KC, [(1536, 128)], hT, "hTp", LTILES, k_epi(b.o_KT, 1536, vec[:, VC_KN:VC_KN + 1]))
        b.linear(W, KC, [(1792, 512)], hT, "hTp", LTILES, k_epi(b.o_dKT, 1792, None))
        hTe = b.sb("hTe", [128, KC, 32], BF16, es)
        b.cp("pool", hTe[:, :, 0:16], hT[:, :, 0:16], r=[("hTp", 0)], w=[("hTe", 0)])
        b.cp("pool", hTe[:, :, 16:32], hT[:, :, TL - 16:TL], r=[("hTp", 3)], w=[("hTe", 0)])

        def ue_epi(j, ti, tile, pa, pra, pg, prg):
            t0, ts, s = tile
            sg, sgn = b.t32.next()
            b.act(sg[:, :ts], pg, AF.Sigmoid, r=[prg], w=[sgn])
            st, sn = stage.next()
            b.tt("dve", st[:, :ts], pa, sg[:, :ts], ALU.mult, r=[pra, sgn], w=[sn])
            b.dma("sp", b.o_ue[j * 128:(j + 1) * 128, :], st[:, :ts], r=[sn], w=[("pre_out", "ue", j)], key="pre_out")

        b.linear2(W, 512, W, 1024, 512, KC, hTe, "hTe", [(0, 32, 0)], ue_epi)
        self.final_keys = getattr(self, "final_keys", []) + ["pre_out"]
        b.free(es)

    def post_program(self):
        b = self
        vec = b.load_vec(b.vec_d, "")
        modall = b.sb("modall", [128, 128], F32)
        b.dma("sp", modall[:, :], b.modin_d[:, :], w=["modall"], key="modall")
        mod = modall[:, 0:96].rearrange("p (j s) -> p j s", s=2)
        m = dict(mod=mod, amix=modall[:, 96:112].rearrange("p (j s) -> p j s", s=2),
                 affn=modall[:, 112:128].rearrange("p (j s) -> p j s", s=2))
        mres = ["modall", "vec"]
        lt = b.sb("lamt", [128, 128], F32)
        lam2 = b.sb("lam2", [128, 4], F32)
        b.tt("dve", lt[:, 0:64], vec[:, VC_LAM:VC_LAM + 64], vec[:, VC_LAM + 64:VC_LAM + 128], ALU.mult, r=["vec"], w=["lamt"])
        b.tt("dve", lt[:, 64:128], vec[:, VC_LAM + 128:VC_LAM + 192], vec[:, VC_LAM + 192:VC_LAM + 256], ALU.mult, r=["vec"], w=["lamt"])
        b.op("dve", lambda e: e.reduce_sum(out=lam2[:, 0:1], in_=lt[:, 0:64], axis=AX.X), r=["lamt"], w=["lam2a"])
        b.op("dve", lambda e: e.reduce_sum(out=lam2[:, 1:2], in_=lt[:, 64:128], axis=AX.X), r=["lamt"], w=["lam2b"])
        b.act(lam2[:, 0:2], lam2[:, 0:2], AF.Exp, r=["lam2a", "lam2b"], w=["lam2c"])
        b.stt("dve", lam2[:, 2:3], lam2[:, 1:2], vec[:, VC_NLI:VC_NLI + 1], lam2[:, 0:1], ALU.add, ALU.subtract, r=["lam2c", "vec"], w=["nlam"])
        nlam = lam2[:, 2:3]
        b.tt("dve", lam2[:, 3:4], vec[:, VC_DN:VC_DN + 1], vec[:, VC_1LI:VC_1LI + 1], ALU.mult, r=["vec"], w=["gd"])
        gd = lam2[:, 3:4]

        dKTc = b.sb("dKTc", [128, 4, CTXL], BF16)
        dVC = b.sb("dVC", [128, 2, 512], BF16)
        KTc = b.sb("KTc", [128, CTXL], BF16)
        VCc = b.sb("VCc", [128, 2, 128], BF16)
        ZfC = b.sb("ZfC", [128, 2, 512], BF16)
        QTd = b.dscr("QTd", [512, TA], BF16)
        dQTd = b.dscr("dQTd", [512, TA], BF16)
        uTd = b.dscr("uTd", [512, TA], BF16)
        W = b.w["w_in"]
        es = ExitStack()
        b.load_rope(es)
        hT = b.sb("hT", [128, KC, TA], BF16, es)
        b.phase_norm(b.xT_in, "xT_in", TILES, hT, "hT", m["amix"], mod[:, 0:8, :], mres)
        gst = b.rot("gst", [128, 512], BF16, 4, es)

        def u_epi(j, ti, tile, pa, pra, pg, prg):
            t0, ts, s = tile
            sg, sgn = b.t32.next()
            b.act(sg[:, :ts], pg, AF.Sigmoid, r=[prg], w=[sgn])
            st, sn = gst.next()
            b.tt("dve", st[:, :ts], pa, sg[:, :ts], ALU.mult, r=[pra, sgn], w=[sn])
            b.dma("sp", uTd[j * 128:(j + 1) * 128, t0:t0 + ts], st[:, :ts], r=[sn], w=[("uTd", j, ti)], key="uTd")

        b.linear2(W, 512, W, 1024, 512, KC, hT, "hT", TILES, u_epi)

        def q_epi(dst_d, c_base, gain, res):
            def epi(col, ti, tile, ps, psres):
                t0, ts, s = tile
                j = (col - c_base) // 128
                st, sn = gst.next()
                b.rope_epi(ps, psres, tile, st[:, :ts], sn, gain=gain, gres=["vec"])
                b.dma("sp", dst_d[j * 128:(j + 1) * 128, t0:t0 + ts], st[:, :ts], r=[sn], w=[(res, j, ti)], key=res)
            return epi

        b.linear(W, KC, [(1536, 512)], hT, "hT", TILES, q_epi(QTd, 1536, vec[:, VC_QN:VC_QN + 1], "QTd"))
        b.linear(W, KC, [(2304, 512)], hT, "hT", TILES, q_epi(dQTd, 2304, None, "dQTd"))

        def g_epi(col, ti, tile, ps, psres):
            t0, ts, s = tile
            st, sn = gst.next()
            b.act(st[:, :ts], ps, AF.Sigmoid, r=[psres], w=[sn])
            r0 = col - 3840
            b.dma("sp", b.gatesD[r0:r0 + 128, t0:t0 + ts], st[:, :ts], r=[sn], w=[("gD", r0 // 128, ti)], key="gD")

        b.linear(W, KC, [(3840 + 512 * i, 512) for i in range(8)], hT, "hT", TILES, g_epi)
        CT = [TILES[4]]
        ctoks = [(2048, 4), (2176, 4)]

        def ctm(dst, res):
            def epi(tok0, ps, psres):
                i = (tok0 - 2048) // 128
                b.act(dst[:, i, :], ps, AF.Copy, r=[psres], w=[(res, i)])
            return epi

        b.linear_tm(W, 0, 512, hT, "hT", ctoks, ctm(ZfC, "ZfC"))
        b.linear_tm(W, 2176, 128, hT, "hT", ctoks, ctm(VCc, "VCc"))
        b.linear_tm(W, 3328, 512, hT, "hT", ctoks, ctm(dVC, "dVC"))

        def kc_epi(col, ti, tile, ps, psres):
            t0, ts, s = tile
            b.rope_epi(ps, psres, tile, KTc[:, :], "KTc", gain=vec[:, VC_KN:VC_KN + 1], gres=["vec"])

        def dkc_epi(col, ti, tile, ps, psres):
            j = (col - 2816) // 128
            b.rope_epi(ps, psres, tile, dKTc[:, j, :], ("dKTc", j), gain=None)

        b.linear(W, KC, [(2048, 128)], hT, "hT", CT, kc_epi, tile_ids=[4])
        b.linear(W, KC, [(2816, 512)], hT, "hT", CT, dkc_epi, tile_ids=[4])
        b.free(es)

        es_m = ExitStack()
        mT = b.sb("mT", [128, KC, TA], BF16, es_m)

        def load_fm(dst, src_d, res, nm):
            for j in range(4):
                b.dma("sp", dst[:, j, 0:TA], src_d[j * 128:(j + 1) * 128, :],
                      r=[(res, j, ti) for ti in range(5)], w=[nm], key=nm)

        grot = b.rot("gt", [128, 512], BF16, 3, es_m)

        def branch_proj(srcT, srcres, Wb, bi):
            def epi(col, ti, tile, ps, psres):
                t0, ts, s = tile
                j = col // 128
                g, gn = grot.next()
                r0 = bi * D + col
                b.dma("sp", g[:, :ts], b.gatesD[r0:r0 + 128, t0:t0 + ts], r=[("gD", r0 // 128, ti)], w=[gn], key=gn)
                if bi == FIRST_BRANCH:
                    b.tt("dve", mT[:, j, t0:t0 + ts], ps, g[:, :ts], ALU.mult, r=[psres, gn], w=[("mT", j, ti)])
                else:
                    t3, tn = b.t32.next()
                    b.tt("dve", t3[:, :ts], ps, g[:, :ts], ALU.mult, r=[psres, gn], w=[tn])
                    b.tt("pool", mT[:, j, t0:t0 + ts], mT[:, j, t0:t0 + ts], t3[:, :ts], ALU.add, r=[tn], w=[("mT", j, ti)])
            b.linear(Wb, 4, [(0, 512), (512, 512)], srcT, srcres, TILES, epi)

        FIRST_BRANCH = 1
        es = ExitStack()
        uT = b.sb("uT", [128, 4, UTW], BF16, es)
        hv = b.halo.rearrange("(j p) t -> p j t", p=128)
        b.dma("sp", uT[:, :, 0:15], hv[:, :, 0:15], w=["uT"], key="uT")
        b.dma("sp", uT[:, :, 2063:2078], hv[:, :, 15:30], w=["uT"], key="uT")
        b.memset("pool", uT[:, :, 2078:2093], 0.0, w=["uT"])
        b.memset("pool", uT[:, :, 2349:2364], 0.0, w=["uT"])
        for j in range(4):
            b.dma("sp", uT[:, j, UL0:UL0 + TL], uTd[j * 128:(j + 1) * 128, 0:TL], r=[("uTd", j, ti) for ti in range(4)], w=["uT"], key="uT")
            b.dma("sp", uT[:, j, UC0:UC0 + CTXL], uTd[j * 128:(j + 1) * 128, TL:TA], r=[("uTd", j, 4)], w=["uT"], key="uT")
        dg = b.sb("dg", [128, 4, 31, 128], BF16, es)
        for j in range(4):
            for tap in range(31):
                c = VC_CONVW + j * 31 + tap
                b.ts("dve", dg[:, j, tap, :], b.IDENT, vec[:, c:c + 1], None, ALU.mult, r=["cmat", "vec"], w=[("dg", j)])
        convT = b.sb("convT", [128, 4, TA], BF16, es)
        y32 = b.sb("y32", [128, 4, 512], F32, es)
        ybf = b.sb("ybf", [128, 4, 512], BF16, es)
        ysq = b.sb("ysq", [128, 4, 512], BF16, es)
        st4 = b.sb("st4", [128, 4, 512], F32, es)
        for ti, (t0, ts, s) in enumerate(TILES):
            base = t0 if s == 0 else (2078 + t0 - TL)
            ures = ["uT"]
            for j in range(4):
                pb = b.psr.next()
                for tap in range(31):
                    b.mm(b.ps[:, pb, :ts], dg[:, j, tap, :], uT[:, j, base + tap:base + tap + ts], tap == 0, tap == 30,
                         r=[("dg", j)] + ures, w=[("ps", pb)])
                cb = vec[:, VC_CONVB + j:VC_CONVB + j + 1]
                b.act(y32[:, j, :ts], b.ps[:, pb, :ts], AF.Identity, r=[("ps", pb), "vec"], w=[("y32", j)], bias=cb)
                b.act(ysq[:, j, :ts], b.ps[:, pb, :ts], AF.Square, r=[("ps", pb), "vec"], w=[("ysq", j)], bias=cb)
                b.cp("pool", ybf[:, j, :ts], y32[:, j, :ts], r=[("y32", j)], w=[("ybf", j)])
            pm, pq = b.psr.next(), b.psr.next()
            for j in range(4):
                b.mm(b.ps[:, pm, :ts], b.ONESD, ybf[:, j, :ts], j == 0, j == 3, r=[("ybf", j), "cmat"], w=[("ps", pm)])
            for j in range(4):
                b.mm(b.ps[:, pq, :ts], b.ONESD, ysq[:, j, :ts], j == 0, j == 3, r=[("ysq", j), "cmat"], w=[("ps", pq)])
            b.ts("dve", st4[:, 0, :ts], b.ps[:, pm, :ts], 2.0, None, ALU.mult, r=[("ps", pm)], w=["st4m"])
            b.tt("dve", st4[:, 1, :ts], st4[:, 0, :ts], st4[:, 0, :ts], ALU.mult, r=["st4m"], w=["st4q"])
            b.stt("dve", st4[:, 1, :ts], b.ps[:, pq, :ts], 2.0, st4[:, 1, :ts], ALU.mult, ALU.subtract, r=[("ps", pq), "st4q"], w=["st4v"])
            b.rsqrt(st4[:, 2, :ts], "st4r", st4[:, 1, :ts], "st4v", LN_EPS)
            for j in range(4):
                t3, tn = b.t32.next()
                b.tt("dve", t3[:, :ts], y32[:, j, :ts], st4[:, 0, :ts], ALU.subtract, r=[("y32", j), "st4m"], w=[tn])
                b.tt("pool", t3[:, :ts], t3[:, :ts], st4[:, 2, :ts], ALU.mult, r=[tn, "st4r"], w=[tn])
                b.act(convT[:, j, t0:t0 + ts], t3[:, :ts], AF.Silu, r=[tn, "vec"], w=[("convT", ti)],
                      scale=vec[:, VC_LNG + j:VC_LNG + j + 1], bias=vec[:, VC_LNB + j:VC_LNB + j + 1])
        branch_proj(convT, "convT", b.w["w_conv"], 1)
        b.free(es)

        def attention(es, nheads_outer, setup, qsrc, qres, outT, outres, is_diff):
            PT = b.rot("PT", [128, 2, 512], BF16, 3, es)
            Eacc = b.rot("Eacc", [128, 2, 512], F32, 2, es)
            Ehl = b.rot("Ehl", [128, 4, 512], BF16, 2, es)
            rcb = b.rot("rcb", [128, 2, 512], F32, 2, es)
            for ho in range(nheads_outer):
                Kbuf, kres, vfun, vres, qchunks = setup(ho)
                for qc in qchunks:
                    for ti, (t0, ts, s) in enumerate(TILES):
                        chunks = list(range(66)) if s == 0 else [64, 65]
                        sb0 = None
                        ea, ean = Eacc.next()
                        if is_diff:
                            po = [b.psr.next(), b.psr.next()]
                        else:
                            po = [b.psr.next()]
                        for ci, c in enumerate(chunks):
                            while True:
                                pS = b.psr.next()
                                if pS % 2 == 0 and (pS + 1) not in po and pS not in po:
                                    b.psr.i += 1
                                    break
                            for half in range(2):
                                p0 = half * 64
                                b.mm(b.ps[:, pS + half, :ts], Kbuf[p0:p0 + 64, c * 128:(c + 1) * 128], qsrc[p0:p0 + 64, qc, t0:t0 + ts],
                                     True, True, r=[kres, qres], w=[("ps", pS + half)])
                            pt, ptn = PT.next()
                            b.act(pt[:, :, :ts], b.ps[:, pS:pS + 2, :ts], AF.Exp, r=[("ps", pS), ("ps", pS + 1)], w=[ptn], scale=0.125)
                            if ci == 0:
                                b.cp("dve", ea[:, :, :ts], pt[:, :, :ts], r=[ptn], w=[ean])
                            else:
                                b.tt("dve", ea[:, :, :ts], ea[:, :, :ts], pt[:, :, :ts], ALU.add, r=[ptn], w=[ean])
                            first, lastc = ci == 0, ci == len(chunks) - 1
                            if is_diff:
                                for half in range(2):
                                    b.mm(b.ps[:, po[half], :ts], vfun(c, 0), pt[:, half, :ts], first, lastc, r=[vres, ptn], w=[("ps", po[half])])
                            else:
                                b.mm(b.ps[:, po[0], :ts], vfun(c, 0), pt[:, 0, :ts], first, False, r=[vres, ptn], w=[("ps", po[0])])
                                b.mm(b.ps[:, po[0], :ts], vfun(c, 1), pt[:, 1, :ts], False, lastc, r=[vres, ptn], w=[("ps", po[0])])
                        eh, ehn = Ehl.next()
                        b.cp("pool", eh[:, 0:2, :ts], ea[:, :, :ts], r=[ean], w=[ehn + "h"])
                        b.tt("pool", eh[:, 2:4, :ts], ea[:, :, :ts], eh[:, 0:2, :ts], ALU.subtract, r=[ean, ehn + "h"], w=[ehn + "l"])
                        while True:
                            pZ = b.psr.next()
                            if pZ % 2 == 0 and (pZ + 1) not in po and pZ not in po:
                                b.psr.i += 1
                                break
                        for half in range(2):
                            b.mm(b.ps[:, pZ + half, :ts], b.ones1[:, :], eh[:, half, :ts], True, False, r=[ehn + "h", "ones1"], w=[("ps", pZ + half)])
                            b.mm(b.ps[:, pZ + half, :ts], b.ones1[:, :], eh[:, 2 + half, :ts], False, True, r=[ehn + "l", "ones1"], w=[("ps", pZ + half)])
                        rc, rcn = rcb.next()
                        b.op("dve", lambda e, o=rc[:, :, :ts], i=b.ps[:, pZ:pZ + 2, :ts]: e.reciprocal(out=o, in_=i),
                             r=[("ps", pZ), ("ps", pZ + 1)], w=[rcn])
                        if not is_diff:
                            for half in range(2):
                                p0 = half * 64
                                b.tt("dve", outT[p0:p0 + 64, qc, t0:t0 + ts], b.ps[p0:p0 + 64, po[0], :ts], rc[p0:p0 + 64, half, :ts], ALU.mult,
                                     r=[("ps", po[0]), rcn], w=[(outres, ti)])
                        else:
                            t1, n1 = b.t32.next()
                            t2, n2 = b.t32.next()
                            b.tt("dve", t1[:, :ts], b.ps[:, po[0], :ts], rc[:, 0, :ts], ALU.mult, r=[("ps", po[0]), rcn], w=[n1])
                            b.tt("dve", t2[:, :ts], b.ps[:, po[1], :ts], rc[:, 1, :ts], ALU.mult, r=[("ps", po[1]), rcn], w=[n2])
                            b.stt("dve", t1[:, :ts], t2[:, :ts], nlam, t1[:, :ts], ALU.mult, ALU.add, r=[n1, n2, "nlam"], w=[n1])
                            sq, sn = b.tb.next()
                            b.act(sq[:, :ts], t1[:, :ts], AF.Square, r=[n1], w=[sn])
                            pn = b.psr.next()
                            b.mm(b.ps[:, pn, :ts], b.ONESD, sq[:, :ts], True, True, r=[sn, "cmat"], w=[("ps", pn)])
                            rt, rn = b.t32.next()
                            b.rsqrt(rt[:, :ts], rn, b.ps[:, pn, :ts], ("ps", pn), EPS, scale=8.0)
                            b.stt("dve", outT[:, qc, t0:t0 + ts], t1[:, :ts], gd, rt[:, :ts], ALU.mult, ALU.mult, r=[n1, rn, "gd"], w=[(outres, ti)])

        es = ExitStack()
        gOT = b.sb("gOT", [128, 4, TA], BF16, es)
        QT = b.sb("QT", [128, 4, TA], BF16, es)
        load_fm(QT, QTd, "QTd", "QT")
        KTr = b.sb("KTr", [128, SEQ + CTXL], BF16, es)
        Vz = b.sb("Vz", [128, 66, 192], BF16, es)
        b.memset("pool", Vz[:, :, :], 0.0, w=["Vz"])
        Vv = b.V_all.rearrange("(c p) d -> p c d", p=128)

        def gqa_setup(h):
            for half in range(2):
                b.dma("sp", KTr[half * 64:half * 64 + 64, 0:SEQ], b.KT_all[h * 64:h * 64 + 64, :], w=["KTr"], key="KTr")
                b.dma("sp", KTr[half * 64:half * 64 + 64, SEQ:SEQ + CTXL], KTc[h * 64:h * 64 + 64, :], r=["KTc"], w=["KTr"], key="KTr")
            for q4 in range(4):
                for off in (0, 128):
                    b.dma("sp", Vz[:, q4 * 16:(q4 + 1) * 16, off:off + 64], Vv[:, q4 * 16:(q4 + 1) * 16, h * 64:h * 64 + 64], w=["Vz"], key="Vz")
            for off in (0, 128):
                b.cp("pool", Vz[:, 64:66, off:off + 64], VCc[:, :, h * 64:h * 64 + 64], r=[("VCc", 0), ("VCc", 1)], w=["Vz"])
            return KTr, "KTr", (lambda c, half: Vz[:, c, half * 64:half * 64 + 128]), "Vz", [2 * h, 2 * h + 1]

        attention(es, 2, gqa_setup, QT, "QT", gOT, "gOT", False)
        branch_proj(gOT, "gOT", b.w["w_gqa"], 2)
        b.free(es)
        es = ExitStack()
        dOT = b.sb("dOT", [128, 4, TA], BF16, es)
        dQT = b.sb("dQT", [128, 4, TA], BF16, es)
        load_fm(dQT, dQTd, "dQTd", "dQT")
        dKh = b.sb("dKh", [128, SEQ + CTXL], BF16, es)
        dVh = b.sb("dVh", [128, 66, 128], BF16, es)
        dVv = b.dV_all.rearrange("(c p) d -> p c d", p=128)

        def diff_setup(hd):
            b.dma("sp", dKh[:, 0:SEQ], b.dKT_all[hd * 128:(hd + 1) * 128, :], w=["dKh"], key="dKh")
            b.cp("pool", dKh[:, SEQ:SEQ + CTXL], dKTc[:, hd, :], r=[("dKTc", hd)], w=["dKh"])
            for q4 in range(4):
                b.dma("sp", dVh[:, q4 * 16:(q4 + 1) * 16, :], dVv[:, q4 * 16:(q4 + 1) * 16, hd * 128:(hd + 1) * 128], w=["dVh"], key="dVh")
            b.cp("pool", dVh[:, 64:66, :], dVC[:, :, hd * 128:(hd + 1) * 128], r=[("dVC", 0), ("dVC", 1)], w=["dVh"])
            return dKh, "dKh", (lambda c, half: dVh[:, c, :]), "dVh", [hd]

        attention(es, 4, diff_setup, dQT, "dQT", dOT, "dOT", True)
        branch_proj(dOT, "dOT", b.w["w_diff"], 3)
        b.free(es)

        es = ExitStack()
        FT = b.sb("FT", [128, 4, TA], BF16, es)
        Zpm = b.sb("Zpm", [128, 2, 32, 512], BF16, es)
        Zv = b.ZfG.rearrange("(c p) d -> p c d", p=128)
        zld = b.rot("zld", [128, 2, 4, 512], BF16, 2, es)
        for q4 in range(8):
            zl, zn = zld.next()
            b.dma("sp", zl[:, 0, :, :], Zv[:, q4 * 4:(q4 + 1) * 4, :], w=[zn], key=zn)
            b.dma("sp", zl[:, 1, :, :], Zv[:, 32 + q4 * 4:32 + (q4 + 1) * 4, :], w=[zn], key=zn)
            b.tt("dve", Zpm[:, 0, q4 * 4:(q4 + 1) * 4, :], zl[:, 0, :, :], zl[:, 1, :, :], ALU.add, r=[zn], w=["Zp"])
            b.tt("pool", Zpm[:, 1, q4 * 4:(q4 + 1) * 4, :], zl[:, 0, :, :], zl[:, 1, :, :], ALU.subtract, r=[zn], w=["Zm"])
        tcb = b.rot("tcb", [128, 4, 512], BF16, 2, es)
        tsb = b.rot("tsb", [128, 4, 512], BF16, 2, es)
        PcT = b.sb("PcT", [128, 8, 512], BF16, es)
        tcc = b.sb("tcc", [128, 2, 256], BF16, es)
        tsc = b.sb("tsc", [128, 2, 256], BF16, es)
        b.dma("sp", tcc[:], b.dftcc_d[:, :, :], w=["tcc"], key="tcc")
        b.dma("sp", tsc[:], b.dftsc_d[:, :, :], w=["tcc"], key="tcc")
        FT2 = FT[:, :, 0:TL].rearrange("p j (i two) -> p j i two", two=2)

        def four_finish(dst_of, ts, fres):
            for i in range(8):
                if i % 2 == 0:
                    b.act(PcT[:, i, :ts], b.ps[:, i, :ts], AF.Copy, r=[("ps", i)], w=[("PcT", i)])
                else:
                    b.cp("dve", PcT[:, i, :ts], b.ps[:, i, :ts], r=[("ps", i)], w=[("PcT", i)])
            for cj in range(4):
                pb = b.psr.next()
                b.mm(b.ps[:, pb, :ts], b.C128, PcT[:, 2 * cj, :ts], True, False, r=[("PcT", 2 * cj), "cmat"], w=[("ps", pb)])
                b.mm(b.ps[:, pb, :ts], b.S128N, PcT[:, 2 * cj + 1, :ts], False, True, r=[("PcT", 2 * cj + 1), "cmat"], w=[("ps", pb)])
                b.act(dst_of(cj), b.ps[:, pb, :ts], AF.Copy, r=[("ps", pb)], w=[fres])

        for cls in range(2):
            zres = "Zp" if cls == 0 else "Zm"
            for kt2 in range(2):
                for grp in range(8):
                    tc_, tcn = tcb.next()
                    ts_, tsn = tsb.next()
                    b.dma("sp", tc_[:], b.dftc_d[cls, kt2, grp, :, :, :], w=[tcn], key=tcn)
                    b.dma("sp", ts_[:], b.dfts_d[cls, kt2, grp, :, :, :], w=[tsn], key=tsn)
                    for tl in range(4):
                        tch = grp * 4 + tl
                        for cj in range(4):
                            lhs = Zpm[:, cls, tch, cj * 128:(cj + 1) * 128]
                            b.mm(b.ps[:, 2 * cj, :], lhs, tc_[:, tl, :], tch == 0, tch == 31, r=[zres, tcn], w=[("ps", 2 * cj)])
                            b.mm(b.ps[:, 2 * cj + 1, :], lhs, ts_[:, tl, :], tch == 0, tch == 31, r=[zres, tsn], w=[("ps", 2 * cj + 1)])
                four_finish(lambda cj, a=cls, k=kt2: FT2[:, cj, 512 * k:512 * (k + 1), a], 512, "FTl")
        for tch in range(2):
            for cj in range(4):
                lhs = ZfC[:, tch, cj * 128:(cj + 1) * 128]
                b.mm(b.ps[:, 2 * cj, :256], lhs, tcc[:, tch, :], tch == 0, tch == 1, r=[("ZfC", tch), "tcc"], w=[("ps", 2 * cj)])
                b.mm(b.ps[:, 2 * cj + 1, :256], lhs, tsc[:, tch, :], tch == 0, tch == 1, r=[("ZfC", tch), "tcc"], w=[("ps", 2 * cj + 1)])
        four_finish(lambda cj: FT[:, cj, TL:TA], 256, "FTc")
        for ti in range(5):
            b.op("pool", lambda e: e.engine_nop(), r=["FTl", "FTc"], w=[("FT", ti)])
        branch_proj(FT, "FT", b.w["w_four"], 0)
        b.free(es)

        es = ExitStack()
        xin = b.rot("xin", [128, 512], F32, 3, es)
        xv = b.xT_in.rearrange("(kc p) t -> p kc t", p=128)
        xmv = b.xmidD.rearrange("(kc p) t -> p kc t", p=128)

        def wout_epi(col, ti, tile, ps, psres):
            t0, ts, s = tile
            j = col // 128
            xt, xn = xin.next()
            b.dma("sp", xt[:, :ts], xv[:, j, t0:t0 + ts], r=[("xT_in", ti)], w=[xn], key=xn)
            t3, tn = b.t32.next()
            b.stt("dve", t3[:, :ts], ps, mod[:, 16 + j, s:s + 1], xt[:, :ts], ALU.mult, ALU.add, r=[psres, xn] + mres, w=[tn])
            b.dma("sp", xmv[:, j, t0:t0 + ts], t3[:, :ts], r=[tn], w=[("xmid", ti)], key="xmid")

        for ti in range(5):
            b.op("pool", lambda e: e.engine_nop(), r=[("mT", j, ti) for j in range(KC)], w=[("mTa", ti)])
        b.linear(b.w["w_out"], KC, [(0, 512), (512, 512)], mT, "mTa", TILES, wout_epi)
        b.free(es)
        b.free(es_m)

        es = ExitStack()
        hT = b.sb("hTf", [128, KC, TA], BF16, es)
        b.wb2 = b.rot("wc", [128, KC, 512], BF16, 2, es)
        b.phase_norm(b.xmidD, "xmid", TILES, hT, "hTf", m["affn"], mod[:, 24:32, :], mres)
        hid = b.sb("hid", [128, 22, 1024], BF16, es)
        xo_d = b.xT_out
        xov = xo_d.rearrange("(kc p) t -> p kc t", p=128)
        w2v = b.w["w_ffn2"].rearrange("(kc p) n -> p kc n", p=128)
        w2b = b.rot("w2b", [128, 22, 128], BF16, 2, es)
        groups = [[0, 1], [2, 3], [4]]
        for gi, grp in enumerate(groups):
            gt0 = TILES[grp[0]][0]
            gtiles = [TILES[i] for i in grp]

            def h_epi(j, ti, tile, pa, pra, pg, prg):
                t0, ts, s = tile
                sg, sgn = b.t32.next()
                b.act(sg[:, :ts], pa, AF.Silu, r=[pra], w=[sgn])
                b.tt("dve", hid[:, j, t0 - gt0:t0 - gt0 + ts], pg, sg[:, :ts], ALU.mult, r=[prg, sgn], w=[("hid", ti)])

            b.linear2(b.w["w_ffn1"], 0, b.w["w_ffn3"], 0, HID, KC, hT, "hTf", gtiles, h_epi, tile_ids=grp)
            for n in range(KC):
                wb_, wr = w2b.next()
                b.dma("pool", wb_[:, :, :], w2v[:, :, n * 128:(n + 1) * 128], w=[wr], key=wr)
                for ti in grp:
                    t0, ts, s = TILES[ti]
                    pb = b.psr.next()
                    for kc in range(22):
                        b.mm(b.ps[:, pb, :ts], wb_[:, kc, :], hid[:, kc, t0 - gt0:t0 - gt0 + ts], kc == 0, kc == 21,
                             r=[wr, ("hid", ti)], w=[("ps", pb)])
                    xt, xn = b.t32.next()
                    b.dma("sp", xt[:, :ts], xmv[:, n, t0:t0 + ts], r=[("xmid", ti)], w=[xn], key=xn)
                    t3, tn = b.t32.next()
                    b.stt("dve", t3[:, :ts], b.ps[:, pb, :ts], mod[:, 40 + n, s:s + 1], xt[:, :ts], ALU.mult, ALU.add,
                          r=[("ps", pb), xn] + mres, w=[tn])
                    b.dma("sp", xov[:, n, t0:t0 + ts], t3[:, :ts], r=[tn], w=[("xT_out", ti)], key="xT_out")
        b.free(es)
        nf = vec[:, VC_NFIN:VC_NFIN + 8]
        b.phase_norm(b.xT_out, "xT_out", LTILES, None, None, nf, None, ["vec"], out_d=b.out_d, outres="outF")
        self.final_keys = ["xT_out", "outF"]


def _bf(a):
    return np.ascontiguousarray(a).astype(NPBF)


def _rope_tables(chunk):
    t = np.arange(chunk * TL, (chunk + 1) * TL)
    row = (t // 64).astype(np.float32)
    col = (t % 64).astype(np.float32)
    inv = (10000.0 ** (-np.arange(0, 32, 2, dtype=np.float32) / 32)).astype(np.float32)
    ang = np.concatenate([row[:, None] * inv, col[:, None] * inv], -1)
    cos = np.cos(ang).astype(np.float32).T
    sin = np.sin(ang).astype(np.float32).T
    C = np.ones((128, TA), np.float32)
    S = np.zeros((128, TA), np.float32)
    for blk in range(4):
        C[blk * 32:(blk + 1) * 32, :TL] = cos
        S[blk * 32:(blk + 1) * 32, :TL] = sin
    return C, S


def _const_mats():
    m = np.arange(64)
    ang = 2 * np.pi * np.outer(m, m) / 64.0
    c64, s64 = np.cos(ang), np.sin(ang)
    z = np.zeros((64, 64))
    c128 = np.block([[c64, z], [z, c64]])
    s128n = -np.block([[s64, z], [z, s64]])
    rm = np.zeros((128, 128))
    for blk in range(2):
        for i in range(32):
            rm[blk * 64 + i + 32, blk * 64 + i] = -1.0
            rm[blk * 64 + i, blk * 64 + i + 32] = 1.0
    onesd = np.full((128, 128), 1.0 / 1024)
    bones = np.block([[np.ones((64, 64)), z], [z, np.ones((64, 64))]]) / 64.0
    ident = np.eye(128)
    return _bf(np.stack([c128, s128n, rm, onesd, bones, ident], 1))


def _dft_tables(chunk):
    t = np.arange(SEQ // 2, dtype=np.int64)
    sc = 1.0 / math.sqrt(SEQ * 64)
    outc = np.zeros((2, 2, 8, 128, 4, 512), NPBF)
    outs = np.zeros((2, 2, 8, 128, 4, 512), NPBF)
    for cls in range(2):
        for kt2 in range(2):
            kp = 2 * (512 * kt2 + np.arange(512, dtype=np.int64)) + cls
            k = chunk * TL + kp
            ph = (np.outer(t, k) % SEQ).astype(np.float64) * (2 * np.pi / SEQ)
            for nm, arr in ((outc, np.cos(ph) * sc), (outs, np.sin(ph) * sc)):
                a4 = arr.reshape(8, 4, 128, 512)
                nm[cls, kt2] = a4.transpose(0, 2, 1, 3).astype(NPBF)
    return outc, outs


def _dft_ctx():
    t = np.arange(CTXL)
    ph = (np.outer(t, t) % CTXL) * (2 * np.pi / CTXL)
    sc = 1.0 / math.sqrt(CTXL * 64)

    def lay(a):
        return _bf(a.reshape(2, 128, 256).transpose(1, 0, 2))

    return lay(np.cos(ph) * sc), lay(np.sin(ph) * sc)


def _fm(v, n):
    return np.asarray(v, np.float32).reshape(n, 128).T


def _vec(inp, l, bidx):
    v = np.zeros((128, NVEC), np.float32)
    cc = np.stack([_fm(inp["c"][bidx], 8), _fm(inp["c_ctx"], 8)], -1)
    v[:, VC_C:VC_C + 16] = cc.reshape(128, 16)
    v[:, VC_BADA:VC_BADA + 48] = _fm(inp["b_ada"][l], 48)
    v[:, VC_NMIX:VC_NMIX + 8] = _fm(inp["norm_mix"][l], 8)
    v[:, VC_NFFN:VC_NFFN + 8] = _fm(inp["norm_ffn"][l], 8)
    v[:, VC_NFIN:VC_NFIN + 8] = _fm(inp["final_norm"], 8)
    cw = np.asarray(inp["conv_w"][l], np.float32)
    v[:, VC_CONVW:VC_CONVW + 124] = cw.T.reshape(4, 128, 31).transpose(1, 0, 2).reshape(128, 124)
    v[:, VC_CONVB:VC_CONVB + 4] = _fm(inp["conv_b"][l], 4)
    v[:, VC_LNG:VC_LNG + 4] = _fm(inp["conv_ln_g"][l], 4)
    v[:, VC_LNB:VC_LNB + 4] = _fm(inp["conv_ln_b"][l], 4)
    v[:, VC_QN] = np.tile(np.asarray(inp["q_norm"][l], np.float32), 2)
    v[:, VC_KN] = np.tile(np.asarray(inp["k_norm"][l], np.float32), 2)
    v[:, VC_DN] = np.asarray(inp["diff_norm"][l], np.float32)
    lam = np.concatenate([inp["lam_q1"][l], inp["lam_k1"][l], inp["lam_q2"][l], inp["lam_k2"][l]]).astype(np.float32)
    v[:, VC_LAM:VC_LAM + 256] = lam[None, :]
    li = 0.8 - 0.6 * math.exp(-0.3 * l)
    v[:, VC_NLI] = -li
    v[:, VC_1LI] = 1.0 - li
    return v


_PROGS = {}
_CONST = {}
_PRE_COLS = [(0, 512), (512, 1536), (2048, 2304), (2816, 3840)]


def _prog(do_post):
    if do_post not in _PROGS:
        bld = Builder(do_post, True, False, 0.0)
        nc = bld.build()
        _PROGS[do_post] = (nc, bld.out_keys)
    return _PROGS[do_post]


def _consts():
    if not _CONST:
        _CONST["cmat"] = _const_mats()
        _CONST["rope"] = [_rope_tables(j) for j in range(4)]
        _CONST["dft"] = [_dft_tables(j) for j in range(4)]
        _CONST["dftc"] = _dft_ctx()
    return _CONST


DEPTH = 4


def _gather(outs):
    gathered = []
    for bi in range(NB):
        cs = [outs[bi * 4 + j] for j in range(4)]
        g = {
            "ZfG": np.concatenate([np.asarray(c["o_Zf"]) for c in cs], 0),
            "KT_all": np.concatenate([np.asarray(c["o_KT"]) for c in cs], 1),
            "dKT_all": np.concatenate([np.asarray(c["o_dKT"]) for c in cs], 1),
            "V_all": np.concatenate([np.asarray(c["o_V"]) for c in cs], 0),
            "dV_all": np.concatenate([np.asarray(c["o_dV"]) for c in cs], 0),
        }
        g = {k: np.ascontiguousarray(v) for k, v in g.items()}
        ue = [np.asarray(c["o_ue"]) for c in cs]
        zero = np.zeros((512, 15), ue[0].dtype)
        halo = []
        for j in range(4):
            left = ue[j - 1][:, 17:32] if j > 0 else zero
            right = ue[j + 1][:, 0:15] if j < 3 else zero
            halo.append(np.ascontiguousarray(np.concatenate([left, right], 1)))
        g["halo"] = halo
        g["mod"] = [np.asarray(c["o_mod"]) for c in cs]
        gathered.append(g)
    return gathered


def kernel(**inp):
    inp = {k: np.asarray(v) for k, v in inp.items()}
    cst = _consts()
    ncore = 8
    xT = []
    for core in range(ncore):
        bi, j = core // 4, core % 4
        xs = np.concatenate([inp["x"][bi, j * TL:(j + 1) * TL], inp["ctx"][bi]], 0)
        xT.append(np.ascontiguousarray(xs.T.astype(np.float32)))
    wl = lambda n, l: np.ascontiguousarray(inp[n][l].astype(np.float32))

    def pre_w(l):
        if l >= DEPTH:
            return np.zeros((D, 6 * D), np.float32), np.zeros((D, 2816), np.float32)
        w = inp["w_in"][l]
        return wl("w_ada", l), np.ascontiguousarray(np.concatenate([w[:, a:b] for a, b in _PRE_COLS], 1).astype(np.float32))

    res = None
    gathered = None
    for step in range(DEPTH + 1):
        do_post = step > 0
        lpost, lpre = step - 1, step
        nc, out_keys = _prog(do_post)
        wada_n, winp_n = pre_w(lpre)
        in_maps = []
        for core in range(ncore):
            bi, j = core // 4, core % 4
            mp = {"xT": xT[core], "ropeC": cst["rope"][j][0], "ropeS": cst["rope"][j][1], "cmat": cst["cmat"]}
            vec_n = _vec(inp, min(lpre, DEPTH - 1), bi)
            if do_post:
                for n in ("w_in", "w_four", "w_conv", "w_gqa", "w_diff", "w_out", "w_ffn1", "w_ffn3", "w_ffn2"):
                    mp[n] = wl(n, lpost)
                mp["vec"] = _vec(inp, lpost, bi)
                mp["dftc"], mp["dfts"] = cst["dft"][j]
                mp["dftcc"], mp["dftsc"] = cst["dftc"]
                g = gathered[bi]
                for n in ("ZfG", "KT_all", "dKT_all", "V_all", "dV_all"):
                    mp[n] = g[n]
                mp["halo"] = g["halo"][j]
                mp["modin"] = g["mod"][j]
                mp["w_ada_n"], mp["w_inp_n"], mp["vec_n"] = wada_n, winp_n, vec_n
            else:
                mp["w_ada"], mp["w_inp"], mp["vec"] = wada_n, winp_n, vec_n
            in_maps.append(mp)
        res = run_bass_kernel_spmd(nc, in_maps, core_ids=list(range(ncore)))
        outs = res.results
        if do_post:
            xT = [np.asarray(outs[c]["xT_out"]) for c in range(ncore)]
        gathered = _gather(outs)
    out = np.zeros((NB, SEQ, D), np.float32)
    for core in range(ncore):
        bi, j = core // 4, core % 4
        out[bi, j * TL:(j + 1) * TL] = np.asarray(res.results[core]["out"]).T
    return out
```

```python
import math
from contextlib import ExitStack

import numpy as np
import ml_dtypes

import concourse.bass as bass
import concourse.mybir as mybir
from concourse.bass_utils import run_bass_kernel_spmd

F32 = mybir.dt.float32
BF16 = mybir.dt.bfloat16
AF = mybir.ActivationFunctionType
ALU = mybir.AluOpType
AX = mybir.AxisListType
NPBF = ml_dtypes.bfloat16

D = 1024
KC = 8
SEQ = 8192
NB = 2
CTXL = 256
TL = 2048
TA = TL + CTXL
NIN = 7936
HID = 2816
EPS = 1e-6
LN_EPS = 1e-5
TILES = [(0, 512, 0), (512, 512, 0), (1024, 512, 0), (1536, 512, 0), (2048, 256, 1)]
LTILES = TILES[:4]
UL0 = 15
UC0 = 2078 + 15
UTW = 2078 + 286
SEM_EPOCH = 20000

VC_C = 0
VC_BADA = 16
VC_NMIX = 64
VC_NFFN = 72
VC_NFIN = 80
VC_CONVW = 88
VC_CONVB = 212
VC_LNG = 216
VC_LNB = 220
VC_QN = 224
VC_KN = 225
VC_DN = 226
VC_LAM = 227
VC_NLI = 483
VC_1LI = 484
NVEC = 485


class Sched:
    def __init__(self, nc, es):
        self.nc = nc
        self.es = es
        self.ops = []
        self.lastw = {}
        self.readers = {}
        self.dma_cnt = {}

    def add(self, eng, fn, reads=(), writes=(), dma_key=None):
        idx = len(self.ops)
        cdeps = {}
        ddeps = {}

        def dep(o):
            if o is None:
                return
            op = self.ops[o]
            if op["dma_key"] is not None:
                k = op["dma_key"]
                ddeps[k] = 16 * self.dma_cnt[k]
            else:
                e = op["eng"]
                if e == "pe" and eng == "pe" and dma_key is None:
                    return
                if e not in cdeps or cdeps[e] < o:
                    cdeps[e] = o

        for r in list(reads) + list(writes):
            dep(self.lastw.get(r))
        for r in writes:
            rd = self.readers.get(r)
            if rd:
                for o in rd.values():
                    dep(o)
        for e, o in getattr(self, "fence_c", {}).items():
            if not (e == "pe" and eng == "pe" and dma_key is None):
                if e not in cdeps or cdeps[e] < o:
                    cdeps[e] = o
        for k, v in getattr(self, "fence_d", {}).items():
            if ddeps.get(k, 0) < v:
                ddeps[k] = v
        op = dict(eng=eng, fn=fn, cdeps=cdeps, ddeps=ddeps, dma_key=dma_key)
        if dma_key is not None:
            self.dma_cnt[dma_key] = self.dma_cnt.get(dma_key, 0) + 1
        self.ops.append(op)
        for r in writes:
            self.lastw[r] = idx
            self.readers[r] = {}
        for r in reads:
            key = dma_key if dma_key is not None else eng
            self.readers.setdefault(r, {})[("d", key) if dma_key is not None else key] = idx
        return idx

    def fence(self):
        last = {}
        for i, op in enumerate(self.ops):
            if op["dma_key"] is None:
                last[op["eng"]] = i
        self.fence_c = last
        self.fence_d = {k: 16 * v for k, v in self.dma_cnt.items()}

    def emit(self, final_waits=()):
        nc = self.nc
        ops = self.ops
        needed = set()
        for op in ops:
            for o in op["cdeps"].values():
                needed.add(o)
        cnt = {}
        sems = {}

        def new_sem(name):
            return self.es.enter_context(nc.semaphore(name))

        for i, op in enumerate(ops):
            if op["dma_key"] is not None:
                k = ("dma", op["dma_key"])
                if k not in sems:
                    sems[k] = new_sem("d%d" % len(sems))
                op["sem"] = sems[k]
                continue
            if i in needed:
                e = op["eng"]
                c = cnt.get(e, 0)
                ep = c // SEM_EPOCH
                k = (e, ep)
                if k not in sems:
                    sems[k] = new_sem("%s%d" % (e, ep))
                op["sem"] = sems[k]
                op["val"] = c % SEM_EPOCH + 1
                cnt[e] = c + 1
        by_eng = {}
        for i, op in enumerate(ops):
            by_eng.setdefault(op["eng"], []).append(i)
        block = self.es.enter_context(nc.Block())
        dma_cnt = self.dma_cnt

        def run(engname, e):
            waited = {}
            for i in by_eng.get(engname, []):
                op = ops[i]
                for o in op["cdeps"].values():
                    p = ops[o]
                    s, v = p["sem"], p["val"]
                    if waited.get(id(s), 0) < v:
                        e.wait_ge(s, v)
                        waited[id(s)] = v
                for k, v in op["ddeps"].items():
                    s = sems[("dma", k)]
                    if waited.get(id(s), 0) < v:
                        e.wait_ge(s, v)
                        waited[id(s)] = v
                ins = op["fn"](e)
                if op["dma_key"] is not None:
                    ins.then_inc(op["sem"], 16)
                elif "sem" in op:
                    ins.then_inc(op["sem"], 1)
            if engname == "sp":
                for k in final_waits:
                    e.wait_ge(sems[("dma", k)], 16 * dma_cnt[k])

        @block.tensor
        def _(e):
            run("pe", e)

        @block.scalar
        def _(e):
            run("act", e)

        @block.vector
        def _(e):
            run("dve", e)

        @block.gpsimd
        def _(e):
            run("pool", e)

        @block.sync
        def _(e):
            run("sp", e)


class Rot:
    def __init__(self, items):
        self.items = items
        self.i = 0

    def next(self):
        it = self.items[self.i % len(self.items)]
        self.i += 1
        return it


class Builder:
    def __init__(self, do_post, do_pre, last, lam_init):
        self.do_post, self.do_pre, self.last, self.lam_init = do_post, do_pre, last, lam_init
        self.nc = bass.Bass("TRN2", target_bir_lowering=False)
        self.es = ExitStack()
        self.S = Sched(self.nc, self.es)
        self.uid = 0
        self.out_keys = []

    def din(self, name, shape, dt=F32):
        return self.nc.dram_tensor(name, list(shape), dt, kind="ExternalInput").ap()

    def dout(self, name, shape, dt=F32):
        return self.nc.dram_tensor(name, list(shape), dt, kind="ExternalOutput").ap()

    def dscr(self, name, shape, dt):
        return self.nc.dram_tensor(name, list(shape), dt).ap()

    def sb(self, name, shape, dt, es=None):
        self.uid += 1
        return (es or self.es).enter_context(self.nc.sbuf_tensor("%s_%d" % (name, self.uid), list(shape), dt))

    def free(self, es):
        self.S.fence()
        es.close()

    def rot(self, name, shape, dt, n, es=None):
        return Rot([(self.sb("%s%d" % (name, i), shape, dt, es), "%s%d" % (name, i)) for i in range(n)])

    def op(self, eng, fn, r=(), w=()):
        return self.S.add(eng, fn, r, w)

    def dma(self, q, out, in_, r=(), w=(), key=None):
        self.S.add(q, lambda e, o=out, i=in_: e.dma_start(out=o, in_=i), r, w, dma_key=key)

    def mm(self, out, lhsT, rhs, start, stop, r=(), w=()):
        self.S.add("pe", lambda e, o=out, l=lhsT, rr=rhs, s=start, t=stop: e.matmul(o, lhsT=l, rhs=rr, start=s, stop=t), r, w)

    def act(self, out, in_, func, r=(), w=(), bias=None, scale=None):
        kw = {}
        if bias is not None:
            kw["bias"] = bias
        if scale is not None:
            kw["scale"] = scale
        self.S.add("act", lambda e, o=out, i=in_, f=func, k=kw: e.activation(out=o, in_=i, func=f, **k), r, w)

    def tt(self, eng, out, in0, in1, op, r=(), w=()):
        self.S.add(eng, lambda e, o=out, a=in0, b=in1, p=op: e.tensor_tensor(out=o, in0=a, in1=b, op=p), r, w)

    def ts(self, eng, out, in0, s1, s2, op0, op1=None, r=(), w=()):
        if op1 is None:
            self.S.add(eng, lambda e, o=out, a=in0, x=s1, p=op0: e.tensor_scalar(out=o, in0=a, scalar1=x, scalar2=None, op0=p), r, w)
        else:
            self.S.add(eng, lambda e, o=out, a=in0, x=s1, y=s2, p=op0, q=op1: e.tensor_scalar(out=o, in0=a, scalar1=x, scalar2=y, op0=p, op1=q), r, w)

    def stt(self, eng, out, in0, scalar, in1, op0, op1, r=(), w=()):
        self.S.add(eng, lambda e, o=out, a=in0, s=scalar, b=in1, p=op0, q=op1: e.scalar_tensor_tensor(out=o, in0=a, scalar=s, in1=b, op0=p, op1=q), r, w)

    def rsqrt(self, out, outres, in_, inres, eps, scale=1.0):
        self.act(out, in_, AF.Sqrt, r=[inres, "epsc"], w=[outres], bias=self.epsc[:, 0:1] if eps == EPS else self.epsc[:, 1:2], scale=scale)
        self.S.add("dve", lambda e, o=out: e.reciprocal(out=o, in_=o), [outres], [outres])

    def cp(self, eng, out, in_, r=(), w=()):
        self.S.add(eng, lambda e, o=out, i=in_: e.tensor_copy(out=o, in_=i), r, w)

    def memset(self, eng, ap, val, w=()):
        self.S.add(eng, lambda e, a=ap, v=val: e.memset(a, v), (), w)

    def build(self):
        b = self
        post, pre = self.do_post, self.do_pre
        if post:
            b.xT_in = b.din("xT", [D, TA])
            b.w = {n: b.din(n, s) for n, s in [("w_in", [D, NIN]), ("w_four", [512, D]),
                                               ("w_conv", [512, D]), ("w_gqa", [512, D]), ("w_diff", [512, D]),
                                               ("w_out", [D, D]), ("w_ffn1", [D, HID]), ("w_ffn3", [D, HID]),
                                               ("w_ffn2", [HID, D])]}
            b.vec_d = b.din("vec", [128, NVEC])
            b.modin_d = b.din("modin", [128, 128])
            b.ropeC_d = b.din("ropeC", [128, TA])
            b.ropeS_d = b.din("ropeS", [128, TA])
            b.dftc_d = b.din("dftc", [2, 2, 8, 128, 4, 512], BF16)
            b.dfts_d = b.din("dfts", [2, 2, 8, 128, 4, 512], BF16)
            b.dftcc_d = b.din("dftcc", [128, 2, 256], BF16)
            b.dftsc_d = b.din("dftsc", [128, 2, 256], BF16)
            b.cmat_d = b.din("cmat", [128, 6, 128], BF16)
            b.ZfG = b.din("ZfG", [SEQ, 512], BF16)
            b.KT_all = b.din("KT_all", [128, SEQ], BF16)
            b.dKT_all = b.din("dKT_all", [512, SEQ], BF16)
            b.V_all = b.din("V_all", [SEQ, 128], BF16)
            b.dV_all = b.din("dV_all", [SEQ, 512], BF16)
            b.halo = b.din("halo", [512, 30], BF16)
            b.out_d = b.dout("out", [D, TL])
            b.xT_out = b.dout("xT_out", [D, TA])
            self.out_keys += ["out", "xT_out"]
            b.gatesD = b.dscr("gatesD", [4 * D, TA], BF16)
            b.xmidD = b.dscr("xmidD", [D, TA], F32)
        if pre:
            sfx = "_n" if post else ""
            b.w2 = {"w_ada": b.din("w_ada" + sfx, [D, 6 * D]), "w_in": b.din("w_inp" + sfx, [D, 2816])}
            b.vec2_d = b.din("vec_n" if post else "vec", [128, NVEC])
            if post:
                b.xT_pre = b.xT_out
            else:
                b.xT_pre = b.din("xT", [D, TA])
                b.ropeC_d = b.din("ropeC", [128, TA])
                b.ropeS_d = b.din("ropeS", [128, TA])
                b.cmat_d = b.din("cmat", [128, 6, 128], BF16)
            b.o_Zf = b.dout("o_Zf", [TL, 512], BF16)
            b.o_KT = b.dout("o_KT", [128, TL], BF16)
            b.o_dKT = b.dout("o_dKT", [512, TL], BF16)
            b.o_V = b.dout("o_V", [TL, 128], BF16)
            b.o_dV = b.dout("o_dV", [TL, 512], BF16)
            b.o_ue = b.dout("o_ue", [512, 32], BF16)
            b.o_mod = b.dout("o_mod", [128, 128])
            self.out_keys += ["o_Zf", "o_KT", "o_dKT", "o_V", "o_dV", "o_ue", "o_mod"]

        b.ps = b.es.enter_context(b.nc.psum_tensor("ps", [128, 8, 512], F32))
        b.cmat = b.sb("cmat", [128, 6, 128], BF16)
        b.dma("sp", b.cmat[:], b.cmat_d[:, :, :], w=["cmat"], key="cmat")
        b.C128, b.S128N, b.RMAT, b.ONESD, b.BONES, b.IDENT = [b.cmat[:, i, :] for i in range(6)]
        b.ones1 = b.sb("ones1", [128, 128], BF16)
        b.epsc = b.sb("epsc", [128, 2], F32)
        b.memset("pool", b.epsc[:, 0:1], EPS, w=["epsc"])
        b.memset("pool", b.epsc[:, 1:2], LN_EPS, w=["epsc"])
        b.memset("pool", b.ones1[:], 1.0, w=["ones1"])
        b.wb = b.rot("wb", [128, KC, 512], BF16, 2)
        b.t32 = b.rot("t32_", [128, 512], F32, 4)
        b.tb = b.rot("tb_", [128, 512], BF16, 4)
        b.psr = Rot(list(range(8)))

        if post:
            self.post_program()
        if pre:
            self.pre_program()
        finals = []
        self.S.emit(final_waits=self.final_keys)
        return self.nc

    def load_rope(self, es):
        b = self
        b.ropeC = b.sb("ropeC", [128, TA], F32, es)
        b.ropeS = b.sb("ropeS", [128, TA], F32, es)
        b.dma("sp", b.ropeC[:], b.ropeC_d[:, :], w=["ropeC"], key="rope")
        b.dma("sp", b.ropeS[:], b.ropeS_d[:, :], w=["ropeS"], key="rope")
        b.wb2 = b.rot("wc", [128, KC, 512], BF16, 2, es)

    def load_vec(self, vec_d, tag):
        b = self
        vec = b.sb("vec" + tag, [128, NVEC], F32)
        b.dma("sp", vec[:], vec_d[:, :], w=["vec" + tag], key="vec" + tag)
        return vec

    def phase_mod(self, w_ada, vec, tag):
        b = self
        vr = "vec" + tag
        modall = b.sb("modall" + tag, [128, 128], F32)
        mod = modall[:, 0:96].rearrange("p (j s) -> p j s", s=2)
        amix = modall[:, 96:112].rearrange("p (j s) -> p j s", s=2)
        affn = modall[:, 112:128].rearrange("p (j s) -> p j s", s=2)
        cvb = b.sb("cvb" + tag, [128, 16], BF16)
        mr = "mod" + tag
        b.act(cvb[:], vec[:, VC_C:VC_C + 16], AF.Silu, r=[vr], w=["cvb" + tag])
        wv = w_ada.rearrange("(kc p) n -> p kc n", p=128)
        psb = b.psr.next()
        psm = b.ps[:, psb, 0:96]
        first = True
        for pc in range(12):
            wbuf, wr = b.wb.next()
            b.dma("pool", wbuf[:, :, :], wv[:, :, pc * 512:(pc + 1) * 512], w=[wr], key=wr)
            for sub in range(4):
                j = pc * 4 + sub
                for kc in range(KC):
                    b.mm(psm[:, 2 * j:2 * j + 2], wbuf[:, kc, sub * 128:(sub + 1) * 128], cvb[:, 2 * kc:2 * kc + 2],
                         kc == 0, kc == KC - 1, r=[wr, "cvb" + tag], w=[("ps", psb)])
        psm3 = b.ps[:, psb, 0:96].rearrange("p (j s) -> p j s", s=2)
        for s in range(2):
            b.tt("dve", mod[:, :, s], psm3[:, :, s], vec[:, VC_BADA:VC_BADA + 48], ALU.add, r=[("ps", psb), vr], w=[mr])
        for s in range(2):
            b.stt("dve", amix[:, :, s], mod[:, 8:16, s], 1.0, vec[:, VC_NMIX:VC_NMIX + 8], ALU.add, ALU.mult, r=[mr, vr], w=[mr + "a"])
            b.stt("dve", affn[:, :, s], mod[:, 32:40, s], 1.0, vec[:, VC_NFFN:VC_NFFN + 8], ALU.add, ALU.mult, r=[mr, vr], w=[mr + "a"])
        return dict(mod=mod, amix=amix, affn=affn, r=[mr, mr + "a"], modall=modall)

    def phase_norm(self, src_d, srcres, tiles, hT, hres, A, Sh, rres, out_d=None, outres=None, es=None):
        b = self
        xv = src_d.rearrange("(kc p) t -> p kc t", p=128)
        es = ExitStack()
        xr = b.rot("nx", [128, KC, 512], F32, 1, es)
        sq = b.rot("nsq", [128, KC, 512], BF16, 1, es)
        rs = b.rot("nrs", [128, 512], F32, 2, es)
        ov = out_d.rearrange("(kc p) t -> p kc t", p=128) if out_d is not None else None
        for ti, (t0, ts, s) in enumerate(tiles):
            xt, xn = xr.next()
            b.dma("sp", xt[:, :, :ts], xv[:, :, t0:t0 + ts], r=[(srcres, ti)], w=[xn], key=xn)
            sqt, sn = sq.next()
            b.act(sqt[:, :, :ts], xt[:, :, :ts], AF.Square, r=[xn], w=[sn])
            pb = b.psr.next()
            for kc in range(KC):
                b.mm(b.ps[:, pb, :ts], b.ONESD, sqt[:, kc, :ts], kc == 0, kc == KC - 1, r=[sn, "cmat"], w=[("ps", pb)])
            rt, rn = rs.next()
            b.rsqrt(rt[:, :ts], rn, b.ps[:, pb, :ts], ("ps", pb), EPS)
            for kc in range(KC):
                t3, tn = b.t32.next()
                a_ap = A[:, kc, s:s + 1] if len(A.shape) == 3 else A[:, kc:kc + 1]
                b.stt("dve", t3[:, :ts], xt[:, kc, :ts], a_ap, rt[:, :ts], ALU.mult, ALU.mult, r=[xn, rn] + rres, w=[tn])
                if out_d is None:
                    b.act(hT[:, kc, t0:t0 + ts], t3[:, :ts], AF.Identity, r=[tn] + rres, w=[(hres, ti)], bias=Sh[:, kc, s:s + 1])
                else:
                    b.dma("sp", ov[:, kc, t0:t0 + ts], t3[:, :ts], r=[tn], w=[(outres, ti)], key=outres)
        b.free(es)

    def linear(self, W, kc_n, pieces, srcT, srcres, tiles, epi, tile_ids=None):
        b = self
        wv = W.rearrange("(kc p) n -> p kc n", p=128)
        loaded = {}

        def load(i):
            c0, nw = pieces[i]
            wbuf, wr = b.wb.next()
            b.dma("pool", wbuf[:, 0:kc_n, 0:nw], wv[:, :, c0:c0 + nw], w=[wr], key=wr)
            loaded[i] = (wbuf, wr)

        load(0)
        for i, (c0, nw) in enumerate(pieces):
            if i + 1 < len(pieces):
                load(i + 1)
            wbuf, wr = loaded.pop(i)
            for sub in range(nw // 128):
                for tix, (t0, ts, s) in enumerate(tiles):
                    ti = tile_ids[tix] if tile_ids else tix
                    pb = b.psr.next()
                    for kc in range(kc_n):
                        b.mm(b.ps[:, pb, :ts], wbuf[:, kc, sub * 128:(sub + 1) * 128], srcT[:, kc, t0:t0 + ts],
                             kc == 0, kc == kc_n - 1, r=[wr, (srcres, ti)], w=[("ps", pb)])
                    epi(c0 + sub * 128, ti, (t0, ts, s), b.ps[:, pb, :ts], ("ps", pb))

    def linear2(self, Wa, ca, Wb, cb, ncols, kc_n, srcT, srcres, tiles, epi, tile_ids=None):
        b = self
        wva = Wa.rearrange("(kc p) n -> p kc n", p=128)
        wvb = Wb.rearrange("(kc p) n -> p kc n", p=128)
        pieces = [(o, min(512, ncols - o)) for o in range(0, ncols, 512)]
        loaded = {}

        def load(i):
            o, nw = pieces[i]
            wa, war = b.wb.next()
            wb_, wbr = b.wb2.next()
            b.dma("pool", wa[:, 0:kc_n, 0:nw], wva[:, :, ca + o:ca + o + nw], w=[war], key=war)
            b.dma("pool", wb_[:, 0:kc_n, 0:nw], wvb[:, :, cb + o:cb + o + nw], w=[wbr], key=wbr)
            loaded[i] = (wa, war, wb_, wbr)

        load(0)
        for i, (o, nw) in enumerate(pieces):
            if i + 1 < len(pieces):
                load(i + 1)
            wa, war, wb_, wbr = loaded.pop(i)
            for sub in range(nw // 128):
                for tix, (t0, ts, s) in enumerate(tiles):
                    ti = tile_ids[tix] if tile_ids else tix
                    pa, pb = b.psr.next(), b.psr.next()
                    for kc in range(kc_n):
                        b.mm(b.ps[:, pa, :ts], wa[:, kc, sub * 128:(sub + 1) * 128], srcT[:, kc, t0:t0 + ts],
                             kc == 0, kc == kc_n - 1, r=[war, (srcres, ti)], w=[("ps", pa)])
                    for kc in range(kc_n):
                        b.mm(b.ps[:, pb, :ts], wb_[:, kc, sub * 128:(sub + 1) * 128], srcT[:, kc, t0:t0 + ts],
                             kc == 0, kc == kc_n - 1, r=[wbr, (srcres, ti)], w=[("ps", pb)])
                    epi((o + sub * 128) // 128, ti, (t0, ts, s), b.ps[:, pa, :ts], ("ps", pa), b.ps[:, pb, :ts], ("ps", pb))

    def linear_tm(self, W, c0, nw, srcT, srcres, tok_subs, epi):
        b = self
        wv = W.rearrange("(kc p) n -> p kc n", p=128)
        wbuf, wr = b.wb.next()
        b.dma("pool", wbuf[:, :, 0:nw], wv[:, :, c0:c0 + nw], w=[wr], key=wr)
        for tok0, ti in tok_subs:
            pb = b.psr.next()
            for kc in range(KC):
                b.mm(b.ps[:, pb, :nw], srcT[:, kc, tok0:tok0 + 128], wbuf[:, kc, 0:nw], kc == 0, kc == KC - 1,
                     r=[wr, (srcres, ti)], w=[("ps", pb)])
            epi(tok0, b.ps[:, pb, :nw], ("ps", pb))

    def rope_epi(self, ps, psres, tile, dst, dstres, gain=None, gres=()):
        b = self
        t0, ts, s = tile
        xb, xn = b.tb.next()
        if gain is not None:
            sq, sn = b.tb.next()
            b.act(sq[:, :ts], ps, AF.Square, r=[psres], w=[sn])
            p2 = b.psr.next()
            b.mm(b.ps[:, p2, :ts], b.BONES, sq[:, :ts], True, True, r=[sn, "cmat"], w=[("ps", p2)])
            rt, rn = b.t32.next()
            b.rsqrt(rt[:, :ts], rn, b.ps[:, p2, :ts], ("ps", p2), EPS)
            b.stt("dve", xb[:, :ts], ps, gain, rt[:, :ts], ALU.mult, ALU.mult, r=[psres, rn] + list(gres), w=[xn])
        else:
            b.act(xb[:, :ts], ps, AF.Copy, r=[psres], w=[xn])
        p3 = b.psr.next()
        b.mm(b.ps[:, p3, :ts], b.RMAT, xb[:, :ts], True, True, r=[xn, "cmat"], w=[("ps", p3)])
        t1, n1 = b.t32.next()
        t2, n2 = b.t32.next()
        b.tt("dve", t1[:, :ts], xb[:, :ts], b.ropeC[:, t0:t0 + ts], ALU.mult, r=[xn, "ropeC"], w=[n1])
        b.tt("dve", t2[:, :ts], b.ps[:, p3, :ts], b.ropeS[:, t0:t0 + ts], ALU.mult, r=[("ps", p3), "ropeS"], w=[n2])
        b.tt("pool", dst, t1[:, :ts], t2[:, :ts], ALU.add, r=[n1, n2], w=[dstres])

    def pre_program(self):
        b = self
        es = ExitStack()
        vec = b.load_vec(b.vec2_d, "P")
        m = b.phase_mod(b.w2["w_ada"], vec, "P")
        b.load_rope(es)
        hT = b.sb("hTp", [128, KC, TL], BF16, es)
        srcres = "xT_out" if b.do_post else "xT_in"
        b.phase_norm(b.xT_pre, srcres, LTILES, hT, "hTp", m["amix"], m["mod"][:, 0:8, :], m["r"] + ["vecP"])
        W = b.w2["w_in"]
        toks = [(tok0, tok0 // 512) for tok0 in range(0, TL, 128)]
        stage = b.rot("pst", [128, 512], BF16, 3, es)

        def tm_out(dst):
            def epi(tok0, ps, psres):
                st, sn = stage.next()
                nw = ps.shape[1]
                b.act(st[:, :nw], ps, AF.Copy, r=[psres], w=[sn])
                b.dma("sp", dst[tok0:tok0 + 128, :], st[:, :nw], r=[sn], w=[("pre_out", id(dst), tok0)], key="pre_out")
            return epi

        b.dma("sp", b.o_mod[:, :], m["modall"][:, :], r=m["r"], w=["o_mod"], key="pre_out")
        b.linear_tm(W, 0, 512, hT, "hTp", toks, tm_out(b.o_Zf))
        b.linear_tm(W, 1664, 128, hT, "hTp", toks, tm_out(b.o_V))
        b.linear_tm(W, 2304, 512, hT, "hTp", toks, tm_out(b.o_dV))

        def k_epi(dst, c_base, gain):
            def epi(col, ti, tile, ps, psres):
                t0, ts, s = tile
                st, sn = stage.next()
                b.rope_epi(ps, psres, tile, st[:, :ts], sn, gain=gain, gres=["vecP"])
                r0 = col - c_base
                b.dma("sp", dst[r0:r0 + 128, t0:t0 + ts], st[:, :ts], r=[sn], w=[("pre_out", id(dst), col, ti)], key="pre_out")
            return epi

        b.linear(W, KC, [(1536, 128)], hT, "hTp", LTILES, k_epi(b.o_KT, 1536, vec[:, VC_KN:VC_KN + 1]))
        b.linear(W, KC, [(1792, 512)], hT, "hTp", LTILES, k_epi(b.o_dKT, 1792, None))
        hTe = b.sb("hTe", [128, KC, 32], BF16, es)
        b.cp("pool", hTe[:, :, 0:16], hT[:, :, 0:16], r=[("hTp", 0)], w=[("hTe", 0)])
        b.cp("pool", hTe[:, :, 16:32], hT[:, :, TL - 16:TL], r=[("hTp", 3)], w=[("hTe", 0)])

        def ue_epi(j, ti, tile, pa, pra, pg, prg):
            t0, ts, s = tile
            sg, sgn = b.t32.next()
            b.act(sg[:, :ts], pg, AF.Sigmoid, r=[prg], w=[sgn])
            st, sn = stage.next()
            b.tt("dve", st[:, :ts], pa, sg[:, :ts], ALU.mult, r=[pra, sgn], w=[sn])
            b.dma("sp", b.o_ue[j * 128:(j + 1) * 128, :], st[:, :ts], r=[sn], w=[("pre_out", "ue", j)], key="pre_out")

        b.linear2(W, 512, W, 1024, 512, KC, hTe, "hTe", [(0, 32, 0)], ue_epi)
        self.final_keys = getattr(self, "final_keys", []) + ["pre_out"]
        b.free(es)

    def post_program(self):
        b = self
        vec = b.load_vec(b.vec_d, "")
        modall = b.sb("modall", [128, 128], F32)
        b.dma("sp", modall[:, :], b.modin_d[:, :], w=["modall"], key="modall")
        mod = modall[:, 0:96].rearrange("p (j s) -> p j s", s=2)
        m = dict(mod=mod, amix=modall[:, 96:112].rearrange("p (j s) -> p j s", s=2),
                 affn=modall[:, 112:128].rearrange("p (j s) -> p j s", s=2))
        mres = ["modall", "vec"]
        lt = b.sb("lamt", [128, 128], F32)
        lam2 = b.sb("lam2", [128, 4], F32)
        b.tt("dve", lt[:, 0:64], vec[:, VC_LAM:VC_LAM + 64], vec[:, VC_LAM + 64:VC_LAM + 128], ALU.mult, r=["vec"], w=["lamt"])
        b.tt("dve", lt[:, 64:128], vec[:, VC_LAM + 128:VC_LAM + 192], vec[:, VC_LAM + 192:VC_LAM + 256], ALU.mult, r=["vec"], w=["lamt"])
        b.op("dve", lambda e: e.reduce_sum(out=lam2[:, 0:1], in_=lt[:, 0:64], axis=AX.X), r=["lamt"], w=["lam2a"])
        b.op("dve", lambda e: e.reduce_sum(out=lam2[:, 1:2], in_=lt[:, 64:128], axis=AX.X), r=["lamt"], w=["lam2b"])
        b.act(lam2[:, 0:2], lam2[:, 0:2], AF.Exp, r=["lam2a", "lam2b"], w=["lam2c"])
        b.stt("dve", lam2[:, 2:3], lam2[:, 1:2], vec[:, VC_NLI:VC_NLI + 1], lam2[:, 0:1], ALU.add, ALU.subtract, r=["lam2c", "vec"], w=["nlam"])
        nlam = lam2[:, 2:3]
        b.tt("dve", lam2[:, 3:4], vec[:, VC_DN:VC_DN + 1], vec[:, VC_1LI:VC_1LI + 1], ALU.mult, r=["vec"], w=["gd"])
        gd = lam2[:, 3:4]

        dKTc = b.sb("dKTc", [128, 4, CTXL], BF16)
        dVC = b.sb("dVC", [128, 2, 512], BF16)
        KTc = b.sb("KTc", [128, CTXL], BF16)
        VCc = b.sb("VCc", [128, 2, 128], BF16)
        ZfC = b.sb("ZfC", [128, 2, 512], BF16)
        QTd = b.dscr("QTd", [512, TA], BF16)
        dQTd = b.dscr("dQTd", [512, TA], BF16)
        uTd = b.dscr("uTd", [512, TA], BF16)
        W = b.w["w_in"]
        es = ExitStack()
        b.load_rope(es)
        hT = b.sb("hT", [128, KC, TA], BF16, es)
        b.phase_norm(b.xT_in, "xT_in", TILES, hT, "hT", m["amix"], mod[:, 0:8, :], mres)
        gst = b.rot("gst", [128, 512], BF16, 4, es)

        def u_epi(j, ti, tile, pa, pra, pg, prg):
            t0, ts, s = tile
            sg, sgn = b.t32.next()
            b.act(sg[:, :ts], pg, AF.Sigmoid, r=[prg], w=[sgn])
            st, sn = gst.next()
            b.tt("dve", st[:, :ts], pa, sg[:, :ts], ALU.mult, r=[pra, sgn], w=[sn])
            b.dma("sp", uTd[j * 128:(j + 1) * 128, t0:t0 + ts], st[:, :ts], r=[sn], w=[("uTd", j, ti)], key="uTd")

        b.linear2(W, 512, W, 1024, 512, KC, hT, "hT", TILES, u_epi)

        def q_epi(dst_d, c_base, gain, res):
            def epi(col, ti, tile, ps, psres):
                t0, ts, s = tile
                j = (col - c_base) // 128
                st, sn = gst.next()
                b.rope_epi(ps, psres, tile, st[:, :ts], sn, gain=gain, gres=["vec"])
                b.dma("sp", dst_d[j * 128:(j + 1) * 128, t0:t0 + ts], st[:, :ts], r=[sn], w=[(res, j, ti)], key=res)
            return epi

        b.linear(W, KC, [(1536, 512)], hT, "hT", TILES, q_epi(QTd, 1536, vec[:, VC_QN:VC_QN + 1], "QTd"))
        b.linear(W, KC, [(2304, 512)], hT, "hT", TILES, q_epi(dQTd, 2304, None, "dQTd"))

        def g_epi(col, ti, tile, ps, psres):
            t0, ts, s = tile
            st, sn = gst.next()
            b.act(st[:, :ts], ps, AF.Sigmoid, r=[psres], w=[sn])
            r0 = col - 3840
            b.dma("sp", b.gatesD[r0:r0 + 128, t0:t0 + ts], st[:, :ts], r=[sn], w=[("gD", r0 // 128, ti)], key="gD")

        b.linear(W, KC, [(3840 + 512 * i, 512) for i in range(8)], hT, "hT", TILES, g_epi)
        CT = [TILES[4]]
        ctoks = [(2048, 4), (2176, 4)]

        def ctm(dst, res):
            def epi(tok0, ps, psres):
                i = (tok0 - 2048) // 128
                b.act(dst[:, i, :], ps, AF.Copy, r=[psres], w=[(res, i)])
            return epi

        b.linear_tm(W, 0, 512, hT, "hT", ctoks, ctm(ZfC, "ZfC"))
        b.linear_tm(W, 2176, 128, hT, "hT", ctoks, ctm(VCc, "VCc"))
        b.linear_tm(W, 3328, 512, hT, "hT", ctoks, ctm(dVC, "dVC"))

        def kc_epi(col, ti, tile, ps, psres):
            t0, ts, s = tile
            b.rope_epi(ps, psres, tile, KTc[:, :], "KTc", gain=vec[:, VC_KN:VC_KN + 1], gres=["vec"])

        def dkc_epi(col, ti, tile, ps, psres):
            j = (col - 2816) // 128
            b.rope_epi(ps, psres, tile, dKTc[:, j, :], ("dKTc", j), gain=None)

        b.linear(W, KC, [(2048, 128)], hT, "hT", CT, kc_epi, tile_ids=[4])
        b.linear(W, KC, [(2816, 512)], hT, "hT", CT, dkc_epi, tile_ids=[4])
        b.free(es)

        es_m = ExitStack()
        mT = b.sb("mT", [128, KC, TA], BF16, es_m)

        def load_fm(dst, src_d, res, nm):
            for j in range(4):
                b.dma("sp", dst[:, j, 0:TA], src_d[j * 128:(j + 1) * 128, :],
                      r=[(res, j, ti) for ti in range(5)], w=[nm], key=nm)

        grot = b.rot("gt", [128, 512], BF16, 3, es_m)

        def branch_proj(srcT, srcres, Wb, bi):
            def epi(col, ti, tile, ps, psres):
                t0, ts, s = tile
                j = col // 128
                g, gn = grot.next()
                r0 = bi * D + col
                b.dma("sp", g[:, :ts], b.gatesD[r0:r0 + 128, t0:t0 + ts], r=[("gD", r0 // 128, ti)], w=[gn], key=gn)
                if bi == FIRST_BRANCH:
                    b.tt("dve", mT[:, j, t0:t0 + ts], ps, g[:, :ts], ALU.mult, r=[psres, gn], w=[("mT", j, ti)])
                else:
                    t3, tn = b.t32.next()
                    b.tt("dve", t3[:, :ts], ps, g[:, :ts], ALU.mult, r=[psres, gn], w=[tn])
                    b.tt("pool", mT[:, j, t0:t0 + ts], mT[:, j, t0:t0 + ts], t3[:, :ts], ALU.add, r=[tn], w=[("mT", j, ti)])
            b.linear(Wb, 4, [(0, 512), (512, 512)], srcT, srcres, TILES, epi)

        FIRST_BRANCH = 1
        es = ExitStack()
        uT = b.sb("uT", [128, 4, UTW], BF16, es)
        hv = b.halo.rearrange("(j p) t -> p j t", p=128)
        b.dma("sp", uT[:, :, 0:15], hv[:, :, 0:15], w=["uT"], key="uT")
        b.dma("sp", uT[:, :, 2063:2078], hv[:, :, 15:30], w=["uT"], key="uT")
        b.memset("pool", uT[:, :, 2078:2093], 0.0, w=["uT"])
        b.memset("pool", uT[:, :, 2349:2364], 0.0, w=["uT"])
        for j in range(4):
            b.dma("sp", uT[:, j, UL0:UL0 + TL], uTd[j * 128:(j + 1) * 128, 0:TL], r=[("uTd", j, ti) for ti in range(4)], w=["uT"], key="uT")
            b.dma("sp", uT[:, j, UC0:UC0 + CTXL], uTd[j * 128:(j + 1) * 128, TL:TA], r=[("uTd", j, 4)], w=["uT"], key="uT")
        dg = b.sb("dg", [128, 4, 31, 128], BF16, es)
        for j in range(4):
            for tap in range(31):
                c = VC_CONVW + j * 31 + tap
                b.ts("dve", dg[:, j, tap, :], b.IDENT, vec[:, c:c + 1], None, ALU.mult, r=["cmat", "vec"], w=[("dg", j)])
        convT = b.sb("convT", [128, 4, TA], BF16, es)
        y32 = b.sb("y32", [128, 4, 512], F32, es)
        ybf = b.sb("ybf", [128, 4, 512], BF16, es)
        ysq = b.sb("ysq", [128, 4, 512], BF16, es)
        st4 = b.sb("st4", [128, 4, 512], F32, es)
        for ti, (t0, ts, s) in enumerate(TILES):
            base = t0 if s == 0 else (2078 + t0 - TL)
            ures = ["uT"]
            for j in range(4):
                pb = b.psr.next()
                for tap in range(31):
                    b.mm(b.ps[:, pb, :ts], dg[:, j, tap, :], uT[:, j, base + tap:base + tap + ts], tap == 0, tap == 30,
                         r=[("dg", j)] + ures, w=[("ps", pb)])
                cb = vec[:, VC_CONVB + j:VC_CONVB + j + 1]
                b.act(y32[:, j, :ts], b.ps[:, pb, :ts], AF.Identity, r=[("ps", pb), "vec"], w=[("y32", j)], bias=cb)
                b.act(ysq[:, j, :ts], b.ps[:, pb, :ts], AF.Square, r=[("ps", pb), "vec"], w=[("ysq", j)], bias=cb)
                b.cp("pool", ybf[:, j, :ts], y32[:, j, :ts], r=[("y32", j)], w=[("ybf", j)])
            pm, pq = b.psr.next(), b.psr.next()
            for j in range(4):
                b.mm(b.ps[:, pm, :ts], b.ONESD, ybf[:, j, :ts], j == 0, j == 3, r=[("ybf", j), "cmat"], w=[("ps", pm)])
            for j in range(4):
                b.mm(b.ps[:, pq, :ts], b.ONESD, ysq[:, j, :ts], j == 0, j == 3, r=[("ysq", j), "cmat"], w=[("ps", pq)])
            b.ts("dve", st4[:, 0, :ts], b.ps[:, pm, :ts], 2.0, None, ALU.mult, r=[("ps", pm)], w=["st4m"])
            b.tt("dve", st4[:, 1, :ts], st4[:, 0, :ts], st4[:, 0, :ts], ALU.mult, r=["st4m"], w=["st4q"])
            b.stt("dve", st4[:, 1, :ts], b.ps[:, pq, :ts], 2.0, st4[:, 1, :ts], ALU.mult, ALU.subtract, r=[("ps", pq), "st4q"], w=["st4v"])
            b.rsqrt(st4[:, 2, :ts], "st4r", st4[:, 1, :ts], "st4v", LN_EPS)
            for j in range(4):
                t3, tn = b.t32.next()
                b.tt("dve", t3[:, :ts], y32[:, j, :ts], st4[:, 0, :ts], ALU.subtract, r=[("y32", j), "st4m"], w=[tn])
                b.tt("pool", t3[:, :ts], t3[:, :ts], st4[:, 2, :ts], ALU.mult, r=[tn, "st4r"], w=[tn])
                b.act(convT[:, j, t0:t0 + ts], t3[:, :ts], AF.Silu, r=[tn, "vec"], w=[("convT", ti)],
                      scale=vec[:, VC_LNG + j:VC_LNG + j + 1], bias=vec[:, VC_LNB + j:VC_LNB + j + 1])
        branch_proj(convT, "convT", b.w["w_conv"], 1)
        b.free(es)

        def attention(es, nheads_outer, setup, qsrc, qres, outT, outres, is_diff):
            PT = b.rot("PT", [128, 2, 512], BF16, 3, es)
            Eacc = b.rot("Eacc", [128, 2, 512], F32, 2, es)
            Ehl = b.rot("Ehl", [128, 4, 512], BF16, 2, es)
            rcb = b.rot("rcb", [128, 2, 512], F32, 2, es)
            for ho in range(nheads_outer):
                Kbuf, kres, vfun, vres, qchunks = setup(ho)
                for qc in qchunks:
                    for ti, (t0, ts, s) in enumerate(TILES):
                        chunks = list(range(66)) if s == 0 else [64, 65]
                        sb0 = None
                        ea, ean = Eacc.next()
                        if is_diff:
                            po = [b.psr.next(), b.psr.next()]
                        else:
                            po = [b.psr.next()]
                        def issue_S(c):
                            while True:
                                pS = b.psr.next()
                                if pS % 2 == 0 and (pS + 1) not in po and pS not in po:
                                    b.psr.i += 1
                                    break
                            for half in range(2):
                                p0 = half * 64
                                b.mm(b.ps[:, pS + half, :ts], Kbuf[p0:p0 + 64, c * 128:(c + 1) * 128], qsrc[p0:p0 + 64, qc, t0:t0 + ts],
                                     True, True, r=[kres, qres], w=[("ps", pS + half)])
                            return pS

                        pS_next = issue_S(chunks[0])
                        for ci, c in enumerate(chunks):
                            pS = pS_next
                            if ci + 1 < len(chunks):
                                pS_next = issue_S(chunks[ci + 1])
                            pt, ptn = PT.next()
                            b.act(pt[:, :, :ts], b.ps[:, pS:pS + 2, :ts], AF.Exp, r=[("ps", pS), ("ps", pS + 1)], w=[ptn], scale=0.125)
                            if ci == 0:
                                b.cp("dve", ea[:, :, :ts], pt[:, :, :ts], r=[ptn], w=[ean])
                            else:
                                b.tt("dve", ea[:, :, :ts], ea[:, :, :ts], pt[:, :, :ts], ALU.add, r=[ptn], w=[ean])
                            first, lastc = ci == 0, ci == len(chunks) - 1
                            if is_diff:
                                for half in range(2):
                                    b.mm(b.ps[:, po[half], :ts], vfun(c, 0), pt[:, half, :ts], first, lastc, r=[vres, ptn], w=[("ps", po[half])])
                            else:
                                b.mm(b.ps[:, po[0], :ts], vfun(c, 0), pt[:, 0, :ts], first, False, r=[vres, ptn], w=[("ps", po[0])])
                                b.mm(b.ps[:, po[0], :ts], vfun(c, 1), pt[:, 1, :ts], False, lastc, r=[vres, ptn], w=[("ps", po[0])])
                        eh, ehn = Ehl.next()
                        b.cp("pool", eh[:, 0:2, :ts], ea[:, :, :ts], r=[ean], w=[ehn + "h"])
                        b.tt("pool", eh[:, 2:4, :ts], ea[:, :, :ts], eh[:, 0:2, :ts], ALU.subtract, r=[ean, ehn + "h"], w=[ehn + "l"])
                        while True:
                            pZ = b.psr.next()
                            if pZ % 2 == 0 and (pZ + 1) not in po and pZ not in po:
                                b.psr.i += 1
                                break
                        for half in range(2):
                            b.mm(b.ps[:, pZ + half, :ts], b.ones1[:, :], eh[:, half, :ts], True, False, r=[ehn + "h", "ones1"], w=[("ps", pZ + half)])
                            b.mm(b.ps[:, pZ + half, :ts], b.ones1[:, :], eh[:, 2 + half, :ts], False, True, r=[ehn + "l", "ones1"], w=[("ps", pZ + half)])
                        rc, rcn = rcb.next()
                        b.op("dve", lambda e, o=rc[:, :, :ts], i=b.ps[:, pZ:pZ + 2, :ts]: e.reciprocal(out=o, in_=i),
                             r=[("ps", pZ), ("ps", pZ + 1)], w=[rcn])
                        if not is_diff:
                            for half in range(2):
                                p0 = half * 64
                                b.tt("dve", outT[p0:p0 + 64, qc, t0:t0 + ts], b.ps[p0:p0 + 64, po[0], :ts], rc[p0:p0 + 64, half, :ts], ALU.mult,
                                     r=[("ps", po[0]), rcn], w=[(outres, ti)])
                        else:
                            t1, n1 = b.t32.next()
                            t2, n2 = b.t32.next()
                            b.tt("dve", t1[:, :ts], b.ps[:, po[0], :ts], rc[:, 0, :ts], ALU.mult, r=[("ps", po[0]), rcn], w=[n1])
                            b.tt("dve", t2[:, :ts], b.ps[:, po[1], :ts], rc[:, 1, :ts], ALU.mult, r=[("ps", po[1]), rcn], w=[n2])
                            b.stt("dve", t1[:, :ts], t2[:, :ts], nlam, t1[:, :ts], ALU.mult, ALU.add, r=[n1, n2, "nlam"], w=[n1])
                            sq, sn = b.tb.next()
                            b.act(sq[:, :ts], t1[:, :ts], AF.Square, r=[n1], w=[sn])
                            pn = b.psr.next()
                            b.mm(b.ps[:, pn, :ts], b.ONESD, sq[:, :ts], True, True, r=[sn, "cmat"], w=[("ps", pn)])
                            rt, rn = b.t32.next()
                            b.rsqrt(rt[:, :ts], rn, b.ps[:, pn, :ts], ("ps", pn), EPS, scale=8.0)
                            b.stt("dve", outT[:, qc, t0:t0 + ts], t1[:, :ts], gd, rt[:, :ts], ALU.mult, ALU.mult, r=[n1, rn, "gd"], w=[(outres, ti)])

        es = ExitStack()
        gOT = b.sb("gOT", [128, 4, TA], BF16, es)
        QT = b.sb("QT", [128, 4, TA], BF16, es)
        load_fm(QT, QTd, "QTd", "QT")
        KTr = b.sb("KTr", [128, SEQ + CTXL], BF16, es)
        Vz = b.sb("Vz", [128, 66, 192], BF16, es)
        b.memset("pool", Vz[:, :, :], 0.0, w=["Vz"])
        Vv = b.V_all.rearrange("(c p) d -> p c d", p=128)

        def gqa_setup(h):
            for half in range(2):
                b.dma("sp", KTr[half * 64:half * 64 + 64, 0:SEQ], b.KT_all[h * 64:h * 64 + 64, :], w=["KTr"], key="KTr")
                b.dma("sp", KTr[half * 64:half * 64 + 64, SEQ:SEQ + CTXL], KTc[h * 64:h * 64 + 64, :], r=["KTc"], w=["KTr"], key="KTr")
            for q4 in range(4):
                for off in (0, 128):
                    b.dma("sp", Vz[:, q4 * 16:(q4 + 1) * 16, off:off + 64], Vv[:, q4 * 16:(q4 + 1) * 16, h * 64:h * 64 + 64], w=["Vz"], key="Vz")
            for off in (0, 128):
                b.cp("pool", Vz[:, 64:66, off:off + 64], VCc[:, :, h * 64:h * 64 + 64], r=[("VCc", 0), ("VCc", 1)], w=["Vz"])
            return KTr, "KTr", (lambda c, half: Vz[:, c, half * 64:half * 64 + 128]), "Vz", [2 * h, 2 * h + 1]

        attention(es, 2, gqa_setup, QT, "QT", gOT, "gOT", False)
        branch_proj(gOT, "gOT", b.w["w_gqa"], 2)
        b.free(es)
        es = ExitStack()
        dOT = b.sb("dOT", [128, 4, TA], BF16, es)
        dQT = b.sb("dQT", [128, 4, TA], BF16, es)
        load_fm(dQT, dQTd, "dQTd", "dQT")
        dKh = b.sb("dKh", [128, SEQ + CTXL], BF16, es)
        dVh = b.sb("dVh", [128, 66, 128], BF16, es)
        dVv = b.dV_all.rearrange("(c p) d -> p c d", p=128)

        def diff_setup(hd):
            b.dma("sp", dKh[:, 0:SEQ], b.dKT_all[hd * 128:(hd + 1) * 128, :], w=["dKh"], key="dKh")
            b.cp("pool", dKh[:, SEQ:SEQ + CTXL], dKTc[:, hd, :], r=[("dKTc", hd)], w=["dKh"])
            for q4 in range(4):
                b.dma("sp", dVh[:, q4 * 16:(q4 + 1) * 16, :], dVv[:, q4 * 16:(q4 + 1) * 16, hd * 128:(hd + 1) * 128], w=["dVh"], key="dVh")
            b.cp("pool", dVh[:, 64:66, :], dVC[:, :, hd * 128:(hd + 1) * 128], r=[("dVC", 0), ("dVC", 1)], w=["dVh"])
            return dKh, "dKh", (lambda c, half: dVh[:, c, :]), "dVh", [hd]

        attention(es, 4, diff_setup, dQT, "dQT", dOT, "dOT", True)
        branch_proj(dOT, "dOT", b.w["w_diff"], 3)
        b.free(es)

        es = ExitStack()
        FT = b.sb("FT", [128, 4, TA], BF16, es)
        Zpm = b.sb("Zpm", [128, 2, 32, 512], BF16, es)
        Zv = b.ZfG.rearrange("(c p) d -> p c d", p=128)
        zld = b.rot("zld", [128, 2, 4, 512], BF16, 2, es)
        for q4 in range(8):
            zl, zn = zld.next()
            b.dma("sp", zl[:, 0, :, :], Zv[:, q4 * 4:(q4 + 1) * 4, :], w=[zn], key=zn)
            b.dma("sp", zl[:, 1, :, :], Zv[:, 32 + q4 * 4:32 + (q4 + 1) * 4, :], w=[zn], key=zn)
            b.tt("dve", Zpm[:, 0, q4 * 4:(q4 + 1) * 4, :], zl[:, 0, :, :], zl[:, 1, :, :], ALU.add, r=[zn], w=["Zp"])
            b.tt("pool", Zpm[:, 1, q4 * 4:(q4 + 1) * 4, :], zl[:, 0, :, :], zl[:, 1, :, :], ALU.subtract, r=[zn], w=["Zm"])
        tcb = b.rot("tcb", [128, 4, 512], BF16, 2, es)
        tsb = b.rot("tsb", [128, 4, 512], BF16, 2, es)
        PcT = b.sb("PcT", [128, 8, 512], BF16, es)
        tcc = b.sb("tcc", [128, 2, 256], BF16, es)
        tsc = b.sb("tsc", [128, 2, 256], BF16, es)
        b.dma("sp", tcc[:], b.dftcc_d[:, :, :], w=["tcc"], key="tcc")
        b.dma("sp", tsc[:], b.dftsc_d[:, :, :], w=["tcc"], key="tcc")
        FT2 = FT[:, :, 0:TL].rearrange("p j (i two) -> p j i two", two=2)

        def four_finish(dst_of, ts, fres):
            for i in range(8):
                if i % 2 == 0:
                    b.act(PcT[:, i, :ts], b.ps[:, i, :ts], AF.Copy, r=[("ps", i)], w=[("PcT", i)])
                else:
                    b.cp("dve", PcT[:, i, :ts], b.ps[:, i, :ts], r=[("ps", i)], w=[("PcT", i)])
            for cj in range(4):
                pb = b.psr.next()
                b.mm(b.ps[:, pb, :ts], b.C128, PcT[:, 2 * cj, :ts], True, False, r=[("PcT", 2 * cj), "cmat"], w=[("ps", pb)])
                b.mm(b.ps[:, pb, :ts], b.S128N, PcT[:, 2 * cj + 1, :ts], False, True, r=[("PcT", 2 * cj + 1), "cmat"], w=[("ps", pb)])
                b.act(dst_of(cj), b.ps[:, pb, :ts], AF.Copy, r=[("ps", pb)], w=[fres])

        for cls in range(2):
            zres = "Zp" if cls == 0 else "Zm"
            for kt2 in range(2):
                for grp in range(8):
                    tc_, tcn = tcb.next()
                    ts_, tsn = tsb.next()
                    b.dma("sp", tc_[:], b.dftc_d[cls, kt2, grp, :, :, :], w=[tcn], key=tcn)
                    b.dma("sp", ts_[:], b.dfts_d[cls, kt2, grp, :, :, :], w=[tsn], key=tsn)
                    for tl in range(4):
                        tch = grp * 4 + tl
                        for cj in range(4):
                            lhs = Zpm[:, cls, tch, cj * 128:(cj + 1) * 128]
                            b.mm(b.ps[:, 2 * cj, :], lhs, tc_[:, tl, :], tch == 0, tch == 31, r=[zres, tcn], w=[("ps", 2 * cj)])
                            b.mm(b.ps[:, 2 * cj + 1, :], lhs, ts_[:, tl, :], tch == 0, tch == 31, r=[zres, tsn], w=[("ps", 2 * cj + 1)])
                four_finish(lambda cj, a=cls, k=kt2: FT2[:, cj, 512 * k:512 * (k + 1), a], 512, "FTl")
        for tch in range(2):
            for cj in range(4):
                lhs = ZfC[:, tch, cj * 128:(cj + 1) * 128]
                b.mm(b.ps[:, 2 * cj, :256], lhs, tcc[:, tch, :], tch == 0, tch == 1, r=[("ZfC", tch), "tcc"], w=[("ps", 2 * cj)])
                b.mm(b.ps[:, 2 * cj + 1, :256], lhs, tsc[:, tch, :], tch == 0, tch == 1, r=[("ZfC", tch), "tcc"], w=[("ps", 2 * cj + 1)])
        four_finish(lambda cj: FT[:, cj, TL:TA], 256, "FTc")
        for ti in range(5):
            b.op("pool", lambda e: e.engine_nop(), r=["FTl", "FTc"], w=[("FT", ti)])
        branch_proj(FT, "FT", b.w["w_four"], 0)
        b.free(es)

        es = ExitStack()
        xin = b.rot("xin", [128, 512], F32, 3, es)
        xv = b.xT_in.rearrange("(kc p) t -> p kc t", p=128)
        xmv = b.xmidD.rearrange("(kc p) t -> p kc t", p=128)

        def wout_epi(col, ti, tile, ps, psres):
            t0, ts, s = tile
            j = col // 128
            xt, xn = xin.next()
            b.dma("sp", xt[:, :ts], xv[:, j, t0:t0 + ts], r=[("xT_in", ti)], w=[xn], key=xn)
            t3, tn = b.t32.next()
            b.stt("dve", t3[:, :ts], ps, mod[:, 16 + j, s:s + 1], xt[:, :ts], ALU.mult, ALU.add, r=[psres, xn] + mres, w=[tn])
            b.dma("sp", xmv[:, j, t0:t0 + ts], t3[:, :ts], r=[tn], w=[("xmid", ti)], key="xmid")

        for ti in range(5):
            b.op("pool", lambda e: e.engine_nop(), r=[("mT", j, ti) for j in range(KC)], w=[("mTa", ti)])
        b.linear(b.w["w_out"], KC, [(0, 512), (512, 512)], mT, "mTa", TILES, wout_epi)
        b.free(es)
        b.free(es_m)

        es = ExitStack()
        hT = b.sb("hTf", [128, KC, TA], BF16, es)
        b.wb2 = b.rot("wc", [128, KC, 512], BF16, 2, es)
        b.phase_norm(b.xmidD, "xmid", TILES, hT, "hTf", m["affn"], mod[:, 24:32, :], mres)
        hid = b.sb("hid", [128, 22, 1024], BF16, es)
        xo_d = b.xT_out
        xov = xo_d.rearrange("(kc p) t -> p kc t", p=128)
        w2v = b.w["w_ffn2"].rearrange("(kc p) n -> p kc n", p=128)
        w2b = b.rot("w2b", [128, 22, 128], BF16, 2, es)
        groups = [[0, 1], [2, 3], [4]]
        for gi, grp in enumerate(groups):
            gt0 = TILES[grp[0]][0]
            gtiles = [TILES[i] for i in grp]

            def h_epi(j, ti, tile, pa, pra, pg, prg):
                t0, ts, s = tile
                sg, sgn = b.t32.next()
                b.act(sg[:, :ts], pa, AF.Silu, r=[pra], w=[sgn])
                b.tt("dve", hid[:, j, t0 - gt0:t0 - gt0 + ts], pg, sg[:, :ts], ALU.mult, r=[prg, sgn], w=[("hid", ti)])

            b.linear2(b.w["w_ffn1"], 0, b.w["w_ffn3"], 0, HID, KC, hT, "hTf", gtiles, h_epi, tile_ids=grp)
            for n in range(KC):
                wb_, wr = w2b.next()
                b.dma("pool", wb_[:, :, :], w2v[:, :, n * 128:(n + 1) * 128], w=[wr], key=wr)
                for ti in grp:
                    t0, ts, s = TILES[ti]
                    pb = b.psr.next()
                    for kc in range(22):
                        b.mm(b.ps[:, pb, :ts], wb_[:, kc, :], hid[:, kc, t0 - gt0:t0 - gt0 + ts], kc == 0, kc == 21,
                             r=[wr, ("hid", ti)], w=[("ps", pb)])
                    xt, xn = b.t32.next()
                    b.dma("sp", xt[:, :ts], xmv[:, n, t0:t0 + ts], r=[("xmid", ti)], w=[xn], key=xn)
                    t3, tn = b.t32.next()
                    b.stt("dve", t3[:, :ts], b.ps[:, pb, :ts], mod[:, 40 + n, s:s + 1], xt[:, :ts], ALU.mult, ALU.add,
                          r=[("ps", pb), xn] + mres, w=[tn])
                    b.dma("sp", xov[:, n, t0:t0 + ts], t3[:, :ts], r=[tn], w=[("xT_out", ti)], key="xT_out")
        b.free(es)
        nf = vec[:, VC_NFIN:VC_NFIN + 8]
        b.phase_norm(b.xT_out, "xT_out", LTILES, None, None, nf, None, ["vec"], out_d=b.out_d, outres="outF")
        self.final_keys = ["xT_out", "outF"]


def _bf(a):
    return np.ascontiguousarray(a).astype(NPBF)


def _rope_tables(chunk):
    t = np.arange(chunk * TL, (chunk + 1) * TL)
    row = (t // 64).astype(np.float32)
    col = (t % 64).astype(np.float32)
    inv = (10000.0 ** (-np.arange(0, 32, 2, dtype=np.float32) / 32)).astype(np.float32)
    ang = np.concatenate([row[:, None] * inv, col[:, None] * inv], -1)
    cos = np.cos(ang).astype(np.float32).T
    sin = np.sin(ang).astype(np.float32).T
    C = np.ones((128, TA), np.float32)
    S = np.zeros((128, TA), np.float32)
    for blk in range(4):
        C[blk * 32:(blk + 1) * 32, :TL] = cos
        S[blk * 32:(blk + 1) * 32, :TL] = sin
    return C, S


def _const_mats():
    m = np.arange(64)
    ang = 2 * np.pi * np.outer(m, m) / 64.0
    c64, s64 = np.cos(ang), np.sin(ang)
    z = np.zeros((64, 64))
    c128 = np.block([[c64, z], [z, c64]])
    s128n = -np.block([[s64, z], [z, s64]])
    rm = np.zeros((128, 128))
    for blk in range(2):
        for i in range(32):
            rm[blk * 64 + i + 32, blk * 64 + i] = -1.0
            rm[blk * 64 + i, blk * 64 + i + 32] = 1.0
    onesd = np.full((128, 128), 1.0 / 1024)
    bones = np.block([[np.ones((64, 64)), z], [z, np.ones((64, 64))]]) / 64.0
    ident = np.eye(128)
    return _bf(np.stack([c128, s128n, rm, onesd, bones, ident], 1))


def _dft_tables(chunk):
    t = np.arange(SEQ // 2, dtype=np.int64)
    sc = 1.0 / math.sqrt(SEQ * 64)
    outc = np.zeros((2, 2, 8, 128, 4, 512), NPBF)
    outs = np.zeros((2, 2, 8, 128, 4, 512), NPBF)
    for cls in range(2):
        for kt2 in range(2):
            kp = 2 * (512 * kt2 + np.arange(512, dtype=np.int64)) + cls
            k = chunk * TL + kp
            ph = (np.outer(t, k) % SEQ).astype(np.float64) * (2 * np.pi / SEQ)
            for nm, arr in ((outc, np.cos(ph) * sc), (outs, np.sin(ph) * sc)):
                a4 = arr.reshape(8, 4, 128, 512)
                nm[cls, kt2] = a4.transpose(0, 2, 1, 3).astype(NPBF)
    return outc, outs


def _dft_ctx():
    t = np.arange(CTXL)
    ph = (np.outer(t, t) % CTXL) * (2 * np.pi / CTXL)
    sc = 1.0 / math.sqrt(CTXL * 64)

    def lay(a):
        return _bf(a.reshape(2, 128, 256).transpose(1, 0, 2))

    return lay(np.cos(ph) * sc), lay(np.sin(ph) * sc)


def _fm(v, n):
    return np.asarray(v, np.float32).reshape(n, 128).T


def _vec(inp, l, bidx):
    v = np.zeros((128, NVEC), np.float32)
    cc = np.stack([_fm(inp["c"][bidx], 8), _fm(inp["c_ctx"], 8)], -1)
    v[:, VC_C:VC_C + 16] = cc.reshape(128, 16)
    v[:, VC_BADA:VC_BADA + 48] = _fm(inp["b_ada"][l], 48)
    v[:, VC_NMIX:VC_NMIX + 8] = _fm(inp["norm_mix"][l], 8)
    v[:, VC_NFFN:VC_NFFN + 8] = _fm(inp["norm_ffn"][l], 8)
    v[:, VC_NFIN:VC_NFIN + 8] = _fm(inp["final_norm"], 8)
    cw = np.asarray(inp["conv_w"][l], np.float32)
    v[:, VC_CONVW:VC_CONVW + 124] = cw.T.reshape(4, 128, 31).transpose(1, 0, 2).reshape(128, 124)
    v[:, VC_CONVB:VC_CONVB + 4] = _fm(inp["conv_b"][l], 4)
    v[:, VC_LNG:VC_LNG + 4] = _fm(inp["conv_ln_g"][l], 4)
    v[:, VC_LNB:VC_LNB + 4] = _fm(inp["conv_ln_b"][l], 4)
    v[:, VC_QN] = np.tile(np.asarray(inp["q_norm"][l], np.float32), 2)
    v[:, VC_KN] = np.tile(np.asarray(inp["k_norm"][l], np.float32), 2)
    v[:, VC_DN] = np.asarray(inp["diff_norm"][l], np.float32)
    lam = np.concatenate([inp["lam_q1"][l], inp["lam_k1"][l], inp["lam_q2"][l], inp["lam_k2"][l]]).astype(np.float32)
    v[:, VC_LAM:VC_LAM + 256] = lam[None, :]
    li = 0.8 - 0.6 * math.exp(-0.3 * l)
    v[:, VC_NLI] = -li
    v[:, VC_1LI] = 1.0 - li
    return v


_PROGS = {}
_CONST = {}
_PRE_COLS = [(0, 512), (512, 1536), (2048, 2304), (2816, 3840)]


def _prog(do_post):
    if do_post not in _PROGS:
        bld = Builder(do_post, True, False, 0.0)
        nc = bld.build()
        _PROGS[do_post] = (nc, bld.out_keys)
    return _PROGS[do_post]


def _consts():
    if not _CONST:
        _CONST["cmat"] = _const_mats()
        _CONST["rope"] = [_rope_tables(j) for j in range(4)]
        _CONST["dft"] = [_dft_tables(j) for j in range(4)]
        _CONST["dftc"] = _dft_ctx()
    return _CONST


DEPTH = 4


def _gather(outs):
    gathered = []
    for bi in range(NB):
        cs = [outs[bi * 4 + j] for j in range(4)]
        g = {
            "ZfG": np.concatenate([np.asarray(c["o_Zf"]) for c in cs], 0),
            "KT_all": np.concatenate([np.asarray(c["o_KT"]) for c in cs], 1),
            "dKT_all": np.concatenate([np.asarray(c["o_dKT"]) for c in cs], 1),
            "V_all": np.concatenate([np.asarray(c["o_V"]) for c in cs], 0),
            "dV_all": np.concatenate([np.asarray(c["o_dV"]) for c in cs], 0),
        }
        g = {k: np.ascontiguousarray(v) for k, v in g.items()}
        ue = [np.asarray(c["o_ue"]) for c in cs]
        zero = np.zeros((512, 15), ue[0].dtype)
        halo = []
        for j in range(4):
            left = ue[j - 1][:, 17:32] if j > 0 else zero
            right = ue[j + 1][:, 0:15] if j < 3 else zero
            halo.append(np.ascontiguousarray(np.concatenate([left, right], 1)))
        g["halo"] = halo
        g["mod"] = [np.asarray(c["o_mod"]) for c in cs]
        gathered.append(g)
    return gathered


def kernel(**inp):
    inp = {k: np.asarray(v) for k, v in inp.items()}
    cst = _consts()
    ncore = 8
    xT = []
    for core in range(ncore):
        bi, j = core // 4, core % 4
        xs = np.concatenate([inp["x"][bi, j * TL:(j + 1) * TL], inp["ctx"][bi]], 0)
        xT.append(np.ascontiguousarray(xs.T.astype(np.float32)))
    wl = lambda n, l: np.ascontiguousarray(inp[n][l].astype(np.float32))

    def pre_w(l):
        if l >= DEPTH:
            return np.zeros((D, 6 * D), np.float32), np.zeros((D, 2816), np.float32)
        w = inp["w_in"][l]
        return wl("w_ada", l), np.ascontiguousarray(np.concatenate([w[:, a:b] for a, b in _PRE_COLS], 1).astype(np.float32))

    res = None
    gathered = None
    for step in range(DEPTH + 1):
        do_post = step > 0
        lpost, lpre = step - 1, step
        nc, out_keys = _prog(do_post)
        wada_n, winp_n = pre_w(lpre)
        in_maps = []
        for core in range(ncore):
            bi, j = core // 4, core % 4
            mp = {"xT": xT[core], "ropeC": cst["rope"][j][0], "ropeS": cst["rope"][j][1], "cmat": cst["cmat"]}
            vec_n = _vec(inp, min(lpre, DEPTH - 1), bi)
            if do_post:
                for n in ("w_in", "w_four", "w_conv", "w_gqa", "w_diff", "w_out", "w_ffn1", "w_ffn3", "w_ffn2"):
                    mp[n] = wl(n, lpost)
                mp["vec"] = _vec(inp, lpost, bi)
                mp["dftc"], mp["dfts"] = cst["dft"][j]
                mp["dftcc"], mp["dftsc"] = cst["dftc"]
                g = gathered[bi]
                for n in ("ZfG", "KT_all", "dKT_all", "V_all", "dV_all"):
                    mp[n] = g[n]
                mp["halo"] = g["halo"][j]
                mp["modin"] = g["mod"][j]
                mp["w_ada_n"], mp["w_inp_n"], mp["vec_n"] = wada_n, winp_n, vec_n
            else:
                mp["w_ada"], mp["w_inp"], mp["vec"] = wada_n, winp_n, vec_n
            in_maps.append(mp)
        res = run_bass_kernel_spmd(nc, in_maps, core_ids=list(range(ncore)))
        outs = res.results
        if do_post:
            xT = [np.asarray(outs[c]["xT_out"]) for c in range(ncore)]
        gathered = _gather(outs)
    out = np.zeros((NB, SEQ, D), np.float32)
    for core in range(ncore):
        bi, j = core // 4, core % 4
        out[bi, j * TL:(j + 1) * TL] = np.asarray(res.results[core]["out"]).T
    return out
```

```python
import math
from contextlib import ExitStack

import numpy as np
import ml_dtypes

import concourse.bass as bass
import concourse.mybir as mybir
from concourse.bass_utils import run_bass_kernel_spmd

F32 = mybir.dt.float32
BF16 = mybir.dt.bfloat16
AF = mybir.ActivationFunctionType
ALU = mybir.AluOpType
AX = mybir.AxisListType
NPBF = ml_dtypes.bfloat16

D = 1024
KC = 8
SEQ = 8192
NB = 2
CTXL = 256
TL = 2048
TA = TL + CTXL
NIN = 7936
HID = 2816
EPS = 1e-6
LN_EPS = 1e-5
TILES = [(0, 512, 0), (512, 512, 0), (1024, 512, 0), (1536, 512, 0), (2048, 256, 1)]
LTILES = TILES[:4]
UL0 = 15
UC0 = 2078 + 15
UTW = 2078 + 286
SEM_EPOCH = 20000

VC_C = 0
VC_BADA = 16
VC_NMIX = 64
VC_NFFN = 72
VC_NFIN = 80
VC_CONVW = 88
VC_CONVB = 212
VC_LNG = 216
VC_LNB = 220
VC_QN = 224
VC_KN = 225
VC_DN = 226
VC_LAM = 227
VC_NLI = 483
VC_1LI = 484
NVEC = 485


class Sched:
    def __init__(self, nc, es):
        self.nc = nc
        self.es = es
        self.ops = []
        self.lastw = {}
        self.readers = {}
        self.dma_cnt = {}

    def add(self, eng, fn, reads=(), writes=(), dma_key=None):
        idx = len(self.ops)
        cdeps = {}
        ddeps = {}

        def dep(o):
            if o is None:
                return
            op = self.ops[o]
            if op["dma_key"] is not None:
                k = op["dma_key"]
                ddeps[k] = 16 * self.dma_cnt[k]
            else:
                e = op["eng"]
                if e == "pe" and eng == "pe" and dma_key is None:
                    return
                if e not in cdeps or cdeps[e] < o:
                    cdeps[e] = o

        for r in list(reads) + list(writes):
            dep(self.lastw.get(r))
        for r in writes:
            rd = self.readers.get(r)
            if rd:
                for o in rd.values():
                    dep(o)
        for e, o in getattr(self, "fence_c", {}).items():
            if not (e == "pe" and eng == "pe" and dma_key is None):
                if e not in cdeps or cdeps[e] < o:
                    cdeps[e] = o
        for k, v in getattr(self, "fence_d", {}).items():
            if ddeps.get(k, 0) < v:
                ddeps[k] = v
        op = dict(eng=eng, fn=fn, cdeps=cdeps, ddeps=ddeps, dma_key=dma_key)
        if dma_key is not None:
            self.dma_cnt[dma_key] = self.dma_cnt.get(dma_key, 0) + 1
        self.ops.append(op)
        for r in writes:
            self.lastw[r] = idx
            self.readers[r] = {}
        for r in reads:
            key = dma_key if dma_key is not None else eng
            self.readers.setdefault(r, {})[("d", key) if dma_key is not None else key] = idx
        return idx

    def fence(self):
        last = {}
        for i, op in enumerate(self.ops):
            if op["dma_key"] is None:
                last[op["eng"]] = i
        self.fence_c = last
        self.fence_d = {k: 16 * v for k, v in self.dma_cnt.items()}

    def emit(self, final_waits=()):
        nc = self.nc
        ops = self.ops
        needed = set()
        for op in ops:
            for o in op["cdeps"].values():
                needed.add(o)
        cnt = {}
        sems = {}

        def new_sem(name):
            return self.es.enter_context(nc.semaphore(name))

        for i, op in enumerate(ops):
            if op["dma_key"] is not None:
                k = ("dma", op["dma_key"])
                if k not in sems:
                    sems[k] = new_sem("d%d" % len(sems))
                op["sem"] = sems[k]
                continue
            if i in needed:
                e = op["eng"]
                c = cnt.get(e, 0)
                ep = c // SEM_EPOCH
                k = (e, ep)
                if k not in sems:
                    sems[k] = new_sem("%s%d" % (e, ep))
                op["sem"] = sems[k]
                op["val"] = c % SEM_EPOCH + 1
                cnt[e] = c + 1
        by_eng = {}
        for i, op in enumerate(ops):
            by_eng.setdefault(op["eng"], []).append(i)
        block = self.es.enter_context(nc.Block())
        dma_cnt = self.dma_cnt

        def run(engname, e):
            waited = {}
            for i in by_eng.get(engname, []):
                op = ops[i]
                for o in op["cdeps"].values():
                    p = ops[o]
                    s, v = p["sem"], p["val"]
                    if waited.get(id(s), 0) < v:
                        e.wait_ge(s, v)
                        waited[id(s)] = v
                for k, v in op["ddeps"].items():
                    s = sems[("dma", k)]
                    if waited.get(id(s), 0) < v:
                        e.wait_ge(s, v)
                        waited[id(s)] = v
                ins = op["fn"](e)
                if op["dma_key"] is not None:
                    ins.then_inc(op["sem"], 16)
                elif "sem" in op:
                    ins.then_inc(op["sem"], 1)
            if engname == "sp":
                for k in final_waits:
                    e.wait_ge(sems[("dma", k)], 16 * dma_cnt[k])

        @block.tensor
        def _(e):
            run("pe", e)

        @block.scalar
        def _(e):
            run("act", e)

        @block.vector
        def _(e):
            run("dve", e)

        @block.gpsimd
        def _(e):
            run("pool", e)

        @block.sync
        def _(e):
            run("sp", e)


class Rot:
    def __init__(self, items):
        self.items = items
        self.i = 0

    def next(self):
        it = self.items[self.i % len(self.items)]
        self.i += 1
        return it


class Builder:
    def __init__(self, do_post, do_pre, last, lam_init):
        self.do_post, self.do_pre, self.last, self.lam_init = do_post, do_pre, last, lam_init
        self.nc = bass.Bass("TRN2", target_bir_lowering=False)
        self.es = ExitStack()
        self.S = Sched(self.nc, self.es)
        self.uid = 0
        self.out_keys = []

    def din(self, name, shape, dt=F32):
        return self.nc.dram_tensor(name, list(shape), dt, kind="ExternalInput").ap()

    def dout(self, name, shape, dt=F32):
        return self.nc.dram_tensor(name, list(shape), dt, kind="ExternalOutput").ap()

    def dscr(self, name, shape, dt):
        return self.nc.dram_tensor(name, list(shape), dt).ap()

    def sb(self, name, shape, dt, es=None):
        self.uid += 1
        return (es or self.es).enter_context(self.nc.sbuf_tensor("%s_%d" % (name, self.uid), list(shape), dt))

    def free(self, es):
        self.S.fence()
        es.close()

    def rot(self, name, shape, dt, n, es=None):
        return Rot([(self.sb("%s%d" % (name, i), shape, dt, es), "%s%d" % (name, i)) for i in range(n)])

    def op(self, eng, fn, r=(), w=()):
        return self.S.add(eng, fn, r, w)

    def dma(self, q, out, in_, r=(), w=(), key=None):
        self.S.add(q, lambda e, o=out, i=in_: e.dma_start(out=o, in_=i), r, w, dma_key=key)

    def mm(self, out, lhsT, rhs, start, stop, r=(), w=()):
        self.S.add("pe", lambda e, o=out, l=lhsT, rr=rhs, s=start, t=stop: e.matmul(o, lhsT=l, rhs=rr, start=s, stop=t), r, w)

    def act(self, out, in_, func, r=(), w=(), bias=None, scale=None):
        kw = {}
        if bias is not None:
            kw["bias"] = bias
        if scale is not None:
            kw["scale"] = scale
        self.S.add("act", lambda e, o=out, i=in_, f=func, k=kw: e.activation(out=o, in_=i, func=f, **k), r, w)

    def tt(self, eng, out, in0, in1, op, r=(), w=()):
        self.S.add(eng, lambda e, o=out, a=in0, b=in1, p=op: e.tensor_tensor(out=o, in0=a, in1=b, op=p), r, w)

    def ts(self, eng, out, in0, s1, s2, op0, op1=None, r=(), w=()):
        if op1 is None:
            self.S.add(eng, lambda e, o=out, a=in0, x=s1, p=op0: e.tensor_scalar(out=o, in0=a, scalar1=x, scalar2=None, op0=p), r, w)
        else:
            self.S.add(eng, lambda e, o=out, a=in0, x=s1, y=s2, p=op0, q=op1: e.tensor_scalar(out=o, in0=a, scalar1=x, scalar2=y, op0=p, op1=q), r, w)

    def stt(self, eng, out, in0, scalar, in1, op0, op1, r=(), w=()):
        self.S.add(eng, lambda e, o=out, a=in0, s=scalar, b=in1, p=op0, q=op1: e.scalar_tensor_tensor(out=o, in0=a, scalar=s, in1=b, op0=p, op1=q), r, w)

    def rsqrt(self, out, outres, in_, inres, eps, scale=1.0):
        self.act(out, in_, AF.Sqrt, r=[inres, "epsc"], w=[outres], bias=self.epsc[:, 0:1] if eps == EPS else self.epsc[:, 1:2], scale=scale)
        self.S.add("dve", lambda e, o=out: e.reciprocal(out=o, in_=o), [outres], [outres])

    def cp(self, eng, out, in_, r=(), w=()):
        self.S.add(eng, lambda e, o=out, i=in_: e.tensor_copy(out=o, in_=i), r, w)

    def memset(self, eng, ap, val, w=()):
        self.S.add(eng, lambda e, a=ap, v=val: e.memset(a, v), (), w)

    def build(self):
        b = self
        post, pre = self.do_post, self.do_pre
        if post:
            b.xT_in = b.din("xT", [D, TA])
            b.w = {n: b.din(n, s) for n, s in [("w_in", [D, NIN]), ("w_four", [512, D]),
                                               ("w_conv", [512, D]), ("w_gqa", [512, D]), ("w_diff", [512, D]),
                                               ("w_out", [D, D]), ("w_ffn1", [D, HID]), ("w_ffn3", [D, HID]),
                                               ("w_ffn2", [HID, D])]}
            b.vec_d = b.din("vec", [128, NVEC])
            b.modin_d = b.din("modin", [128, 128])
            b.ropeC_d = b.din("ropeC", [128, TA])
            b.ropeS_d = b.din("ropeS", [128, TA])
            b.dftc_d = b.din("dftc", [2, 2, 8, 128, 4, 512], BF16)
            b.dfts_d = b.din("dfts", [2, 2, 8, 128, 4, 512], BF16)
            b.dftcc_d = b.din("dftcc", [128, 2, 256], BF16)
            b.dftsc_d = b.din("dftsc", [128, 2, 256], BF16)
            b.cmat_d = b.din("cmat", [128, 6, 128], BF16)
            b.ZfG = b.din("ZfG", [SEQ, 512], BF16)
            b.KT_all = b.din("KT_all", [128, SEQ], BF16)
            b.dKT_all = b.din("dKT_all", [512, SEQ], BF16)
            b.V_all = b.din("V_all", [SEQ, 128], BF16)
            b.dV_all = b.din("dV_all", [SEQ, 512], BF16)
            b.halo = b.din("halo", [512, 30], BF16)
            b.out_d = b.dout("out", [D, TL])
            b.xT_out = b.dout("xT_out", [D, TA])
            self.out_keys += ["out", "xT_out"]
            b.gatesD = b.dscr("gatesD", [4 * D, TA], BF16)
            b.xmidD = b.dscr("xmidD", [D, TA], F32)
        if pre:
            sfx = "_n" if post else ""
            b.w2 = {"w_ada": b.din("w_ada" + sfx, [D, 6 * D]), "w_in": b.din("w_inp" + sfx, [D, 2816])}
            b.vec2_d = b.din("vec_n" if post else "vec", [128, NVEC])
            if post:
                b.xT_pre = b.xT_out
            else:
                b.xT_pre = b.din("xT", [D, TA])
                b.ropeC_d = b.din("ropeC", [128, TA])
                b.ropeS_d = b.din("ropeS", [128, TA])
                b.cmat_d = b.din("cmat", [128, 6, 128], BF16)
            b.o_Zf = b.dout("o_Zf", [TL, 512], BF16)
            b.o_KT = b.dout("o_KT", [128, TL], BF16)
            b.o_dKT = b.dout("o_dKT", [512, TL], BF16)
            b.o_V = b.dout("o_V", [TL, 128], BF16)
            b.o_dV = b.dout("o_dV", [TL, 512], BF16)
            b.o_ue = b.dout("o_ue", [512, 32], BF16)
            b.o_mod = b.dout("o_mod", [128, 128])
            self.out_keys += ["o_Zf", "o_KT", "o_dKT", "o_V", "o_dV", "o_ue", "o_mod"]

        b.ps = b.es.enter_context(b.nc.psum_tensor("ps", [128, 8, 512], F32))
        b.cmat = b.sb("cmat", [128, 6, 128], BF16)
        b.dma("sp", b.cmat[:], b.cmat_d[:, :, :], w=["cmat"], key="cmat")
        b.C128, b.S128N, b.RMAT, b.ONESD, b.BONES, b.IDENT = [b.cmat[:, i, :] for i in range(6)]
        b.ones1 = b.sb("ones1", [128, 128], BF16)
        b.epsc = b.sb("epsc", [128, 2], F32)
        b.memset("pool", b.epsc[:, 0:1], EPS, w=["epsc"])
        b.memset("pool", b.epsc[:, 1:2], LN_EPS, w=["epsc"])
        b.memset("pool", b.ones1[:], 1.0, w=["ones1"])
        b.wb = b.rot("wb", [128, KC, 512], BF16, 2)
        b.t32 = b.rot("t32_", [128, 512], F32, 4)
        b.tb = b.rot("tb_", [128, 512], BF16, 4)
        b.psr = Rot(list(range(8)))

        if post:
            self.post_program()
        if pre:
            self.pre_program()
        finals = []
        self.S.emit(final_waits=self.final_keys)
        return self.nc

    def load_rope(self, es):
        b = self
        b.ropeC = b.sb("ropeC", [128, TA], F32, es)
        b.ropeS = b.sb("ropeS", [128, TA], F32, es)
        b.dma("sp", b.ropeC[:], b.ropeC_d[:, :], w=["ropeC"], key="rope")
        b.dma("sp", b.ropeS[:], b.ropeS_d[:, :], w=["ropeS"], key="rope")
        b.wb2 = b.rot("wc", [128, KC, 512], BF16, 2, es)

    def load_vec(self, vec_d, tag):
        b = self
        vec = b.sb("vec" + tag, [128, NVEC], F32)
        b.dma("sp", vec[:], vec_d[:, :], w=["vec" + tag], key="vec" + tag)
        return vec

    def phase_mod(self, w_ada, vec, tag):
        b = self
        vr = "vec" + tag
        modall = b.sb("modall" + tag, [128, 128], F32)
        mod = modall[:, 0:96].rearrange("p (j s) -> p j s", s=2)
        amix = modall[:, 96:112].rearrange("p (j s) -> p j s", s=2)
        affn = modall[:, 112:128].rearrange("p (j s) -> p j s", s=2)
        cvb = b.sb("cvb" + tag, [128, 16], BF16)
        mr = "mod" + tag
        b.act(cvb[:], vec[:, VC_C:VC_C + 16], AF.Silu, r=[vr], w=["cvb" + tag])
        wv = w_ada.rearrange("(kc p) n -> p kc n", p=128)
        psb = b.psr.next()
        psm = b.ps[:, psb, 0:96]
        first = True
        for pc in range(12):
            wbuf, wr = b.wb.next()
            b.dma("pool", wbuf[:, :, :], wv[:, :, pc * 512:(pc + 1) * 512], w=[wr], key=wr)
            for sub in range(4):
                j = pc * 4 + sub
                for kc in range(KC):
                    b.mm(psm[:, 2 * j:2 * j + 2], wbuf[:, kc, sub * 128:(sub + 1) * 128], cvb[:, 2 * kc:2 * kc + 2],
                         kc == 0, kc == KC - 1, r=[wr, "cvb" + tag], w=[("ps", psb)])
        psm3 = b.ps[:, psb, 0:96].rearrange("p (j s) -> p j s", s=2)
        for s in range(2):
            b.tt("dve", mod[:, :, s], psm3[:, :, s], vec[:, VC_BADA:VC_BADA + 48], ALU.add, r=[("ps", psb), vr], w=[mr])
        for s in range(2):
            b.stt("dve", amix[:, :, s], mod[:, 8:16, s], 1.0, vec[:, VC_NMIX:VC_NMIX + 8], ALU.add, ALU.mult, r=[mr, vr], w=[mr + "a"])
            b.stt("dve", affn[:, :, s], mod[:, 32:40, s], 1.0, vec[:, VC_NFFN:VC_NFFN + 8], ALU.add, ALU.mult, r=[mr, vr], w=[mr + "a"])
        return dict(mod=mod, amix=amix, affn=affn, r=[mr, mr + "a"], modall=modall)

    def phase_norm(self, src_d, srcres, tiles, hT, hres, A, Sh, rres, out_d=None, outres=None, es=None):
        b = self
        xv = src_d.rearrange("(kc p) t -> p kc t", p=128)
        es = ExitStack()
        xr = b.rot("nx", [128, KC, 512], F32, 1, es)
        sq = b.rot("nsq", [128, KC, 512], BF16, 1, es)
        rs = b.rot("nrs", [128, 512], F32, 2, es)
        ov = out_d.rearrange("(kc p) t -> p kc t", p=128) if out_d is not None else None
        for ti, (t0, ts, s) in enumerate(tiles):
            xt, xn = xr.next()
            b.dma("sp", xt[:, :, :ts], xv[:, :, t0:t0 + ts], r=[(srcres, ti)], w=[xn], key=xn)
            sqt, sn = sq.next()
            b.act(sqt[:, :, :ts], xt[:, :, :ts], AF.Square, r=[xn], w=[sn])
            pb = b.psr.next()
            for kc in range(KC):
                b.mm(b.ps[:, pb, :ts], b.ONESD, sqt[:, kc, :ts], kc == 0, kc == KC - 1, r=[sn, "cmat"], w=[("ps", pb)])
            rt, rn = rs.next()
            b.rsqrt(rt[:, :ts], rn, b.ps[:, pb, :ts], ("ps", pb), EPS)
            for kc in range(KC):
                t3, tn = b.t32.next()
                a_ap = A[:, kc, s:s + 1] if len(A.shape) == 3 else A[:, kc:kc + 1]
                b.stt("dve", t3[:, :ts], xt[:, kc, :ts], a_ap, rt[:, :ts], ALU.mult, ALU.mult, r=[xn, rn] + rres, w=[tn])
                if out_d is None:
                    b.act(hT[:, kc, t0:t0 + ts], t3[:, :ts], AF.Identity, r=[tn] + rres, w=[(hres, ti)], bias=Sh[:, kc, s:s + 1])
                else:
                    b.dma("sp", ov[:, kc, t0:t0 + ts], t3[:, :ts], r=[tn], w=[(outres, ti)], key=outres)
        b.free(es)

    def linear(self, W, kc_n, pieces, srcT, srcres, tiles, epi, tile_ids=None):
        b = self
        wv = W.rearrange("(kc p) n -> p kc n", p=128)
        loaded = {}

        def load(i):
            c0, nw = pieces[i]
            wbuf, wr = b.wb.next()
            b.dma("pool", wbuf[:, 0:kc_n, 0:nw], wv[:, :, c0:c0 + nw], w=[wr], key=wr)
            loaded[i] = (wbuf, wr)

        load(0)
        for i, (c0, nw) in enumerate(pieces):
            if i + 1 < len(pieces):
                load(i + 1)
            wbuf, wr = loaded.pop(i)
            for sub in range(nw // 128):
                for tix, (t0, ts, s) in enumerate(tiles):
                    ti = tile_ids[tix] if tile_ids else tix
                    pb = b.psr.next()
                    for kc in range(kc_n):
                        b.mm(b.ps[:, pb, :ts], wbuf[:, kc, sub * 128:(sub + 1) * 128], srcT[:, kc, t0:t0 + ts],
                             kc == 0, kc == kc_n - 1, r=[wr, (srcres, ti)], w=[("ps", pb)])
                    epi(c0 + sub * 128, ti, (t0, ts, s), b.ps[:, pb, :ts], ("ps", pb))

    def linear2(self, Wa, ca, Wb, cb, ncols, kc_n, srcT, srcres, tiles, epi, tile_ids=None):
        b = self
        wva = Wa.rearrange("(kc p) n -> p kc n", p=128)
        wvb = Wb.rearrange("(kc p) n -> p kc n", p=128)
        pieces = [(o, min(512, ncols - o)) for o in range(0, ncols, 512)]
        loaded = {}

        def load(i):
            o, nw = pieces[i]
            wa, war = b.wb.next()
            wb_, wbr = b.wb2.next()
            b.dma("pool", wa[:, 0:kc_n, 0:nw], wva[:, :, ca + o:ca + o + nw], w=[war], key=war)
            b.dma("pool", wb_[:, 0:kc_n, 0:nw], wvb[:, :, cb + o:cb + o + nw], w=[wbr], key=wbr)
            loaded[i] = (wa, war, wb_, wbr)

        load(0)
        for i, (o, nw) in enumerate(pieces):
            if i + 1 < len(pieces):
                load(i + 1)
            wa, war, wb_, wbr = loaded.pop(i)
            for sub in range(nw // 128):
                for tix, (t0, ts, s) in enumerate(tiles):
                    ti = tile_ids[tix] if tile_ids else tix
                    pa, pb = b.psr.next(), b.psr.next()
                    for kc in range(kc_n):
                        b.mm(b.ps[:, pa, :ts], wa[:, kc, sub * 128:(sub + 1) * 128], srcT[:, kc, t0:t0 + ts],
                             kc == 0, kc == kc_n - 1, r=[war, (srcres, ti)], w=[("ps", pa)])
                    for kc in range(kc_n):
                        b.mm(b.ps[:, pb, :ts], wb_[:, kc, sub * 128:(sub + 1) * 128], srcT[:, kc, t0:t0 + ts],
                             kc == 0, kc == kc_n - 1, r=[wbr, (srcres, ti)], w=[("ps", pb)])
                    epi((o + sub * 128) // 128, ti, (t0, ts, s), b.ps[:, pa, :ts], ("ps", pa), b.ps[:, pb, :ts], ("ps", pb))

    def linear_tm(self, W, c0, nw, srcT, srcres, tok_subs, epi):
        b = self
        wv = W.rearrange("(kc p) n -> p kc n", p=128)
        wbuf, wr = b.wb.next()
        b.dma("pool", wbuf[:, :, 0:nw], wv[:, :, c0:c0 + nw], w=[wr], key=wr)
        for tok0, ti in tok_subs:
            pb = b.psr.next()
            for kc in range(KC):
                b.mm(b.ps[:, pb, :nw], srcT[:, kc, tok0:tok0 + 128], wbuf[:, kc, 0:nw], kc == 0, kc == KC - 1,
                     r=[wr, (srcres, ti)], w=[("ps", pb)])
            epi(tok0, b.ps[:, pb, :nw], ("ps", pb))

    def rope_epi(self, ps, psres, tile, dst, dstres, gain=None, gres=()):
        b = self
        t0, ts, s = tile
        xb, xn = b.tb.next()
        if gain is not None:
            sq, sn = b.tb.next()
            b.act(sq[:, :ts], ps, AF.Square, r=[psres], w=[sn])
            p2 = b.psr.next()
            b.mm(b.ps[:, p2, :ts], b.BONES, sq[:, :ts], True, True, r=[sn, "cmat"], w=[("ps", p2)])
            rt, rn = b.t32.next()
            b.rsqrt(rt[:, :ts], rn, b.ps[:, p2, :ts], ("ps", p2), EPS)
            b.stt("dve", xb[:, :ts], ps, gain, rt[:, :ts], ALU.mult, ALU.mult, r=[psres, rn] + list(gres), w=[xn])
        else:
            b.act(xb[:, :ts], ps, AF.Copy, r=[psres], w=[xn])
        p3 = b.psr.next()
        b.mm(b.ps[:, p3, :ts], b.RMAT, xb[:, :ts], True, True, r=[xn, "cmat"], w=[("ps", p3)])
        t1, n1 = b.t32.next()
        t2, n2 = b.t32.next()
        b.tt("dve", t1[:, :ts], xb[:, :ts], b.ropeC[:, t0:t0 + ts], ALU.mult, r=[xn, "ropeC"], w=[n1])
        b.tt("dve", t2[:, :ts], b.ps[:, p3, :ts], b.ropeS[:, t0:t0 + ts], ALU.mult, r=[("ps", p3), "ropeS"], w=[n2])
        b.tt("pool", dst, t1[:, :ts], t2[:, :ts], ALU.add, r=[n1, n2], w=[dstres])

    def pre_program(self):
        b = self
        es = ExitStack()
        vec = b.load_vec(b.vec2_d, "P")
        m = b.phase_mod(b.w2["w_ada"], vec, "P")
        b.load_rope(es)
        hT = b.sb("hTp", [128, KC, TL], BF16, es)
        srcres = "xT_out" if b.do_post else "xT_in"
        b.phase_norm(b.xT_pre, srcres, LTILES, hT, "hTp", m["amix"], m["mod"][:, 0:8, :], m["r"] + ["vecP"])
        W = b.w2["w_in"]
        toks = [(tok0, tok0 // 512) for tok0 in range(0, TL, 128)]
        stage = b.rot("pst", [128, 512], BF16, 3, es)

        def tm_out(dst):
            def epi(tok0, ps, psres):
                st, sn = stage.next()
                nw = ps.shape[1]
                b.act(st[:, :nw], ps, AF.Copy, r=[psres], w=[sn])
                b.dma("sp", dst[tok0:tok0 + 128, :], st[:, :nw], r=[sn], w=[("pre_out", id(dst), tok0)], key="pre_out")
            return epi

        b.dma("sp", b.o_mod[:, :], m["modall"][:, :], r=m["r"], w=["o_mod"], key="pre_out")
        b.linear_tm(W, 0, 512, hT, "hTp", toks, tm_out(b.o_Zf))
        b.linear_tm(W, 1664, 128, hT, "hTp", toks, tm_out(b.o_V))
        b.linear_tm(W, 2304, 512, hT, "hTp", toks, tm_out(b.o_dV))

        def k_epi(dst, c_base, gain):
            def epi(col, ti, tile, ps, psres):
                t0, ts, s = tile
                st, sn = stage.next()
                b.rope_epi(ps, psres, tile, st[:, :ts], sn, gain=gain, gres=["vecP"])
                r0 = col - c_base
                b.dma("sp", dst[r0:r0 + 128, t0:t0 + ts], st[:, :ts], r=[sn], w=[("pre_out", id(dst), col, ti)], key="pre_out")
            return epi

        b.linear(W, KC, [(1536, 128)], hT, "hTp", LTILES, k_epi(b.o_KT, 1536, vec[:, VC_KN:VC_KN + 1]))
        b.linear(W, KC, [(1792, 512)], hT, "hTp", LTILES, k_epi(b.o_dKT, 1792, None))
        hTe = b.sb("hTe", [128, KC, 32], BF16, es)
        b.cp("pool", hTe[:, :, 0:16], hT[:, :, 0:16], r=[("hTp", 0)], w=[("hTe", 0)])
        b.cp("pool", hTe[:, :, 16:32], hT[:, :, TL - 16:TL], r=[("hTp", 3)], w=[("hTe", 0)])

        def ue_epi(j, ti, tile, pa, pra, pg, prg):
            t0, ts, s = tile
            sg, sgn = b.t32.next()
            b.act(sg[:, :ts], pg, AF.Sigmoid, r=[prg], w=[sgn])
            st, sn = stage.next()
            b.tt("dve", st[:, :ts], pa, sg[:, :ts], ALU.mult, r=[pra, sgn], w=[sn])
            b.dma("sp", b.o_ue[j * 128:(j + 1) * 128, :], st[:, :ts], r=[sn], w=[("pre_out", "ue", j)], key="pre_out")

        b.linear2(W, 512, W, 1024, 512, KC, hTe, "hTe", [(0, 32, 0)], ue_epi)
        self.final_keys = getattr(self, "final_keys", []) + ["pre_out"]
        b.free(es)

    def post_program(self):
        b = self
        vec = b.load_vec(b.vec_d, "")
        modall = b.sb("modall", [128, 128], F32)
        b.dma("sp", modall[:, :], b.modin_d[:, :], w=["modall"], key="modall")
        mod = modall[:, 0:96].rearrange("p (j s) -> p j s", s=2)
        m = dict(mod=mod, amix=modall[:, 96:112].rearrange("p (j s) -> p j s", s=2),
                 affn=modall[:, 112:128].rearrange("p (j s) -> p j s", s=2))
        mres = ["modall", "vec"]
        lt = b.sb("lamt", [128, 128], F32)
        lam2 = b.sb("lam2", [128, 4], F32)
        b.tt("dve", lt[:, 0:64], vec[:, VC_LAM:VC_LAM + 64], vec[:, VC_LAM + 64:VC_LAM + 128], ALU.mult, r=["vec"], w=["lamt"])
        b.tt("dve", lt[:, 64:128], vec[:, VC_LAM + 128:VC_LAM + 192], vec[:, VC_LAM + 192:VC_LAM + 256], ALU.mult, r=["vec"], w=["lamt"])
        b.op("dve", lambda e: e.reduce_sum(out=lam2[:, 0:1], in_=lt[:, 0:64], axis=AX.X), r=["lamt"], w=["lam2a"])
        b.op("dve", lambda e: e.reduce_sum(out=lam2[:, 1:2], in_=lt[:, 64:128], axis=AX.X), r=["lamt"], w=["lam2b"])
        b.act(lam2[:, 0:2], lam2[:, 0:2], AF.Exp, r=["lam2a", "lam2b"], w=["lam2c"])
        b.stt("dve", lam2[:, 2:3], lam2[:, 1:2], vec[:, VC_NLI:VC_NLI + 1], lam2[:, 0:1], ALU.add, ALU.subtract, r=["lam2c", "vec"], w=["nlam"])
        nlam = lam2[:, 2:3]
        b.tt("dve", lam2[:, 3:4], vec[:, VC_DN:VC_DN + 1], vec[:, VC_1LI:VC_1LI + 1], ALU.mult, r=["vec"], w=["gd"])
        gd = lam2[:, 3:4]

        dKTc = b.sb("dKTc", [128, 4, CTXL], BF16)
        dVC = b.sb("dVC", [128, 2, 512], BF16)
        KTc = b.sb("KTc", [128, CTXL], BF16)
        VCc = b.sb("VCc", [128, 2, 128], BF16)
        ZfC = b.sb("ZfC", [128, 2, 512], BF16)
        QTd = b.dscr("QTd", [512, TA], BF16)
        dQTd = b.dscr("dQTd", [512, TA], BF16)
        uTd = b.dscr("uTd", [512, TA], BF16)
        W = b.w["w_in"]
        es = ExitStack()
        b.load_rope(es)
        hT = b.sb("hT", [128, KC, TA], BF16, es)
        b.phase_norm(b.xT_in, "xT_in", TILES, hT, "hT", m["amix"], mod[:, 0:8, :], mres)
        gst = b.rot("gst", [128, 512], BF16, 4, es)

        def u_epi(j, ti, tile, pa, pra, pg, prg):
            t0, ts, s = tile
            sg, sgn = b.t32.next()
            b.act(sg[:, :ts], pg, AF.Sigmoid, r=[prg], w=[sgn])
            st, sn = gst.next()
            b.tt("dve", st[:, :ts], pa, sg[:, :ts], ALU.mult, r=[pra, sgn], w=[sn])
            b.dma("sp", uTd[j * 128:(j + 1) * 128, t0:t0 + ts], st[:, :ts], r=[sn], w=[("uTd", j, ti)], key="uTd")

        b.linear2(W, 512, W, 1024, 512, KC, hT, "hT", TILES, u_epi)

        def q_epi(dst_d, c_base, gain, res):
            def epi(col, ti, tile, ps, psres):
                t0, ts, s = tile
                j = (col - c_base) // 128
                st, sn = gst.next()
                b.rope_epi(ps, psres, tile, st[:, :ts], sn, gain=gain, gres=["vec"])
                b.dma("sp", dst_d[j * 128:(j + 1) * 128, t0:t0 + ts], st[:, :ts], r=[sn], w=[(res, j, ti)], key=res)
            return epi

        b.linear(W, KC, [(1536, 512)], hT, "hT", TILES, q_epi(QTd, 1536, vec[:, VC_QN:VC_QN + 1], "QTd"))
        b.linear(W, KC, [(2304, 512)], hT, "hT", TILES, q_epi(dQTd, 2304, None, "dQTd"))

        def g_epi(col, ti, tile, ps, psres):
            t0, ts, s = tile
            st, sn = gst.next()
            b.act(st[:, :ts], ps, AF.Sigmoid, r=[psres], w=[sn])
            r0 = col - 3840
            b.dma("sp", b.gatesD[r0:r0 + 128, t0:t0 + ts], st[:, :ts], r=[sn], w=[("gD", r0 // 128, ti)], key="gD")

        b.linear(W, KC, [(3840 + 512 * i, 512) for i in range(8)], hT, "hT", TILES, g_epi)
        CT = [TILES[4]]
        ctoks = [(2048, 4), (2176, 4)]

        def ctm(dst, res):
            def epi(tok0, ps, psres):
                i = (tok0 - 2048) // 128
                b.act(dst[:, i, :], ps, AF.Copy, r=[psres], w=[(res, i)])
            return epi

        b.linear_tm(W, 0, 512, hT, "hT", ctoks, ctm(ZfC, "ZfC"))
        b.linear_tm(W, 2176, 128, hT, "hT", ctoks, ctm(VCc, "VCc"))
        b.linear_tm(W, 3328, 512, hT, "hT", ctoks, ctm(dVC, "dVC"))

        def kc_epi(col, ti, tile, ps, psres):
            t0, ts, s = tile
            b.rope_epi(ps, psres, tile, KTc[:, :], "KTc", gain=vec[:, VC_KN:VC_KN + 1], gres=["vec"])

        def dkc_epi(col, ti, tile, ps, psres):
            j = (col - 2816) // 128
            b.rope_epi(ps, psres, tile, dKTc[:, j, :], ("dKTc", j), gain=None)

        b.linear(W, KC, [(2048, 128)], hT, "hT", CT, kc_epi, tile_ids=[4])
        b.linear(W, KC, [(2816, 512)], hT, "hT", CT, dkc_epi, tile_ids=[4])
        b.free(es)

        es_m = ExitStack()
        mT = b.sb("mT", [128, KC, TA], BF16, es_m)

        def load_fm(dst, src_d, res, nm):
            for j in range(4):
                b.dma("sp", dst[:, j, 0:TA], src_d[j * 128:(j + 1) * 128, :],
                      r=[(res, j, ti) for ti in range(5)], w=[nm], key=nm)

        grot = b.rot("gt", [128, 512], BF16, 3, es_m)

        def branch_proj(srcT, srcres, Wb, bi):
            def epi(col, ti, tile, ps, psres):
                t0, ts, s = tile
                j = col // 128
                g, gn = grot.next()
                r0 = bi * D + col
                b.dma("sp", g[:, :ts], b.gatesD[r0:r0 + 128, t0:t0 + ts], r=[("gD", r0 // 128, ti)], w=[gn], key=gn)
                if bi == FIRST_BRANCH:
                    b.tt("dve", mT[:, j, t0:t0 + ts], ps, g[:, :ts], ALU.mult, r=[psres, gn], w=[("mT", j, ti)])
                else:
                    t3, tn = b.t32.next()
                    b.tt("dve", t3[:, :ts], ps, g[:, :ts], ALU.mult, r=[psres, gn], w=[tn])
                    b.tt("pool", mT[:, j, t0:t0 + ts], mT[:, j, t0:t0 + ts], t3[:, :ts], ALU.add, r=[tn], w=[("mT", j, ti)])
            b.linear(Wb, 4, [(0, 512), (512, 512)], srcT, srcres, TILES, epi)

        FIRST_BRANCH = 1
        es = ExitStack()
        uT = b.sb("uT", [128, 4, UTW], BF16, es)
        hv = b.halo.rearrange("(j p) t -> p j t", p=128)
        b.dma("sp", uT[:, :, 0:15], hv[:, :, 0:15], w=["uT"], key="uT")
        b.dma("sp", uT[:, :, 2063:2078], hv[:, :, 15:30], w=["uT"], key="uT")
        b.memset("pool", uT[:, :, 2078:2093], 0.0, w=["uT"])
        b.memset("pool", uT[:, :, 2349:2364], 0.0, w=["uT"])
        for j in range(4):
            b.dma("sp", uT[:, j, UL0:UL0 + TL], uTd[j * 128:(j + 1) * 128, 0:TL], r=[("uTd", j, ti) for ti in range(4)], w=["uT"], key="uT")
            b.dma("sp", uT[:, j, UC0:UC0 + CTXL], uTd[j * 128:(j + 1) * 128, TL:TA], r=[("uTd", j, 4)], w=["uT"], key="uT")
        dg = b.sb("dg", [128, 4, 31, 128], BF16, es)
        for j in range(4):
            for tap in range(31):
                c = VC_CONVW + j * 31 + tap
                b.ts("dve", dg[:, j, tap, :], b.IDENT, vec[:, c:c + 1], None, ALU.mult, r=["cmat", "vec"], w=[("dg", j)])
        convT = b.sb("convT", [128, 4, TA], BF16, es)
        y32 = b.sb("y32", [128, 4, 512], F32, es)
        ybf = b.sb("ybf", [128, 4, 512], BF16, es)
        ysq = b.sb("ysq", [128, 4, 512], BF16, es)
        st4 = b.sb("st4", [128, 4, 512], F32, es)
        for ti, (t0, ts, s) in enumerate(TILES):
            base = t0 if s == 0 else (2078 + t0 - TL)
            ures = ["uT"]
            for j in range(4):
                pb = b.psr.next()
                for tap in range(31):
                    b.mm(b.ps[:, pb, :ts], dg[:, j, tap, :], uT[:, j, base + tap:base + tap + ts], tap == 0, tap == 30,
                         r=[("dg", j)] + ures, w=[("ps", pb)])
                cb = vec[:, VC_CONVB + j:VC_CONVB + j + 1]
                b.act(y32[:, j, :ts], b.ps[:, pb, :ts], AF.Identity, r=[("ps", pb), "vec"], w=[("y32", j)], bias=cb)
                b.act(ysq[:, j, :ts], b.ps[:, pb, :ts], AF.Square, r=[("ps", pb), "vec"], w=[("ysq", j)], bias=cb)
                b.cp("pool", ybf[:, j, :ts], y32[:, j, :ts], r=[("y32", j)], w=[("ybf", j)])
            pm, pq = b.psr.next(), b.psr.next()
            for j in range(4):
                b.mm(b.ps[:, pm, :ts], b.ONESD, ybf[:, j, :ts], j == 0, j == 3, r=[("ybf", j), "cmat"], w=[("ps", pm)])
            for j in range(4):
                b.mm(b.ps[:, pq, :ts], b.ONESD, ysq[:, j, :ts], j == 0, j == 3, r=[("ysq", j), "cmat"], w=[("ps", pq)])
            b.ts("dve", st4[:, 0, :ts], b.ps[:, pm, :ts], 2.0, None, ALU.mult, r=[("ps", pm)], w=["st4m"])
            b.tt("dve", st4[:, 1, :ts], st4[:, 0, :ts], st4[:, 0, :ts], ALU.mult, r=["st4m"], w=["st4q"])
            b.stt("dve", st4[:, 1, :ts], b.ps[:, pq, :ts], 2.0, st4[:, 1, :ts], ALU.mult, ALU.subtract, r=[("ps", pq), "st4q"], w=["st4v"])
            b.rsqrt(st4[:, 2, :ts], "st4r", st4[:, 1, :ts], "st4v", LN_EPS)
            for j in range(4):
                t3, tn = b.t32.next()
                b.tt("dve", t3[:, :ts], y32[:, j, :ts], st4[:, 0, :ts], ALU.subtract, r=[("y32", j), "st4m"], w=[tn])
                b.tt("pool", t3[:, :ts], t3[:, :ts], st4[:, 2, :ts], ALU.mult, r=[tn, "st4r"], w=[tn])
                b.act(convT[:, j, t0:t0 + ts], t3[:, :ts], AF.Silu, r=[tn, "vec"], w=[("convT", ti)],
                      scale=vec[:, VC_LNG + j:VC_LNG + j + 1], bias=vec[:, VC_LNB + j:VC_LNB + j + 1])
        branch_proj(convT, "convT", b.w["w_conv"], 1)
        b.free(es)

        def attention(es, nheads_outer, setup, qsrc, qres, outT, outres, is_diff):
            PT = b.rot("PT", [128, 2, 512], BF16, 3, es)
            rcb = b.rot("rcb", [128, 2, 512], F32, 2, es)
            spair = [0]
            for ho in range(nheads_outer):
                Kbuf, kres, vfun, vres, qchunks = setup(ho)
                for qc in qchunks:
                    for ti, (t0, ts, s) in enumerate(TILES):
                        chunks = list(range(66)) if s == 0 else [64, 65]
                        sb0 = None
                        po = [0, 1] if is_diff else [0]
                        pZ = 2
                        def issue_S(c):
                            spair[0] ^= 1
                            pS = 4 + 2 * spair[0]
                            for half in range(2):
                                p0 = half * 64
                                b.mm(b.ps[:, pS + half, :ts], Kbuf[p0:p0 + 64, c * 128:(c + 1) * 128], qsrc[p0:p0 + 64, qc, t0:t0 + ts],
                                     True, True, r=[kres, qres], w=[("ps", pS + half)])
                            return pS

                        pS_next = issue_S(chunks[0])
                        for ci, c in enumerate(chunks):
                            pS = pS_next
                            if ci + 1 < len(chunks):
                                pS_next = issue_S(chunks[ci + 1])
                            pt, ptn = PT.next()
                            b.act(pt[:, :, :ts], b.ps[:, pS:pS + 2, :ts], AF.Exp, r=[("ps", pS), ("ps", pS + 1)], w=[ptn], scale=0.125)
                            first, lastc = ci == 0, ci == len(chunks) - 1
                            for half in range(2):
                                b.mm(b.ps[:, pZ + half, :ts], b.ones1[:, :], pt[:, half, :ts], first, lastc, r=["ones1", ptn], w=[("ps", pZ + half)])
                            if is_diff:
                                for half in range(2):
                                    b.mm(b.ps[:, po[half], :ts], vfun(c, 0), pt[:, half, :ts], first, lastc, r=[vres, ptn], w=[("ps", po[half])])
                            else:
                                b.mm(b.ps[:, po[0], :ts], vfun(c, 0), pt[:, 0, :ts], first, False, r=[vres, ptn], w=[("ps", po[0])])
                                b.mm(b.ps[:, po[0], :ts], vfun(c, 1), pt[:, 1, :ts], False, lastc, r=[vres, ptn], w=[("ps", po[0])])
                        rc, rcn = rcb.next()
                        b.op("dve", lambda e, o=rc[:, :, :ts], i=b.ps[:, pZ:pZ + 2, :ts]: e.reciprocal(out=o, in_=i),
                             r=[("ps", pZ), ("ps", pZ + 1)], w=[rcn])
                        if not is_diff:
                            for half in range(2):
                                p0 = half * 64
                                b.tt("dve", outT[p0:p0 + 64, qc, t0:t0 + ts], b.ps[p0:p0 + 64, po[0], :ts], rc[p0:p0 + 64, half, :ts], ALU.mult,
                                     r=[("ps", po[0]), rcn], w=[(outres, ti)])
                        else:
                            t1, n1 = b.t32.next()
                            t2, n2 = b.t32.next()
                            b.tt("dve", t1[:, :ts], b.ps[:, po[0], :ts], rc[:, 0, :ts], ALU.mult, r=[("ps", po[0]), rcn], w=[n1])
                            b.tt("dve", t2[:, :ts], b.ps[:, po[1], :ts], rc[:, 1, :ts], ALU.mult, r=[("ps", po[1]), rcn], w=[n2])
                            b.stt("dve", t1[:, :ts], t2[:, :ts], nlam, t1[:, :ts], ALU.mult, ALU.add, r=[n1, n2, "nlam"], w=[n1])
                            sq, sn = b.tb.next()
                            b.act(sq[:, :ts], t1[:, :ts], AF.Square, r=[n1], w=[sn])
                            pn = b.psr.next()
                            b.mm(b.ps[:, pn, :ts], b.ONESD, sq[:, :ts], True, True, r=[sn, "cmat"], w=[("ps", pn)])
                            rt, rn = b.t32.next()
                            b.rsqrt(rt[:, :ts], rn, b.ps[:, pn, :ts], ("ps", pn), EPS, scale=8.0)
                            b.stt("dve", outT[:, qc, t0:t0 + ts], t1[:, :ts], gd, rt[:, :ts], ALU.mult, ALU.mult, r=[n1, rn, "gd"], w=[(outres, ti)])

        es = ExitStack()
        gOT = b.sb("gOT", [128, 4, TA], BF16, es)
        QT = b.sb("QT", [128, 4, TA], BF16, es)
        load_fm(QT, QTd, "QTd", "QT")
        KTr = b.sb("KTr", [128, SEQ + CTXL], BF16, es)
        Vz = b.sb("Vz", [128, 66, 192], BF16, es)
        b.memset("pool", Vz[:, :, :], 0.0, w=["Vz"])
        Vv = b.V_all.rearrange("(c p) d -> p c d", p=128)

        def gqa_setup(h):
            for half in range(2):
                b.dma("sp", KTr[half * 64:half * 64 + 64, 0:SEQ], b.KT_all[h * 64:h * 64 + 64, :], w=["KTr"], key="KTr")
                b.dma("sp", KTr[half * 64:half * 64 + 64, SEQ:SEQ + CTXL], KTc[h * 64:h * 64 + 64, :], r=["KTc"], w=["KTr"], key="KTr")
            for q4 in range(4):
                for off in (0, 128):
                    b.dma("sp", Vz[:, q4 * 16:(q4 + 1) * 16, off:off + 64], Vv[:, q4 * 16:(q4 + 1) * 16, h * 64:h * 64 + 64], w=["Vz"], key="Vz")
            for off in (0, 128):
                b.cp("pool", Vz[:, 64:66, off:off + 64], VCc[:, :, h * 64:h * 64 + 64], r=[("VCc", 0), ("VCc", 1)], w=["Vz"])
            return KTr, "KTr", (lambda c, half: Vz[:, c, half * 64:half * 64 + 128]), "Vz", [2 * h, 2 * h + 1]

        attention(es, 2, gqa_setup, QT, "QT", gOT, "gOT", False)
        branch_proj(gOT, "gOT", b.w["w_gqa"], 2)
        b.free(es)
        es = ExitStack()
        dOT = b.sb("dOT", [128, 4, TA], BF16, es)
        dQT = b.sb("dQT", [128, 4, TA], BF16, es)
        load_fm(dQT, dQTd, "dQTd", "dQT")
        dKh = b.sb("dKh", [128, SEQ + CTXL], BF16, es)
        dVh = b.sb("dVh", [128, 66, 128], BF16, es)
        dVv = b.dV_all.rearrange("(c p) d -> p c d", p=128)

        def diff_setup(hd):
            b.dma("sp", dKh[:, 0:SEQ], b.dKT_all[hd * 128:(hd + 1) * 128, :], w=["dKh"], key="dKh")
            b.cp("pool", dKh[:, SEQ:SEQ + CTXL], dKTc[:, hd, :], r=[("dKTc", hd)], w=["dKh"])
            for q4 in range(4):
                b.dma("sp", dVh[:, q4 * 16:(q4 + 1) * 16, :], dVv[:, q4 * 16:(q4 + 1) * 16, hd * 128:(hd + 1) * 128], w=["dVh"], key="dVh")
            b.cp("pool", dVh[:, 64:66, :], dVC[:, :, hd * 128:(hd + 1) * 128], r=[("dVC", 0), ("dVC", 1)], w=["dVh"])
            return dKh, "dKh", (lambda c, half: dVh[:, c, :]), "dVh", [hd]

        attention(es, 4, diff_setup, dQT, "dQT", dOT, "dOT", True)
        branch_proj(dOT, "dOT", b.w["w_diff"], 3)
        b.free(es)

        es = ExitStack()
        FT = b.sb("FT", [128, 4, TA], BF16, es)
        Zpm = b.sb("Zpm", [128, 2, 32, 512], BF16, es)
        Zv = b.ZfG.rearrange("(c p) d -> p c d", p=128)
        zld = b.rot("zld", [128, 2, 4, 512], BF16, 2, es)
        for q4 in range(8):
            zl, zn = zld.next()
            b.dma("sp", zl[:, 0, :, :], Zv[:, q4 * 4:(q4 + 1) * 4, :], w=[zn], key=zn)
            b.dma("sp", zl[:, 1, :, :], Zv[:, 32 + q4 * 4:32 + (q4 + 1) * 4, :], w=[zn], key=zn)
            b.tt("dve", Zpm[:, 0, q4 * 4:(q4 + 1) * 4, :], zl[:, 0, :, :], zl[:, 1, :, :], ALU.add, r=[zn], w=["Zp"])
            b.tt("pool", Zpm[:, 1, q4 * 4:(q4 + 1) * 4, :], zl[:, 0, :, :], zl[:, 1, :, :], ALU.subtract, r=[zn], w=["Zm"])
        tcb = b.rot("tcb", [128, 4, 512], BF16, 2, es)
        tsb = b.rot("tsb", [128, 4, 512], BF16, 2, es)
        PcT = b.sb("PcT", [128, 8, 512], BF16, es)
        tcc = b.sb("tcc", [128, 2, 256], BF16, es)
        tsc = b.sb("tsc", [128, 2, 256], BF16, es)
        b.dma("sp", tcc[:], b.dftcc_d[:, :, :], w=["tcc"], key="tcc")
        b.dma("sp", tsc[:], b.dftsc_d[:, :, :], w=["tcc"], key="tcc")
        FT2 = FT[:, :, 0:TL].rearrange("p j (i two) -> p j i two", two=2)

        def four_finish(dst_of, ts, fres):
            for i in range(8):
                if i % 2 == 0:
                    b.act(PcT[:, i, :ts], b.ps[:, i, :ts], AF.Copy, r=[("ps", i)], w=[("PcT", i)])
                else:
                    b.cp("dve", PcT[:, i, :ts], b.ps[:, i, :ts], r=[("ps", i)], w=[("PcT", i)])
            for cj in range(4):
                pb = b.psr.next()
                b.mm(b.ps[:, pb, :ts], b.C128, PcT[:, 2 * cj, :ts], True, False, r=[("PcT", 2 * cj), "cmat"], w=[("ps", pb)])
                b.mm(b.ps[:, pb, :ts], b.S128N, PcT[:, 2 * cj + 1, :ts], False, True, r=[("PcT", 2 * cj + 1), "cmat"], w=[("ps", pb)])
                b.act(dst_of(cj), b.ps[:, pb, :ts], AF.Copy, r=[("ps", pb)], w=[fres])

        for cls in range(2):
            zres = "Zp" if cls == 0 else "Zm"
            for kt2 in range(2):
                for grp in range(8):
                    tc_, tcn = tcb.next()
                    ts_, tsn = tsb.next()
                    b.dma("sp", tc_[:], b.dftc_d[cls, kt2, grp, :, :, :], w=[tcn], key=tcn)
                    b.dma("sp", ts_[:], b.dfts_d[cls, kt2, grp, :, :, :], w=[tsn], key=tsn)
                    for tl in range(4):
                        tch = grp * 4 + tl
                        for cj in range(4):
                            lhs = Zpm[:, cls, tch, cj * 128:(cj + 1) * 128]
                            b.mm(b.ps[:, 2 * cj, :], lhs, tc_[:, tl, :], tch == 0, tch == 31, r=[zres, tcn], w=[("ps", 2 * cj)])
                            b.mm(b.ps[:, 2 * cj + 1, :], lhs, ts_[:, tl, :], tch == 0, tch == 31, r=[zres, tsn], w=[("ps", 2 * cj + 1)])
                four_finish(lambda cj, a=cls, k=kt2: FT2[:, cj, 512 * k:512 * (k + 1), a], 512, "FTl")
        for tch in range(2):
            for cj in range(4):
                lhs = ZfC[:, tch, cj * 128:(cj + 1) * 128]
                b.mm(b.ps[:, 2 * cj, :256], lhs, tcc[:, tch, :], tch == 0, tch == 1, r=[("ZfC", tch), "tcc"], w=[("ps", 2 * cj)])
                b.mm(b.ps[:, 2 * cj + 1, :256], lhs, tsc[:, tch, :], tch == 0, tch == 1, r=[("ZfC", tch), "tcc"], w=[("ps", 2 * cj + 1)])
        four_finish(lambda cj: FT[:, cj, TL:TA], 256, "FTc")
        for ti in range(5):
            b.op("pool", lambda e: e.engine_nop(), r=["FTl", "FTc"], w=[("FT", ti)])
        branch_proj(FT, "FT", b.w["w_four"], 0)
        b.free(es)

        es = ExitStack()
        xin = b.rot("xin", [128, 512], F32, 3, es)
        xv = b.xT_in.rearrange("(kc p) t -> p kc t", p=128)
        xmv = b.xmidD.rearrange("(kc p) t -> p kc t", p=128)

        def wout_epi(col, ti, tile, ps, psres):
            t0, ts, s = tile
            j = col // 128
            xt, xn = xin.next()
            b.dma("sp", xt[:, :ts], xv[:, j, t0:t0 + ts], r=[("xT_in", ti)], w=[xn], key=xn)
            t3, tn = b.t32.next()
            b.stt("dve", t3[:, :ts], ps, mod[:, 16 + j, s:s + 1], xt[:, :ts], ALU.mult, ALU.add, r=[psres, xn] + mres, w=[tn])
            b.dma("sp", xmv[:, j, t0:t0 + ts], t3[:, :ts], r=[tn], w=[("xmid", ti)], key="xmid")

        for ti in range(5):
            b.op("pool", lambda e: e.engine_nop(), r=[("mT", j, ti) for j in range(KC)], w=[("mTa", ti)])
        b.linear(b.w["w_out"], KC, [(0, 512), (512, 512)], mT, "mTa", TILES, wout_epi)
        b.free(es)
        b.free(es_m)

        es = ExitStack()
        hT = b.sb("hTf", [128, KC, TA], BF16, es)
        b.wb2 = b.rot("wc", [128, KC, 512], BF16, 2, es)
        b.phase_norm(b.xmidD, "xmid", TILES, hT, "hTf", m["affn"], mod[:, 24:32, :], mres)
        hid = b.sb("hid", [128, 22, 1024], BF16, es)
        xo_d = b.xT_out
        xov = xo_d.rearrange("(kc p) t -> p kc t", p=128)
        w2v = b.w["w_ffn2"].rearrange("(kc p) n -> p kc n", p=128)
        w2b = b.rot("w2b", [128, 22, 128], BF16, 2, es)
        groups = [[0, 1], [2, 3], [4]]
        for gi, grp in enumerate(groups):
            gt0 = TILES[grp[0]][0]
            gtiles = [TILES[i] for i in grp]

            def h_epi(j, ti, tile, pa, pra, pg, prg):
                t0, ts, s = tile
                sg, sgn = b.t32.next()
                b.act(sg[:, :ts], pa, AF.Silu, r=[pra], w=[sgn])
                b.tt("dve", hid[:, j, t0 - gt0:t0 - gt0 + ts], pg, sg[:, :ts], ALU.mult, r=[prg, sgn], w=[("hid", ti)])

            b.linear2(b.w["w_ffn1"], 0, b.w["w_ffn3"], 0, HID, KC, hT, "hTf", gtiles, h_epi, tile_ids=grp)
            for n in range(KC):
                wb_, wr = w2b.next()
                b.dma("pool", wb_[:, :, :], w2v[:, :, n * 128:(n + 1) * 128], w=[wr], key=wr)
                for ti in grp:
                    t0, ts, s = TILES[ti]
                    pb = b.psr.next()
                    for kc in range(22):
                        b.mm(b.ps[:, pb, :ts], wb_[:, kc, :], hid[:, kc, t0 - gt0:t0 - gt0 + ts], kc == 0, kc == 21,
                             r=[wr, ("hid", ti)], w=[("ps", pb)])
                    xt, xn = b.t32.next()
                    b.dma("sp", xt[:, :ts], xmv[:, n, t0:t0 + ts], r=[("xmid", ti)], w=[xn], key=xn)
                    t3, tn = b.t32.next()
                    b.stt("dve", t3[:, :ts], b.ps[:, pb, :ts], mod[:, 40 + n, s:s + 1], xt[:, :ts], ALU.mult, ALU.add,
                          r=[("ps", pb), xn] + mres, w=[tn])
                    b.dma("sp", xov[:, n, t0:t0 + ts], t3[:, :ts], r=[tn], w=[("xT_out", ti)], key="xT_out")
        b.free(es)
        nf = vec[:, VC_NFIN:VC_NFIN + 8]
        b.phase_norm(b.xT_out, "xT_out", LTILES, None, None, nf, None, ["vec"], out_d=b.out_d, outres="outF")
        self.final_keys = ["xT_out", "outF"]


def _bf(a):
    return np.ascontiguousarray(a).astype(NPBF)


def _rope_tables(chunk):
    t = np.arange(chunk * TL, (chunk + 1) * TL)
    row = (t // 64).astype(np.float32)
    col = (t % 64).astype(np.float32)
    inv = (10000.0 ** (-np.arange(0, 32, 2, dtype=np.float32) / 32)).astype(np.float32)
    ang = np.concatenate([row[:, None] * inv, col[:, None] * inv], -1)
    cos = np.cos(ang).astype(np.float32).T
    sin = np.sin(ang).astype(np.float32).T
    C = np.ones((128, TA), np.float32)
    S = np.zeros((128, TA), np.float32)
    for blk in range(4):
        C[blk * 32:(blk + 1) * 32, :TL] = cos
        S[blk * 32:(blk + 1) * 32, :TL] = sin
    return C, S


def _const_mats():
    m = np.arange(64)
    ang = 2 * np.pi * np.outer(m, m) / 64.0
    c64, s64 = np.cos(ang), np.sin(ang)
    z = np.zeros((64, 64))
    c128 = np.block([[c64, z], [z, c64]])
    s128n = -np.block([[s64, z], [z, s64]])
    rm = np.zeros((128, 128))
    for blk in range(2):
        for i in range(32):
            rm[blk * 64 + i + 32, blk * 64 + i] = -1.0
            rm[blk * 64 + i, blk * 64 + i + 32] = 1.0
    onesd = np.full((128, 128), 1.0 / 1024)
    bones = np.block([[np.ones((64, 64)), z], [z, np.ones((64, 64))]]) / 64.0
    ident = np.eye(128)
    return _bf(np.stack([c128, s128n, rm, onesd, bones, ident], 1))


def _dft_tables(chunk):
    t = np.arange(SEQ // 2, dtype=np.int64)
    sc = 1.0 / math.sqrt(SEQ * 64)
    outc = np.zeros((2, 2, 8, 128, 4, 512), NPBF)
    outs = np.zeros((2, 2, 8, 128, 4, 512), NPBF)
    for cls in range(2):
        for kt2 in range(2):
            kp = 2 * (512 * kt2 + np.arange(512, dtype=np.int64)) + cls
            k = chunk * TL + kp
            ph = (np.outer(t, k) % SEQ).astype(np.float64) * (2 * np.pi / SEQ)
            for nm, arr in ((outc, np.cos(ph) * sc), (outs, np.sin(ph) * sc)):
                a4 = arr.reshape(8, 4, 128, 512)
                nm[cls, kt2] = a4.transpose(0, 2, 1, 3).astype(NPBF)
    return outc, outs


def _dft_ctx():
    t = np.arange(CTXL)
    ph = (np.outer(t, t) % CTXL) * (2 * np.pi / CTXL)
    sc = 1.0 / math.sqrt(CTXL * 64)

    def lay(a):
        return _bf(a.reshape(2, 128, 256).transpose(1, 0, 2))

    return lay(np.cos(ph) * sc), lay(np.sin(ph) * sc)


def _fm(v, n):
    return np.asarray(v, np.float32).reshape(n, 128).T


def _vec(inp, l, bidx):
    v = np.zeros((128, NVEC), np.float32)
    cc = np.stack([_fm(inp["c"][bidx], 8), _fm(inp["c_ctx"], 8)], -1)
    v[:, VC_C:VC_C + 16] = cc.reshape(128, 16)
    v[:, VC_BADA:VC_BADA + 48] = _fm(inp["b_ada"][l], 48)
    v[:, VC_NMIX:VC_NMIX + 8] = _fm(inp["norm_mix"][l], 8)
    v[:, VC_NFFN:VC_NFFN + 8] = _fm(inp["norm_ffn"][l], 8)
    v[:, VC_NFIN:VC_NFIN + 8] = _fm(inp["final_norm"], 8)
    cw = np.asarray(inp["conv_w"][l], np.float32)
    v[:, VC_CONVW:VC_CONVW + 124] = cw.T.reshape(4, 128, 31).transpose(1, 0, 2).reshape(128, 124)
    v[:, VC_CONVB:VC_CONVB + 4] = _fm(inp["conv_b"][l], 4)
    v[:, VC_LNG:VC_LNG + 4] = _fm(inp["conv_ln_g"][l], 4)
    v[:, VC_LNB:VC_LNB + 4] = _fm(inp["conv_ln_b"][l], 4)
    v[:, VC_QN] = np.tile(np.asarray(inp["q_norm"][l], np.float32), 2)
    v[:, VC_KN] = np.tile(np.asarray(inp["k_norm"][l], np.float32), 2)
    v[:, VC_DN] = np.asarray(inp["diff_norm"][l], np.float32)
    lam = np.concatenate([inp["lam_q1"][l], inp["lam_k1"][l], inp["lam_q2"][l], inp["lam_k2"][l]]).astype(np.float32)
    v[:, VC_LAM:VC_LAM + 256] = lam[None, :]
    li = 0.8 - 0.6 * math.exp(-0.3 * l)
    v[:, VC_NLI] = -li
    v[:, VC_1LI] = 1.0 - li
    return v


_PROGS = {}
_CONST = {}
_PRE_COLS = [(0, 512), (512, 1536), (2048, 2304), (2816, 3840)]


def _prog(do_post):
    if do_post not in _PROGS:
        bld = Builder(do_post, True, False, 0.0)
        nc = bld.build()
        _PROGS[do_post] = (nc, bld.out_keys)
    return _PROGS[do_post]


def _consts():
    if not _CONST:
        _CONST["cmat"] = _const_mats()
        _CONST["rope"] = [_rope_tables(j) for j in range(4)]
        _CONST["dft"] = [_dft_tables(j) for j in range(4)]
        _CONST["dftc"] = _dft_ctx()
    return _CONST


DEPTH = 4


def _gather(outs):
    gathered = []
    for bi in range(NB):
        cs = [outs[bi * 4 + j] for j in range(4)]
        g = {
            "ZfG": np.concatenate([np.asarray(c["o_Zf"]) for c in cs], 0),
            "KT_all": np.concatenate([np.asarray(c["o_KT"]) for c in cs], 1),
            "dKT_all": np.concatenate([np.asarray(c["o_dKT"]) for c in cs], 1),
            "V_all": np.concatenate([np.asarray(c["o_V"]) for c in cs], 0),
            "dV_all": np.concatenate([np.asarray(c["o_dV"]) for c in cs], 0),
        }
        g = {k: np.ascontiguousarray(v) for k, v in g.items()}
        ue = [np.asarray(c["o_ue"]) for c in cs]
        zero = np.zeros((512, 15), ue[0].dtype)
        halo = []
        for j in range(4):
            left = ue[j - 1][:, 17:32] if j > 0 else zero
            right = ue[j + 1][:, 0:15] if j < 3 else zero
            halo.append(np.ascontiguousarray(np.concatenate([left, right], 1)))
        g["halo"] = halo
        g["mod"] = [np.asarray(c["o_mod"]) for c in cs]
        gathered.append(g)
    return gathered


def kernel(**inp):
    inp = {k: np.asarray(v) for k, v in inp.items()}
    cst = _consts()
    ncore = 8
    xT = []
    for core in range(ncore):
        bi, j = core // 4, core % 4
        xs = np.concatenate([inp["x"][bi, j * TL:(j + 1) * TL], inp["ctx"][bi]], 0)
        xT.append(np.ascontiguousarray(xs.T.astype(np.float32)))
    wl = lambda n, l: np.ascontiguousarray(inp[n][l].astype(np.float32))

    def pre_w(l):
        if l >= DEPTH:
            return np.zeros((D, 6 * D), np.float32), np.zeros((D, 2816), np.float32)
        w = inp["w_in"][l]
        return wl("w_ada", l), np.ascontiguousarray(np.concatenate([w[:, a:b] for a, b in _PRE_COLS], 1).astype(np.float32))

    res = None
    gathered = None
    for step in range(DEPTH + 1):
        do_post = step > 0
        lpost, lpre = step - 1, step
        nc, out_keys = _prog(do_post)
        wada_n, winp_n = pre_w(lpre)
        in_maps = []
        for core in range(ncore):
            bi, j = core // 4, core % 4
            mp = {"xT": xT[core], "ropeC": cst["rope"][j][0], "ropeS": cst["rope"][j][1], "cmat": cst["cmat"]}
            vec_n = _vec(inp, min(lpre, DEPTH - 1), bi)
            if do_post:
                for n in ("w_in", "w_four", "w_conv", "w_gqa", "w_diff", "w_out", "w_ffn1", "w_ffn3", "w_ffn2"):
                    mp[n] = wl(n, lpost)
                mp["vec"] = _vec(inp, lpost, bi)
                mp["dftc"], mp["dfts"] = cst["dft"][j]
                mp["dftcc"], mp["dftsc"] = cst["dftc"]
                g = gathered[bi]
                for n in ("ZfG", "KT_all", "dKT_all", "V_all", "dV_all"):
                    mp[n] = g[n]
                mp["halo"] = g["halo"][j]
                mp["modin"] = g["mod"][j]
                mp["w_ada_n"], mp["w_inp_n"], mp["vec_n"] = wada_n, winp_n, vec_n
            else:
                mp["w_ada"], mp["w_inp"], mp["vec"] = wada_n, winp_n, vec_n
            in_maps.append(mp)
        res = run_bass_kernel_spmd(nc, in_maps, core_ids=list(range(ncore)))
        outs = res.results
        if do_post:
            xT = [np.asarray(outs[c]["xT_out"]) for c in range(ncore)]
        gathered = _gather(outs)
    out = np.zeros((NB, SEQ, D), np.float32)
    for core in range(ncore):
        bi, j = core // 4, core % 4
        out[bi, j * TL:(j + 1) * TL] = np.asarray(res.results[core]["out"]).T
    return out
```
